# Optimizing a Trainium2 kernel written in Bass

```python
import math
import jax, jax.numpy as jnp
from jax import lax
import numpy as np

D_MODEL = 1024
BATCH = 2
SEQ = 8192
DEPTH = 2
DEC_BATCH = 1
DEC_SEQ = 16384
PAST_LEN = 128

HEAD_DIM = 64
N_DIFF_HEADS = 4
DIFF_V_DIM = 2 * HEAD_DIM
D_DIFF = N_DIFF_HEADS * DIFF_V_DIM
N_DIL_HEADS = 8
D_DIL = N_DIL_HEADS * HEAD_DIM
D_MIX = D_DIFF + D_DIL
IN_SPLITS = (D_DIFF, D_DIFF, D_DIFF, D_DIL, D_DIL, D_DIL)
D_IN = sum(IN_SPLITS)
D_FF = 4 * D_MODEL
DILATED_PATTERNS = ((128, 1), (512, 4), (2048, 16))
Q_BLOCK = 128
ROPE_THETA = 10000.0
NORM_EPS = 1e-5
NEG_INF = -1e30

kernel_name = 'hymba_diff_dilated_encoder'


def rmsnorm(x, g):
    xf = x.astype(jnp.float32)
    y = xf * lax.rsqrt(jnp.mean(xf * xf, axis=-1, keepdims=True) + NORM_EPS)
    return (y * g.astype(jnp.float32)).astype(x.dtype)


def rope(x, pos):
    half = x.shape[-1] // 2
    inv_freq = ROPE_THETA ** (-jnp.arange(half, dtype=jnp.float32) / half)
    ang = pos.astype(jnp.float32)[:, None] * inv_freq[None, :]
    cos = jnp.cos(ang)[:, None, :]
    sin = jnp.sin(ang)[:, None, :]
    xf = x.astype(jnp.float32)
    x1, x2 = xf[..., :half], xf[..., half:]
    return jnp.concatenate([x1 * cos - x2 * sin, x2 * cos + x1 * sin], axis=-1).astype(x.dtype)


def diff_attention(q1, q2, k1, k2, v, lam):
    B, S, H, dh = q1.shape
    nb = S // Q_BLOCK
    scale = dh ** -0.5
    qq = jnp.stack([q1, q2], axis=0).reshape(2, B, nb, Q_BLOCK, H, dh)
    qq = qq.transpose(2, 0, 1, 3, 4, 5)
    kk = jnp.stack([k1, k2], axis=0)

    def block(qblk):
        s = jnp.einsum('pbqhd,pbkhd->pbhqk', qblk, kk).astype(jnp.float32) * scale
        p = jax.nn.softmax(s, axis=-1)
        a = (p[0] - lam * p[1]).astype(v.dtype)
        return jnp.einsum('bhqk,bkhe->bqhe', a, v)

    o = lax.map(block, qq)
    return o.transpose(1, 0, 2, 3, 4).reshape(B, S, H, v.shape[-1])


def dilated_branch(q, k, v, window, dilation):
    B, S, H, dh = q.shape
    half = window // 2 // dilation
    L = S // dilation
    N = B * dilation

    def by_residue(t):
        return t.reshape(B, L, dilation, H, dh).transpose(0, 2, 1, 3, 4).reshape(N, L, H, dh)

    qs, ks, vs = by_residue(q), by_residue(k), by_residue(v)
    blk = half
    nb = -(-L // blk)
    Lp = nb * blk
    qb = jnp.pad(qs, ((0, 0), (0, Lp - L), (0, 0), (0, 0))).reshape(N, nb, blk, H, dh)
    kv_pad = ((0, 0), (blk, Lp - L + blk), (0, 0), (0, 0))
    kb = jnp.pad(ks, kv_pad).reshape(N, nb + 2, blk, H, dh)
    vb = jnp.pad(vs, kv_pad).reshape(N, nb + 2, blk, H, dh)
    kwin = jnp.concatenate([kb[:, :-2], kb[:, 1:-1], kb[:, 2:]], axis=2)
    vwin = jnp.concatenate([vb[:, :-2], vb[:, 1:-1], vb[:, 2:]], axis=2)
    qi = jnp.arange(Lp).reshape(nb, blk)
    kj = jnp.arange(nb)[:, None] * blk - blk + jnp.arange(3 * blk)[None, :]
    mask = ((jnp.abs(qi[:, :, None] - kj[:, None, :]) <= half)
            & (kj[:, None, :] >= 0) & (kj[:, None, :] < L))
    s = jnp.einsum('nbqhd,nbkhd->nbhqk', qb, kwin).astype(jnp.float32) * (dh ** -0.5)
    s = jnp.where(mask[None, :, None], s, NEG_INF)
    lse = jax.nn.logsumexp(s, axis=-1)
    p = jnp.exp(s - lse[..., None])
    o = jnp.einsum('nbhqk,nbkhd->nbqhd', p.astype(v.dtype), vwin)
    o = o.reshape(N, Lp, H, dh)[:, :L]
    lse = lse.transpose(0, 1, 3, 2).reshape(N, Lp, H)[:, :L]
    o = o.reshape(B, dilation, L, H, dh).transpose(0, 2, 1, 3, 4).reshape(B, S, H, dh)
    lse = lse.reshape(B, dilation, L, H).transpose(0, 2, 1, 3).reshape(B, S, H)
    return o, lse


def dilated_attention(q, k, v):
    outs, lses = [], []
    for window, dilation in DILATED_PATTERNS:
        o, lse = dilated_branch(q, k, v, window, dilation)
        outs.append(o)
        lses.append(lse)
    wts = jax.nn.softmax(jnp.stack(lses, axis=0), axis=0)
    o = jnp.einsum('pbsh,pbshd->bshd', wts, jnp.stack(outs, axis=0).astype(jnp.float32))
    return o.astype(q.dtype)


def encoder_layer(x, layer_idx, norm1_g, w_in, lambda_q1, lambda_k1, lambda_q2, lambda_k2,
                  diff_norm_g, dil_norm_g, w_out, norm2_g, w_ff1, w_ff2):
    B, S, _ = x.shape
    pos = jnp.arange(S)
    h = rmsnorm(x, norm1_g)
    proj = h @ w_in
    dq, dk, dv, sq, sk, sv = jnp.split(proj, list(np.cumsum(IN_SPLITS)[:-1]), axis=-1)

    dq = dq.reshape(B, S, N_DIFF_HEADS, 2, HEAD_DIM)
    dk = dk.reshape(B, S, N_DIFF_HEADS, 2, HEAD_DIM)
    q1, q2 = rope(dq[..., 0, :], pos), rope(dq[..., 1, :], pos)
    k1, k2 = rope(dk[..., 0, :], pos), rope(dk[..., 1, :], pos)
    dv = dv.reshape(B, S, N_DIFF_HEADS, DIFF_V_DIM)
    lambda_init = 0.8 - 0.6 * math.exp(-0.3 * layer_idx)
    lam = (jnp.exp(jnp.sum(lambda_q1.astype(jnp.float32) * lambda_k1.astype(jnp.float32)))
           - jnp.exp(jnp.sum(lambda_q2.astype(jnp.float32) * lambda_k2.astype(jnp.float32)))
           + lambda_init)
    od = diff_attention(q1, q2, k1, k2, dv, lam)
    od = (rmsnorm(od, diff_norm_g) * (1.0 - lambda_init)).reshape(B, S, D_DIFF)

    sq = rope(sq.reshape(B, S, N_DIL_HEADS, HEAD_DIM), pos)
    sk = rope(sk.reshape(B, S, N_DIL_HEADS, HEAD_DIM), pos)
    sv = sv.reshape(B, S, N_DIL_HEADS, HEAD_DIM)
    os_ = rmsnorm(dilated_attention(sq, sk, sv).reshape(B, S, D_DIL), dil_norm_g)

    x = x + jnp.concatenate([od, os_], axis=-1) @ w_out

    h2 = rmsnorm(x, norm2_g)
    return x + jnp.square(jax.nn.relu(h2 @ w_ff1)) @ w_ff2


def trunk(x, norm1_g, w_in, lambda_q1, lambda_k1, lambda_q2, lambda_k2, diff_norm_g,
          dil_norm_g, w_out, norm2_g, w_ff1, w_ff2, final_norm_g):
    for l in range(DEPTH):
        x = encoder_layer(x, l, norm1_g[l], w_in[l], lambda_q1[l], lambda_k1[l], lambda_q2[l],
                          lambda_k2[l], diff_norm_g[l], dil_norm_g[l], w_out[l], norm2_g[l],
                          w_ff1[l], w_ff2[l])
    return rmsnorm(x, final_norm_g)


def setup_inputs(seed: int = 0) -> dict:
    key = jax.random.key(seed)
    ks = jax.random.split(key, 16)
    f32 = jnp.float32
    nrm = lambda k, shape, s: jax.random.normal(k, shape, f32) * s
    return {
        'x_prompt': nrm(ks[0], (BATCH, SEQ, D_MODEL), 1.0),
        'x_sample': nrm(ks[1], (DEC_BATCH, DEC_SEQ, D_MODEL), 1.0),
        'norm1_g': 1.0 + nrm(ks[2], (DEPTH, D_MODEL), 0.01),
        'w_in': nrm(ks[3], (DEPTH, D_MODEL, D_IN), D_MODEL ** -0.5),
        'lambda_q1': nrm(ks[4], (DEPTH, HEAD_DIM), 0.1),
        'lambda_k1': nrm(ks[5], (DEPTH, HEAD_DIM), 0.1),
        'lambda_q2': nrm(ks[6], (DEPTH, HEAD_DIM), 0.1),
        'lambda_k2': nrm(ks[7], (DEPTH, HEAD_DIM), 0.1),
        'diff_norm_g': 1.0 + nrm(ks[8], (DEPTH, DIFF_V_DIM), 0.01),
        'dil_norm_g': 1.0 + nrm(ks[9], (DEPTH, D_DIL), 0.01),
        'w_out': nrm(ks[10], (DEPTH, D_MIX, D_MODEL), D_MIX ** -0.5),
        'norm2_g': 1.0 + nrm(ks[11], (DEPTH, D_MODEL), 0.01),
        'w_ff1': nrm(ks[12], (DEPTH, D_MODEL, D_FF), D_MODEL ** -0.5),
        'w_ff2': nrm(ks[13], (DEPTH, D_FF, D_MODEL), D_FF ** -0.5),
        'final_norm_g': 1.0 + nrm(ks[14], (D_MODEL,), 0.01),
    }


def reference(x_prompt, x_sample, norm1_g, w_in, lambda_q1, lambda_k1, lambda_q2, lambda_k2,
              diff_norm_g, dil_norm_g, w_out, norm2_g, w_ff1, w_ff2, final_norm_g):
    y_prompt = trunk(x_prompt, norm1_g, w_in, lambda_q1, lambda_k1, lambda_q2, lambda_k2,
                     diff_norm_g, dil_norm_g, w_out, norm2_g, w_ff1, w_ff2, final_norm_g)
    y_sample = trunk(x_sample, norm1_g, w_in, lambda_q1, lambda_k1, lambda_q2, lambda_k2,
                     diff_norm_g, dil_norm_g, w_out, norm2_g, w_ff1, w_ff2, final_norm_g)
    return (y_prompt, y_sample)
```

```python
import math
import os
from contextlib import ExitStack

import numpy as np
import concourse.bass as bass
import concourse.mybir as mybir
from concourse.bass_utils import run_bass_kernel_spmd

F32 = mybir.dt.float32
BF16 = mybir.dt.bfloat16
AF = mybir.ActivationFunctionType
ALU = mybir.AluOpType

D = 1024
DEPTH = 2
D_IN = 3072
D_FF = 4096
EPS = 1e-5
NEG = -30000.0
ENGS = ("pe", "act", "dve", "pool", "sp")


class Tok:
    __slots__ = ("sem", "val")

    def __init__(self, sem=None, val=None):
        self.sem, self.val = sem, val


class Res:
    __slots__ = ("name", "w", "w_eng", "rs")

    def __init__(self, name):
        self.name, self.w, self.w_eng, self.rs = name, None, None, {}


class Prog:
    def __init__(self, nc, stack):
        self.nc, self.stack = nc, stack
        self.q = {e: [] for e in ENGS}
        self.sems = {}
        self.waited = {e: {} for e in ENGS}
        self.pending = {e: [] for e in ENGS}
        self.epoch = {e: 0 for e in ENGS}
        self.store_sems = set()

    def sem(self, name):
        if name not in self.sems:
            h = self.stack.enter_context(self.nc.semaphore(name))
            self.sems[name] = [h, 0]
        return self.sems[name]

    def new_epoch(self):
        self.full_barrier()
        for e in ENGS:
            self.epoch[e] += 1

    def _wait(self, eng, tok):
        if tok is None:
            return
        assert tok.val is not None, "unresolved lazy token"
        w = self.waited[eng]
        if w.get(tok.sem, 0) >= tok.val:
            return
        w[tok.sem] = tok.val
        h, v = self.sems[tok.sem][0], tok.val
        self.q[eng].append(I("wait_ge", h, v))

    def _deps(self, eng, reads, writes):
        for r in reads:
            if r.w is not None and not (eng == "pe" and r.w_eng == "pe"):
                self._wait(eng, r.w)
        for r in writes:
            if r.w is not None and r.w_eng != eng:
                self._wait(eng, r.w)
            for e2, t in r.rs.items():
                if e2 != eng:
                    self._wait(eng, t)

    def op(self, eng, fn, reads=(), writes=(), inc=True):
        self._deps(eng, reads, writes)
        tok = Tok()
        if inc:
            name = f"{eng}_{self.epoch[eng]}"
            s = self.sem(name)
            s[1] += 1
            tok.sem, tok.val = name, s[1]
            h = s[0]
            self.q[eng].append(lambda e, h=h, fn=fn: fn(e).then_inc(h, 1))
            for t in self.pending[eng]:
                t.sem, t.val = tok.sem, tok.val
            self.pending[eng] = []
        else:
            self.pending[eng].append(tok)
            self.q[eng].append(lambda e, fn=fn: fn(e))
        for r in reads:
            r.rs[eng] = tok
        for r in writes:
            r.w, r.w_eng, r.rs = tok, eng, {}
        return tok

    def dma(self, qeng, fn, dsem, reads=(), writes=(), store=False):
        self._deps(qeng, reads, writes)
        s = self.sem(dsem)
        s[1] += 16
        tok = Tok(dsem, s[1])
        h = s[0]
        self.q[qeng].append(lambda e, h=h, fn=fn: fn(e).then_inc(h, 16))
        key = "dma:" + dsem
        for r in reads:
            r.rs[key] = tok
        for r in writes:
            r.w, r.w_eng, r.rs = tok, key, {}
        if store:
            self.store_sems.add(dsem)
        return tok

    def collective(self, in_ap, out_ap, groups):
        s = self.sem("cc")
        s[1] += 1
        h, v = s[0], s[1]

        def f(e, h=h, v=v):
            e.collective_compute("AllGather", mybir.AluOpType.bypass, replica_groups=groups,
                                 ins=[in_ap], outs=[out_ap]).then_inc(h)
            e.wait_ge(h, v)

        self.q["pool"].append(f)
        self.waited["pool"]["cc"] = v
        self.store_sems.add("cc")

    def dram_barrier(self, engines=("sp", "pool")):
        for name in sorted(self.store_sems):
            tok = Tok(name, self.sems[name][1])
            for e in engines:
                self._wait(e, tok)

    def full_barrier(self):
        for e in ENGS:
            assert not self.pending[e], f"unresolved lazy tokens on {e} at barrier"
        for name in sorted(self.sems):
            cnt = self.sems[name][1]
            if cnt == 0:
                continue
            tok = Tok(name, cnt)
            for e in ENGS:
                self._wait(e, tok)

    def play(self, block):
        q = self.q

        @block.tensor
        def _(e):
            for f in q["pe"]:
                f(e)

        @block.scalar
        def _(e):
            for f in q["act"]:
                f(e)

        @block.vector
        def _(e):
            for f in q["dve"]:
                f(e)

        @block.gpsimd
        def _(e):
            for f in q["pool"]:
                f(e)

        @block.sync
        def _(e):
            for f in q["sp"]:
                f(e)


def I(name, *a, **k):
    return lambda e: getattr(e, name)(*a, **k)


def lambda_init(l):
    return 0.8 - 0.6 * math.exp(-0.3 * l)


def build_nc(T):
    NB = T // 512
    NKT = T // 128
    HALF_KT = NKT // 2
    HALF_QB = NB // 2
    G = 4
    TQ_LAST = T // G
    PADC = 1024
    nc = bass.Bass("TRN2", target_bir_lowering=False)

    def din(name, shape, dt=F32):
        return nc.dram_tensor(name, list(shape), dt, kind="ExternalInput").ap()

    xT = din("xT", [D, T // 4])
    w_in = din("w_in", [DEPTH, D, D_IN])
    w_out = din("w_out", [DEPTH, D, D])
    w_ff1 = din("w_ff1", [DEPTH, D, D_FF])
    w_ff2 = din("w_ff2", [DEPTH, D_FF, D])
    gains = din("gains", [128, 60])
    lamv = din("lamv", [128, 8 * 64])
    perm = din("perm", [128, 128])
    cmask_in = din("cmask", [128, 20 * 512])
    cosL = din("cosL", [128, T // 4])
    sinL = din("sinL", [128, T // 4])
    NBT3 = (NB // G) * 20
    bt3_in = din("bt3", [128, NBT3])
    yT = nc.dram_tensor("yT", [D, TQ_LAST], F32, kind="ExternalOutput").ap()

    dbg = os.environ.get("KDBG", "")

    def scr(name, shape, dt):
        if dbg:
            return nc.dram_tensor(name, list(shape), dt, kind="ExternalOutput").ap()
        return nc.dram_tensor(name, list(shape), dt).ap()

    TQ = TQ_LAST
    NBQ = NB // G
    CH = 1024
    NCH = TQ // CH
    assert NCH >= 1 and TQ % CH == 0
    x1_loc = scr("x1_loc", [D, TQ], F32)
    xm_t = scr("xm_loc", [D, TQ], F32)
    mix_t = scr("mix_loc", [D, TQ], F32)
    q_d = scr("qd_loc", [512, TQ], BF16)
    q_s = scr("qs_loc", [512, TQ], BF16)
    ks_loc = scr("ks_loc", [512, TQ + 2 * PADC], BF16)
    vs_loc = scr("vs_loc", [TQ + 2 * PADC, 512], BF16)

    def cbuf(name, rows):
        return nc.dram_tensor(name, [rows, 1024], BF16).ap()

    kin_d = [cbuf(f"kin_d{j}", 512) for j in range(NCH)]
    kin_s = [cbuf(f"kin_s{j}", 512) for j in range(NCH)]
    vin_d = [cbuf(f"vin_d{j}", 512) for j in range(NCH)]
    vin_s = [cbuf(f"vin_s{j}", 512) for j in range(NCH)]
    kg_d = [cbuf(f"kg_d{j}", G * 512) for j in range(NCH)]
    kg_s = [cbuf(f"kg_s{j}", G * 512) for j in range(NCH)]
    vg_d = [cbuf(f"vg_d{j}", G * 512) for j in range(NCH)]
    vg_s = [cbuf(f"vg_s{j}", G * 512) for j in range(NCH)]
    GROUPS = [list(range(g0, g0 + G)) for g0 in range(0, 8, G)]

    def vview(ap):
        return ap.rearrange("r (two c) -> (r two) c", two=2)

    def fm(ap):
        return ap.rearrange("(c p) t -> p c t", p=128)

    state = {}

    with ExitStack() as gstack:
        P = Prog(nc, gstack)

        def _prologue(e):
            pid = nc.partition_id(engines=[mybir.EngineType.SP])
            state["q0"] = (pid % G) * TQ_LAST
            state["left"] = (pid + (G - 1)) % G
            state["right"] = (pid + 1) % G

        P.q["sp"].append(_prologue)

        uid = [0]

        def sb(stack, name, shape, dt):
            uid[0] += 1
            return stack.enter_context(nc.sbuf_tensor(f"{name}_u{uid[0]}", list(shape), dt))

        def DYN(build):
            return lambda e: build(e, state["q0"])

        ps_banks = [gstack.enter_context(nc.psum_tensor(f"ps{i}", [128, 512], F32)) for i in range(8)]
        ps_res = [Res(f"ps{i}") for i in range(8)]

        ones = sb(gstack, "ones", [128, 128], BF16)
        r_ones = Res("ones")
        perm_bf = sb(gstack, "perm_bf", [128, 128], BF16)
        r_perm = Res("perm")
        g_sb = sb(gstack, "g_sb", [128, 60], F32)
        r_g = Res("g")
        bt3 = sb(gstack, "bt3", [128, NBT3], F32)
        r_bt3 = Res("bt3")
        lam_sb = sb(gstack, "lam_sb", [128, 8 * 64], F32)
        r_lam = Res("lam")
        lam_tmp = sb(gstack, "lam_tmp", [128, 64], F32)
        lam_acc = sb(gstack, "lam_acc", [128, 8], F32)
        r_lamacc = Res("lamacc")

        P.op("pool", I("memset", ones[:], 1.0), writes=[r_ones])
        P.dma("pool", I("dma_start", out=perm_bf[:], in_=perm), "ld_c0", writes=[r_perm])
        P.dma("sp", I("dma_start", out=g_sb[:], in_=gains), "ld_c1", writes=[r_g])
        P.dma("sp", I("dma_start", out=lam_sb[:], in_=lamv), "ld_c2", writes=[r_lam])
        P.dma("sp", I("dma_start", out=bt3[:], in_=bt3_in), "ld_c3", writes=[r_bt3])
        r_lt = Res("lam_tmp")
        for l in range(DEPTH):
            for m in range(2):
                qa = lam_sb[:, ((2 * m) * 2 + l) * 64:((2 * m) * 2 + l + 1) * 64]
                ka = lam_sb[:, ((2 * m + 1) * 2 + l) * 64:((2 * m + 1) * 2 + l + 1) * 64]
                col = lam_acc[:, 2 * l + m:2 * l + m + 1]
                P.op("dve", I("tensor_tensor", out=lam_tmp[:], in0=qa, in1=ka, op=ALU.mult), reads=[r_lam], writes=[r_lt])
                P.op("dve", I("tensor_reduce", out=col, in_=lam_tmp[:], op=ALU.add, axis=mybir.AxisListType.X),
                     reads=[r_lt], writes=[r_lamacc])
            c0 = lam_acc[:, 2 * l:2 * l + 2]
            P.op("act", I("activation", out=c0, in_=c0, func=AF.Exp), reads=[r_lamacc], writes=[r_lamacc])
            nl = lam_acc[:, 4 + l:5 + l]
            P.op("dve", I("scalar_tensor_tensor", out=nl, in0=lam_acc[:, 2 * l + 1:2 * l + 2], scalar=-lambda_init(l),
                          in1=lam_acc[:, 2 * l:2 * l + 1], op0=ALU.add, op1=ALU.subtract),
                 reads=[r_lamacc], writes=[r_lamacc])

        def gcol(i):
            return g_sb[:, i:i + 1]

        def rms_rstd(bufs, chunks, denom, scale_extra, ps_i):
            srt, rstd, r_srt, r_rstd, r_sq = bufs
            n = len(chunks)
            for i, ch in enumerate(chunks):
                P.op("pe", I("matmul", ps_banks[ps_i][:], lhsT=ones[:], rhs=ch, start=(i == 0), stop=(i == n - 1)),
                     reads=[r_ones, r_sq], writes=[ps_res[ps_i]], inc=(i == n - 1))
            s2 = 1.0 / (scale_extra * scale_extra)
            P.op("act", I("activation", out=srt[:], in_=ps_banks[ps_i][:], func=AF.Sqrt, scale=s2 / denom, bias=EPS * s2),
                 reads=[ps_res[ps_i]], writes=[r_srt])
            P.op("dve", I("reciprocal", out=rstd[:], in_=srt[:]), reads=[r_srt], writes=[r_rstd])

        def xcols(ap3, b, dyn):
            if not dyn:
                return lambda q0: ap3[:, :, b * 512:(b + 1) * 512]
            return lambda q0: ap3[:, :, bass.ds(q0 + b * 512, 512)]

        def cols2(ap2, b, dyn):
            if not dyn:
                return lambda q0: ap2[:, b * 512:(b + 1) * 512]
            return lambda q0: ap2[:, bass.ds(q0 + b * 512, 512)]

        for l in range(DEPTH):
            if dbg and l == 1 and dbg != "2":
                break
            last_layer = (l == DEPTH - 1)
            x_src = xT if l == 0 else x1_loc
            bt3_base = 0

            P.new_epoch()
            with ExitStack() as st:
                w = sb(st, "w_in_sb", [128, 8, D_IN], BF16)
                r_w = Res("w_in")
                for kc in range(8):
                    for hh in range(2):
                        P.dma("pool", I("dma_start", out=w[:, kc, hh * 1536:(hh + 1) * 1536],
                                        in_=w_in[l, kc * 128:(kc + 1) * 128, hh * 1536:(hh + 1) * 1536]),
                              f"ld_w{l}", writes=[r_w])
                xb = [sb(st, f"xb{i}", [128, 8, 512], F32) for i in range(2)]
                r_xb = [Res(f"xb{i}") for i in range(2)]
                cs = [sb(st, f"cs{i}", [128, 2, 512], F32) for i in range(2)]
                r_cs = [Res(f"cs{i}") for i in range(2)]
                sq = sb(st, "sq", [128, 8, 512], BF16)
                r_sq = Res("sq")
                hT = sb(st, "hT", [128, 8, 512], BF16)
                r_h = Res("hT")
                srt = sb(st, "srt", [128, 512], F32)
                rstd = sb(st, "rstd", [128, 512], F32)
                r_srt, r_rstd = Res("srt"), Res("rstd")
                qb = [sb(st, f"qb{i}", [128, 512], BF16) for i in range(2)]
                r_qb = [Res(f"qb{i}") for i in range(2)]
                t1 = [sb(st, f"t1_{i}", [128, 512], F32) for i in range(2)]
                r_t1 = [Res(f"t1_{i}") for i in range(2)]
                t2 = [sb(st, f"t2_{i}", [128, 512], F32) for i in range(2)]
                r_t2 = [Res(f"t2_{i}") for i in range(2)]
                qo = [sb(st, f"qo{i}", [128, 512], BF16) for i in range(3)]
                r_qo = [Res(f"qo{i}") for i in range(3)]
                vo = [sb(st, f"vo{i}", [128, 512], BF16) for i in range(2)]
                r_vo = [Res(f"vo{i}") for i in range(2)]
                cnts = {"ld": 0, "qk": 0, "v": 0}

                def load_blk(b, dyn):
                    i = cnts["ld"] % 2
                    cnts["ld"] += 1
                    xs_, cc_, ss_ = (x_src, cosL, sinL)
                    P.dma("sp", I("dma_start", out=xb[i][:], in_=fm(xs_)[:, :, b * 512:(b + 1) * 512]),
                          f"ld_xb{i}", writes=[r_xb[i]])
                    P.dma("sp", I("dma_start", out=cs[i][:, 0, :], in_=cc_[:, b * 512:(b + 1) * 512]),
                          f"ld_cs{i}", writes=[r_cs[i]])
                    P.dma("sp", I("dma_start", out=cs[i][:, 1, :], in_=ss_[:, b * 512:(b + 1) * 512]),
                          f"ld_cs{i}", writes=[r_cs[i]])
                    return i

                def p1_block(i, b, qk_specs, do_v):
                    P.op("act", I("activation", out=sq[:], in_=xb[i][:], func=AF.Square), reads=[r_xb[i]], writes=[r_sq])
                    rms_rstd((srt, rstd, r_srt, r_rstd, r_sq), [sq[:, c, :] for c in range(8)], float(D), 1.0, 0)
                    for c in range(8):
                        P.op("dve", I("scalar_tensor_tensor", out=hT[:, c, :], in0=xb[i][:, c, :], scalar=gcol(l * 8 + c),
                                      in1=rstd[:], op0=ALU.mult, op1=ALU.mult),
                             reads=[r_xb[i], r_g, r_rstd], writes=[r_h])
                    for col0, dst_fn in qk_specs:
                        for j in range(4):
                            cnt = cnts["qk"]
                            cnts["qk"] += 1
                            pj, pr, bi, oi = 1 + cnt % 2, 3 + cnt % 2, cnt % 2, cnt % 3
                            cbase = col0 + j * 128
                            for kc in range(8):
                                P.op("pe", I("matmul", ps_banks[pj][:], lhsT=w[:, kc, cbase:cbase + 128], rhs=hT[:, kc, :],
                                             start=(kc == 0), stop=(kc == 7)),
                                     reads=[r_w, r_h], writes=[ps_res[pj]], inc=(kc == 7))
                            P.op("act", I("copy", out=qb[bi][:], in_=ps_banks[pj][:]), reads=[ps_res[pj]], writes=[r_qb[bi]])
                            P.op("pe", I("matmul", ps_banks[pr][:], lhsT=perm_bf[:], rhs=qb[bi][:], start=True, stop=True),
                                 reads=[r_perm, r_qb[bi]], writes=[ps_res[pr]])
                            P.op("dve", I("tensor_tensor", out=t1[bi][:], in0=qb[bi][:], in1=cs[i][:, 0, :], op=ALU.mult),
                                 reads=[r_qb[bi], r_cs[i]], writes=[r_t1[bi]])
                            P.op("dve", I("tensor_tensor", out=t2[bi][:], in0=ps_banks[pr][:], in1=cs[i][:, 1, :], op=ALU.mult),
                                 reads=[ps_res[pr], r_cs[i]], writes=[r_t2[bi]])
                            P.op("dve", I("tensor_tensor", out=qo[oi][:], in0=t1[bi][:], in1=t2[bi][:], op=ALU.add),
                                 reads=[r_t1[bi], r_t2[bi]], writes=[r_qo[oi]])
                            P.dma("sp", I("dma_start", out=dst_fn(j, b), in_=qo[oi][:]),
                                  f"st_qk{oi}", reads=[r_qo[oi]], store=True)
                    if do_v:
                        for col0, vins in ((1024, vin_d), (2560, vin_s)):
                            for sub in range(4):
                                pv, vi = 5 + cnts["v"] % 2, cnts["v"] % 2
                                cnts["v"] += 1
                                for kc in range(8):
                                    P.op("pe", I("matmul", ps_banks[pv][:], lhsT=hT[:, kc, sub * 128:(sub + 1) * 128],
                                                 rhs=w[:, kc, col0:col0 + 512], start=(kc == 0), stop=(kc == 7)),
                                         reads=[r_w, r_h], writes=[ps_res[pv]], inc=(kc == 7))
                                P.op("act", I("copy", out=vo[vi][:], in_=ps_banks[pv][:]), reads=[ps_res[pv]], writes=[r_vo[vi]])
                                r0 = (b * 512) % CH + sub * 128
                                P.dma("sp", I("dma_start", out=vview(vins[(b * 512) // CH])[r0:r0 + 128, :], in_=vo[vi][:]),
                                      f"st_v{vi}", reads=[r_vo[vi]], store=True)

                def q_dst(t):
                    return lambda j, b: t[j * 128:(j + 1) * 128, b * 512:(b + 1) * 512]

                def k_dst(chunks):
                    return lambda j, b: chunks[(b * 512) // CH][j * 128:(j + 1) * 128, (b * 512) % CH:(b * 512) % CH + 512]

                specs = [(0, q_dst(q_d)), (1536, q_dst(q_s)), (512, k_dst(kin_d)), (2048, k_dst(kin_s))]
                nxt = load_blk(0, True)
                for b in range(NBQ):
                    i = nxt
                    if b + 1 < NBQ:
                        nxt = load_blk(b + 1, True)
                    p1_block(i, b, specs, True)
                    if ((b + 1) * 512) % CH == 0:
                        j = ((b + 1) * 512) // CH - 1
                        P.dram_barrier(engines=("pool",))
                        for ins_, outs_ in ((kin_d, kg_d), (vin_d, vg_d), (kin_s, kg_s), (vin_s, vg_s)):
                            P.collective(ins_[j], outs_[j], GROUPS)
            P.dram_barrier()
            P.dram_barrier()

            P.new_epoch()
            with ExitStack() as st:
                kTb = [sb(st, f"kT{i}", [128, T], BF16) for i in range(2)]
                vvb = [sb(st, f"vv{i}", [128, NKT, 128], BF16) for i in range(2)]
                r_kTb = [Res(f"kT{i}") for i in range(2)]
                r_vvb = [Res(f"vv{i}") for i in range(2)]

                def load_kv(h):
                    i = h % 2
                    for r in range(G):
                        for j in range(NCH):
                            t0 = r * TQ + j * CH
                            P.dma("sp", I("dma_start", out=kTb[i][:, t0:t0 + CH],
                                          in_=kg_d[j][r * 512 + h * 128:r * 512 + (h + 1) * 128, :]), f"ld_kT{i}", writes=[r_kTb[i]])
                            P.dma("sp", I("dma_start", out=vvb[i][:, t0 // 128:(t0 + CH) // 128, :],
                                          in_=vview(vg_d[j])[r * CH:(r + 1) * CH, h * 128:(h + 1) * 128].rearrange(
                                              "(k p) e -> p k e", p=128)), f"ld_vv{i}", writes=[r_vvb[i]])

                def mask_kv(h):
                    i = h % 2
                    for r in range(G):
                        P.op("pool", I("tensor_scalar", out=kTb[i][:, r * TQ:(r + 1) * TQ], in0=kTb[i][:, r * TQ:(r + 1) * TQ],
                                       scalar1=gcol(54 + r), scalar2=1.0, op0=ALU.mult, op1=ALU.mult),
                             reads=[r_kTb[i], r_g], writes=[r_kTb[i]])
                        P.op("pool", I("tensor_scalar", out=vvb[i][:, r * (TQ // 128):(r + 1) * (TQ // 128), :],
                                       in0=vvb[i][:, r * (TQ // 128):(r + 1) * (TQ // 128), :],
                                       scalar1=gcol(54 + r), scalar2=1.0, op0=ALU.mult, op1=ALU.mult),
                             reads=[r_vvb[i], r_g], writes=[r_vvb[i]])

                qA = [sb(st, f"qA{i}", [128, 512], BF16) for i in range(2)]
                qB = [sb(st, f"qB{i}", [128, 512], BF16) for i in range(2)]
                r_qA = [Res(f"qA{i}") for i in range(2)]
                r_qB = [Res(f"qB{i}") for i in range(2)]
                NPB = 8
                pt = [sb(st, f"pt{i}", [128, 512], BF16) for i in range(2 * NPB)]
                r_pt = [Res(f"pt{i}") for i in range(2 * NPB)]
                sab = [sb(st, f"sab{i}", [128, 512], BF16) for i in range(4)]
                scd = [sb(st, f"scd{i}", [128, 512], BF16) for i in range(4)]
                s4 = [sb(st, f"s4_{i}", [128, 512], BF16) for i in range(4)]
                r_sab = [Res(f"sab{i}") for i in range(4)]
                r_scd = [Res(f"scd{i}") for i in range(4)]
                r_s4 = [Res(f"s4_{i}") for i in range(4)]
                slot_of = [0, 0, 0, 0]
                q4cnt = 0
                s8 = [sb(st, f"s8_{i}", [128, 512], BF16) for i in range(4)]
                r_s8 = [Res(f"s8_{i}") for i in range(4)]
                o8cnt = 0
                oc = [sb(st, f"oc{i}", [128, 512], F32) for i in range(4)]
                r_oc = [Res(f"oc{i}") for i in range(4)]
                ob = [sb(st, f"ob{i}", [128, 512], F32) for i in range(2)]
                r_ob = [Res(f"ob{i}") for i in range(2)]
                for i in range(2):
                    P.op("pool", I("memset", qA[i][64:128, :], 0.0), writes=[r_qA[i]])
                    P.op("pool", I("memset", qB[i][0:64, :], 0.0), writes=[r_qB[i]])
                qcnt = 0
                ucnt = 0
                load_kv(0)
                mask_kv(0)
                for h in range(4):
                    kT, vv, r_kT, r_vv = kTb[h % 2], vvb[h % 2], r_kTb[h % 2], r_vvb[h % 2]

                    def load_q(qbi, h=h):
                        i = qbi % 2
                        P.dma("sp", I("dma_start", out=qA[i][0:64, :], in_=q_d[h * 128:h * 128 + 64, qbi * 512:(qbi + 1) * 512]),
                              f"ld_qA{i}", writes=[r_qA[i]])
                        P.dma("sp", I("dma_start", out=qB[i][64:128, :],
                                      in_=q_d[h * 128 + 64:h * 128 + 128, qbi * 512:(qbi + 1) * 512]),
                              f"ld_qB{i}", writes=[r_qB[i]])

                    load_q(0)
                    for qbi in range(NBQ):
                        qi = qbi % 2
                        if qbi + 1 < NBQ:
                            load_q(qbi + 1)
                        if qbi == 0 and h + 1 < 4:
                            load_kv(h + 1)
                        if qbi == NBQ - 1 and h + 1 < 4:
                            mask_kv(h + 1)

                        def qk(kt, qi=qi, kT=kT, r_kT=r_kT):
                            sb_ = (kt % 2) * 2
                            P.op("pe", I("matmul", ps_banks[sb_][:], lhsT=kT[:, kt * 128:(kt + 1) * 128], rhs=qA[qi][:],
                                         start=True, stop=True), reads=[r_kT, r_qA[qi]], writes=[ps_res[sb_]])
                            P.op("pe", I("matmul", ps_banks[sb_ + 1][:], lhsT=kT[:, kt * 128:(kt + 1) * 128], rhs=qB[qi][:],
                                         start=True, stop=True), reads=[r_kT, r_qB[qi]], writes=[ps_res[sb_ + 1]])

                        qk(0)
                        deferred = []
                        for kt in range(NKT):
                            if kt + 1 < NKT:
                                qk(kt + 1)
                            sb_ = (kt % 2) * 2
                            bias_ap = None
                            pi = (ucnt % NPB) * 2
                            ucnt += 1
                            slot_of[kt % 4] = pi
                            for m in range(2):
                                if bias_ap is not None:
                                    P.op("act", I("activation", out=pt[pi + m][:], in_=ps_banks[sb_ + m][:], func=AF.Exp,
                                                  bias=bias_ap, scale=0.125),
                                         reads=[ps_res[sb_ + m], r_g], writes=[r_pt[pi + m]])
                                else:
                                    P.op("act", I("activation", out=pt[pi + m][:], in_=ps_banks[sb_ + m][:], func=AF.Exp,
                                                  scale=0.125), reads=[ps_res[sb_ + m]], writes=[r_pt[pi + m]])
                            last = (kt == NKT - 1)
                            for m in range(2):
                                P.op("pe", I("matmul", ps_banks[4 + m][:], lhsT=vv[:, kt, :], rhs=pt[pi + m][:],
                                             start=(kt == 0), stop=last),
                                     reads=[r_vv, r_pt[pi + m]], writes=[ps_res[4 + m]], inc=last)
                            r4 = kt % 4
                            if r4 == 1:
                                for m in range(2):
                                    bi = 2 * (q4cnt % 2) + m
                                    P.op("dve", I("tensor_tensor", out=sab[bi][:], in0=pt[slot_of[0] + m][:],
                                                  in1=pt[slot_of[1] + m][:], op=ALU.add),
                                         reads=[r_pt[slot_of[0] + m], r_pt[slot_of[1] + m]], writes=[r_sab[bi]])
                            if r4 == 3:
                                for m in range(2):
                                    bi = 2 * (q4cnt % 2) + m
                                    P.op("pool", I("tensor_tensor", out=scd[bi][:], in0=pt[slot_of[2] + m][:],
                                                   in1=pt[slot_of[3] + m][:], op=ALU.add),
                                         reads=[r_pt[slot_of[2] + m], r_pt[slot_of[3] + m]], writes=[r_scd[bi]])
                                    P.op("dve", I("tensor_tensor", out=s4[bi][:], in0=sab[bi][:], in1=scd[bi][:], op=ALU.add),
                                         reads=[r_sab[bi], r_scd[bi]], writes=[r_s4[bi]])
                                    if kt % 8 == 7:
                                        bprev = 2 * ((q4cnt + 1) % 2) + m
                                        b8 = 2 * (o8cnt % 2) + m
                                        P.op("dve", I("tensor_tensor", out=s8[b8][:], in0=s4[bprev][:], in1=s4[bi][:], op=ALU.add),
                                             reads=[r_s4[bprev], r_s4[bi]], writes=[r_s8[b8]])
                                        deferred.append((kt + 2, m, b8, kt == 7, last))
                                q4cnt += 1
                                if kt % 8 == 7:
                                    o8cnt += 1
                            while deferred and (deferred[0][0] <= kt or last):
                                _, m, bi, first_, last_ = deferred.pop(0)
                                P.op("pe", I("matmul", ps_banks[6 + m][:], lhsT=ones[:], rhs=s8[bi][:], start=first_, stop=last_),
                                     reads=[r_ones, r_s8[bi]], writes=[ps_res[6 + m]], inc=last_)
                        for a in range(2):
                            P.op("dve", I("tensor_copy", out=oc[a][:], in_=ps_banks[4 + a][:]),
                                 reads=[ps_res[4 + a]], writes=[r_oc[a]])
                        for a in range(2, 4):
                            P.op("dve", I("tensor_scalar", out=oc[a][:], in0=ps_banks[4 + a][:], scalar1=gcol(58), scalar2=None,
                                          op0=ALU.add), reads=[ps_res[4 + a], r_g], writes=[r_oc[a]])
                        for m in range(2):
                            P.op("dve", I("reciprocal", out=oc[2 + m][:], in_=oc[2 + m][:]),
                                 reads=[r_oc[2 + m]], writes=[r_oc[2 + m]])
                            P.op("dve", I("tensor_tensor", out=oc[m][:], in0=oc[m][:], in1=oc[2 + m][:], op=ALU.mult),
                                 reads=[r_oc[m], r_oc[2 + m]], writes=[r_oc[m]])
                        oi = qcnt % 2
                        qcnt += 1
                        P.op("dve", I("scalar_tensor_tensor", out=ob[oi][:], in0=oc[1][:], scalar=lam_acc[:, 4 + l:5 + l],
                                      in1=oc[0][:], op0=ALU.mult, op1=ALU.add),
                             reads=[r_oc[0], r_oc[1], r_lamacc], writes=[r_ob[oi]])
                        P.dma("sp", I("dma_start", out=mix_t[h * 128:(h + 1) * 128, qbi * 512:(qbi + 1) * 512], in_=ob[oi][:]),
                              f"st_mx{oi}", reads=[r_ob[oi]], store=True)

            P.new_epoch()
            with ExitStack() as st:
                WT = TQ + 2 * PADC
                WKT = WT // 128
                for j in range(NCH):
                    P.dma("sp", I("dma_start", out=ks_loc[:, PADC + j * CH:PADC + (j + 1) * CH], in_=kin_s[j]), "st_kloc", store=True)
                    P.dma("sp", I("dma_start", out=vs_loc[PADC + j * CH:PADC + (j + 1) * CH, :], in_=vview(vin_s[j])), "st_kloc", store=True)
                P.dma("sp", DYN(lambda e, q0: e.dma_start(out=ks_loc[:, 0:PADC], in_=kg_s[NCH - 1][bass.ds(state["left"] * 512, 512), :])),
                      "st_kloc", store=True)
                P.dma("sp", DYN(lambda e, q0: e.dma_start(out=ks_loc[:, PADC + TQ:], in_=kg_s[0][bass.ds(state["right"] * 512, 512), :])),
                      "st_kloc", store=True)
                P.dma("sp", DYN(lambda e, q0: e.dma_start(out=vs_loc[0:PADC, :], in_=vview(vg_s[NCH - 1])[bass.ds(state["left"] * CH, CH), :])),
                      "st_kloc", store=True)
                P.dma("sp", DYN(lambda e, q0: e.dma_start(out=vs_loc[PADC + TQ:, :], in_=vview(vg_s[0])[bass.ds(state["right"] * CH, CH), :])),
                      "st_kloc", store=True)
                P.dram_barrier()
                kT = sb(st, "kTs", [128, WT], BF16)
                vv = sb(st, "vvs", [128, WKT, 128], BF16)
                r_kT, r_vv = Res("kTs"), Res("vvs")
                cm = sb(st, "cm", [128, 20, 512], BF16)
                r_cm = Res("cm")
                for j in range(20):
                    P.dma("pool", I("dma_start", out=cm[:, j, :], in_=cmask_in[:, j * 512:(j + 1) * 512]), "ld_cm", writes=[r_cm])
                qA = [sb(st, f"sqA{i}", [128, 512], BF16) for i in range(2)]
                qB = [sb(st, f"sqB{i}", [128, 512], BF16) for i in range(2)]
                r_qA = [Res(f"sqA{i}") for i in range(2)]
                r_qB = [Res(f"sqB{i}") for i in range(2)]
                NPB = 6
                LOOK = 3
                pt = [sb(st, f"spt{i}", [128, 512], BF16) for i in range(NPB)]
                r_pt = [Res(f"spt{i}") for i in range(NPB)]
                pm = [sb(st, f"spm{i}", [128, 512], BF16) for i in range(NPB)]
                r_pm = [Res(f"spm{i}") for i in range(NPB)]
                oc = [sb(st, f"soc{i}", [64, 512], F32) for i in range(2)]
                r_oc = [Res(f"soc{i}") for i in range(2)]
                ob = [sb(st, f"sob{i}", [64, 512], F32) for i in range(2)]
                r_ob = [Res(f"sob{i}") for i in range(2)]
                for i in range(2):
                    P.op("pool", I("memset", qA[i][64:128, :], 0.0), writes=[r_qA[i]])
                    P.op("pool", I("memset", qB[i][0:64, :], 0.0), writes=[r_qB[i]])
                ucnt = 0
                qcnt = 0
                scnt = [0]
                for cp in range(4):
                    nparts = 4
                    for part in range(nparts):
                        c0, c1 = part * (WT // nparts), (part + 1) * (WT // nparts)
                        k0, k1 = part * (WKT // nparts), (part + 1) * (WKT // nparts)
                        ksrc, vsrc = ks_loc, vs_loc
                        P.dma("sp" if part % 2 == 0 else "pool",
                              I("dma_start", out=kT[:, c0:c1], in_=ksrc[cp * 128:(cp + 1) * 128, c0:c1]),
                              "ld_kTs", writes=[r_kT])
                        P.dma("pool" if part % 2 == 0 else "sp",
                              I("dma_start", out=vv[:, k0:k1, :],
                                in_=vsrc[k0 * 128:k1 * 128, cp * 128:(cp + 1) * 128].rearrange("(k p) e -> p k e", p=128)),
                              "ld_vvs", writes=[r_vv])

                    def load_q(qbi, cp=cp):
                        i = qbi % 2
                        P.dma("sp", I("dma_start", out=qA[i][0:64, :], in_=q_s[cp * 128:cp * 128 + 64, qbi * 512:(qbi + 1) * 512]),
                              f"ld_sqA{i}", writes=[r_qA[i]])
                        P.dma("sp", I("dma_start", out=qB[i][64:128, :],
                                      in_=q_s[cp * 128 + 64:cp * 128 + 128, qbi * 512:(qbi + 1) * 512]),
                              f"ld_sqB{i}", writes=[r_qB[i]])

                    load_q(0)
                    for qbi in range(NBQ):
                        qi = qbi % 2
                        if qbi + 1 < NBQ:
                            load_q(qbi + 1)
                        for hh in range(2):
                            qop, r_qop = (qA[qi], r_qA[qi]) if hh == 0 else (qB[qi], r_qB[qi])
                            po, pl = 4 + hh, 6 + hh
                            tiles = list(range(20))
                            sbank = {}

                            def crange(j):
                                return max(0, 128 * j - 2048), min(512, 128 * j + 128)

                            def qk(idx, tiles=tiles, qop=qop, r_qop=r_qop, qbi=qbi):
                                wt = qbi * 4 + tiles[idx]
                                c0, c1 = crange(tiles[idx])
                                sbk = scnt[0] % 4
                                scnt[0] += 1
                                P.op("pe", I("matmul", ps_banks[sbk][:, c0:c1], lhsT=kT[:, wt * 128:(wt + 1) * 128], rhs=qop[:, c0:c1],
                                             start=True, stop=True), reads=[r_kT, r_qop], writes=[ps_res[sbk]])
                                sbank[idx] = sbk

                            for a in range(min(LOOK, len(tiles))):
                                qk(a)
                            for idx, j in enumerate(tiles):
                                if idx + LOOK < len(tiles):
                                    qk(idx + LOOK)
                                sbk = sbank[idx]
                                wt = qbi * 4 + j
                                bcol = bt3[:, bt3_base + qbi * 20 + j:bt3_base + qbi * 20 + j + 1]
                                pi = ucnt % NPB
                                ucnt += 1
                                c0, c1 = crange(j)
                                P.op("act", I("activation", out=pt[pi][:, c0:c1], in_=ps_banks[sbk][:, c0:c1], func=AF.Exp, bias=bcol,
                                              scale=0.125), reads=[ps_res[sbk], r_bt3], writes=[r_pt[pi]])
                                P.op("dve", I("tensor_tensor", out=pm[pi][:, c0:c1], in0=pt[pi][:, c0:c1], in1=cm[:, j, c0:c1], op=ALU.mult),
                                     reads=[r_pt[pi], r_cm], writes=[r_pm[pi]])
                                first, last = idx == 0, idx == len(tiles) - 1
                                P.op("pe", I("matmul", ps_banks[po][0:64, c0:c1], lhsT=vv[:, wt, hh * 64:(hh + 1) * 64], rhs=pm[pi][:, c0:c1],
                                             start=first, stop=last, skip_group_check=True),
                                     reads=[r_vv, r_pm[pi]], writes=[ps_res[po]], inc=last)
                                P.op("pe", I("matmul", ps_banks[pl][0:64, c0:c1], lhsT=ones[:, 0:64], rhs=pm[pi][:, c0:c1],
                                             start=first, stop=last, skip_group_check=True),
                                     reads=[r_ones, r_pm[pi]], writes=[ps_res[pl]], inc=last)
                            ci = qcnt % 2
                            qcnt += 1
                            P.op("dve", I("reciprocal", out=oc[ci][:], in_=ps_banks[pl][0:64, :]), reads=[ps_res[pl]], writes=[r_oc[ci]])
                            P.op("dve", I("tensor_tensor", out=ob[ci][:], in0=ps_banks[po][0:64, :], in1=oc[ci][:], op=ALU.mult),
                                 reads=[ps_res[po], r_oc[ci]], writes=[r_ob[ci]])
                            row0 = 512 + (cp * 2 + hh) * 64
                            P.dma("sp", I("dma_start", out=mix_t[row0:row0 + 64, qbi * 512:(qbi + 1) * 512], in_=ob[ci][:]),
                                  f"st_ms{ci}", reads=[r_ob[ci]], store=True)
            P.dram_barrier()

            P.new_epoch()
            wst = ExitStack()
            w1 = sb(wst, "w1", [128, 8, D_FF], BF16)
            r_w1 = Res("w1")
            with ExitStack() as st:
                wo = sb(st, "wo", [128, 8, D], BF16)
                r_wo = Res("wo")
                for kc in range(8):
                    P.dma("pool", I("dma_start", out=wo[:, kc, :], in_=w_out[l, kc * 128:(kc + 1) * 128, :]), f"ld_wo{l}", writes=[r_wo])
                for kc in range(8):
                    for hh in range(2):
                        P.dma("pool", I("dma_start", out=w1[:, kc, hh * 2048:(hh + 1) * 2048],
                                        in_=w_ff1[l, kc * 128:(kc + 1) * 128, hh * 2048:(hh + 1) * 2048]), f"ld_w1{l}", writes=[r_w1])
                xb = [sb(st, f"axb{i}", [128, 8, 512], F32) for i in range(2)]
                r_xb = [Res(f"axb{i}") for i in range(2)]
                mr = [sb(st, f"mr{i}", [128, 8, 512], F32) for i in range(2)]
                r_mr = [Res(f"mr{i}") for i in range(2)]
                sq = sb(st, "asq", [128, 8, 512], BF16)
                r_sq = Res("asq")
                mx = sb(st, "amx", [128, 8, 512], BF16)
                r_mx = Res("amx")
                srt = [sb(st, f"asrt{i}", [128, 512], F32) for i in range(2)]
                rstd = [sb(st, f"arstd{i}", [128, 512], F32) for i in range(2)]
                r_srt = [Res(f"asrt{i}") for i in range(2)]
                r_rstd = [Res(f"arstd{i}") for i in range(2)]

                def load_blk(b):
                    i = b % 2
                    xs_ = x_src
                    P.dma("sp", I("dma_start", out=xb[i][:], in_=fm(xs_)[:, :, b * 512:(b + 1) * 512]), f"ld_axb{i}", writes=[r_xb[i]])
                    P.dma("pool", I("dma_start", out=mr[i][:], in_=fm(mix_t)[:, :, b * 512:(b + 1) * 512]), f"ld_mr{i}", writes=[r_mr[i]])

                load_blk(0)
                ncnt = 0
                pcnt = 0
                li = lambda_init(l)
                for b in range(NBQ):
                    i = b % 2
                    if b + 1 < NBQ:
                        load_blk(b + 1)
                    P.op("act", I("activation", out=sq[:], in_=mr[i][:], func=AF.Square), reads=[r_mr[i]], writes=[r_sq])
                    for c in range(4):
                        ni = ncnt % 2
                        ncnt += 1
                        rms_rstd((srt[ni], rstd[ni], r_srt[ni], r_rstd[ni], r_sq), [sq[:, c, :]], 128.0, 1.0 - li, ni)
                        P.op("dve", I("scalar_tensor_tensor", out=mx[:, c, :], in0=mr[i][:, c, :], scalar=gcol(40 + l),
                                      in1=rstd[ni][:], op0=ALU.mult, op1=ALU.mult),
                             reads=[r_mr[i], r_g, r_rstd[ni]], writes=[r_mx])
                    ni = ncnt % 2
                    ncnt += 1
                    rms_rstd((srt[ni], rstd[ni], r_srt[ni], r_rstd[ni], r_sq), [sq[:, c, :] for c in range(4, 8)], 512.0, 1.0, ni)
                    for c in range(4, 8):
                        P.op("dve", I("scalar_tensor_tensor", out=mx[:, c, :], in0=mr[i][:, c, :], scalar=gcol(42 + l * 4 + (c - 4)),
                                      in1=rstd[ni][:], op0=ALU.mult, op1=ALU.mult),
                             reads=[r_mr[i], r_g, r_rstd[ni]], writes=[r_mx])
                    for oc_ in range(8):
                        pb = 2 + pcnt % 4
                        pcnt += 1
                        for kc in range(8):
                            P.op("pe", I("matmul", ps_banks[pb][:], lhsT=wo[:, kc, oc_ * 128:(oc_ + 1) * 128], rhs=mx[:, kc, :],
                                         start=(kc == 0), stop=(kc == 7)), reads=[r_wo, r_mx], writes=[ps_res[pb]], inc=(kc == 7))
                        P.op("dve", I("tensor_tensor", out=xb[i][:, oc_, :], in0=xb[i][:, oc_, :], in1=ps_banks[pb][:], op=ALU.add),
                             reads=[r_xb[i], ps_res[pb]], writes=[r_xb[i]])
                    P.dma("sp", I("dma_start", out=fm(xm_t)[:, :, b * 512:(b + 1) * 512], in_=xb[i][:]),
                          f"st_xm{i}", reads=[r_xb[i]], store=True)
            P.dram_barrier()

            P.new_epoch()
            with ExitStack() as st:
                w2 = sb(st, "w2", [128, 32, D], BF16)
                r_w2 = Res("w2")
                for fc in range(32):
                    P.dma("pool", I("dma_start", out=w2[:, fc, :], in_=w_ff2[l, fc * 128:(fc + 1) * 128, :]), f"ld_w2{l}", writes=[r_w2])
                xbb = [sb(st, f"bxb{i}", [128, 8, 512], F32) for i in range(2)]
                r_xbb = [Res(f"bxb{i}") for i in range(2)]
                h2 = sb(st, "bh2", [128, 8, 512], BF16)
                r_h2 = Res("bh2")
                sq, r_sq = h2, r_h2
                uu = sb(st, "buu", [128, 16, 512], BF16)
                r_uu = Res("buu")
                rr = [sb(st, f"brr{i}", [128, 512], F32) for i in range(2)]
                r_rr = [Res(f"brr{i}") for i in range(2)]
                srt = sb(st, "bsrt", [128, 512], F32)
                rstd = sb(st, "brstd", [128, 512], F32)
                r_srt, r_rstd = Res("bsrt"), Res("brstd")
                pcnt = 0
                rcnt = 0
                def load_bx(b):
                    P.dma("sp", I("dma_start", out=xbb[b % 2][:], in_=fm(xm_t)[:, :, b * 512:(b + 1) * 512]),
                          f"ld_bxb{b % 2}", writes=[r_xbb[b % 2]])

                load_bx(0)
                for b in range(NBQ):
                    xb, r_xb = xbb[b % 2], r_xbb[b % 2]
                    if b + 1 < NBQ:
                        load_bx(b + 1)
                    P.op("act", I("activation", out=sq[:], in_=xb[:], func=AF.Square), reads=[r_xb], writes=[r_sq])
                    rms_rstd((srt, rstd, r_srt, r_rstd, r_sq), [sq[:, c, :] for c in range(8)], float(D), 1.0, 0)
                    for c in range(8):
                        P.op("dve", I("scalar_tensor_tensor", out=h2[:, c, :], in0=xb[:, c, :], scalar=gcol(16 + l * 8 + c),
                                      in1=rstd[:], op0=ALU.mult, op1=ALU.mult), reads=[r_xb, r_g, r_rstd], writes=[r_h2])
                    for half in range(2):
                        for f in range(16):
                            fc = half * 16 + f
                            pb = 1 + pcnt % 3
                            pcnt += 1
                            ri = rcnt % 2
                            rcnt += 1
                            for kc in range(8):
                                P.op("pe", I("matmul", ps_banks[pb][:], lhsT=w1[:, kc, fc * 128:(fc + 1) * 128], rhs=h2[:, kc, :],
                                             start=(kc == 0), stop=(kc == 7)), reads=[r_w1, r_h2], writes=[ps_res[pb]], inc=(kc == 7))
                            P.op("act", I("activation", out=rr[ri][:], in_=ps_banks[pb][:], func=AF.Relu),
                                 reads=[ps_res[pb]], writes=[r_rr[ri]])
                            P.op("dve", I("tensor_tensor", out=uu[:, f, :], in0=rr[ri][:], in1=rr[ri][:], op=ALU.mult),
                                 reads=[r_rr[ri]], writes=[r_uu])
                        for oc_ in range(8):
                            pb = 4 + pcnt % 4
                            pcnt += 1
                            for f in range(16):
                                fc = half * 16 + f
                                P.op("pe", I("matmul", ps_banks[pb][:], lhsT=w2[:, fc, oc_ * 128:(oc_ + 1) * 128], rhs=uu[:, f, :],
                                             start=(f == 0), stop=(f == 15)), reads=[r_w2, r_uu], writes=[ps_res[pb]], inc=(f == 15))
                            P.op("dve", I("tensor_tensor", out=xb[:, oc_, :], in0=xb[:, oc_, :], in1=ps_banks[pb][:], op=ALU.add),
                                 reads=[r_xb, ps_res[pb]], writes=[r_xb])
                    if not last_layer:
                        P.dma("sp", I("dma_start", out=fm(x1_loc)[:, :, b * 512:(b + 1) * 512], in_=xb[:]),
                              "st_x1", reads=[r_xb], store=True)
                    else:
                        P.op("act", I("activation", out=sq[:], in_=xb[:], func=AF.Square), reads=[r_xb], writes=[r_sq])
                        rms_rstd((srt, rstd, r_srt, r_rstd, r_sq), [sq[:, c, :] for c in range(8)], float(D), 1.0, 0)
                        for c in range(8):
                            P.op("dve", I("scalar_tensor_tensor", out=xb[:, c, :], in0=xb[:, c, :], scalar=gcol(32 + c),
                                          in1=rstd[:], op0=ALU.mult, op1=ALU.mult), reads=[r_xb, r_g, r_rstd], writes=[r_xb])
                        P.dma("sp", I("dma_start", out=fm(yT)[:, :, b * 512:(b + 1) * 512], in_=xb[:]),
                              "st_y", reads=[r_xb], store=True)
            wst.close()
            P.dram_barrier()

        with nc.Block() as block:
            P.play(block)
    return nc


def _host_tables(T, pos):
    half = 32
    inv_freq = (10000.0 ** (-np.arange(half, dtype=np.float32) / half)).astype(np.float32)
    ang = pos.astype(np.float32)[None, :] * inv_freq[:, None]
    cos = np.cos(ang).astype(np.float32)
    sin = np.sin(ang).astype(np.float32)
    cosT = np.tile(cos, (4, 1))
    sinT = np.tile(sin, (4, 1))
    return np.ascontiguousarray(cosT), np.ascontiguousarray(sinT)


def _perm_matrix():
    Pm = np.zeros((128, 128), np.float32)
    for blk in range(2):
        for d in range(64):
            m = blk * 64 + d
            if d < 32:
                Pm[blk * 64 + d + 32, m] = -1.0
            else:
                Pm[blk * 64 + d - 32, m] = 1.0
    return Pm


def _cmask():
    cm = np.zeros((128, 20, 512), np.float32)
    kk = np.arange(128)[:, None]
    qq = np.arange(512)[None, :]
    for j in range(20):
        delta = 128 * j - 1024 + kk - qq
        a = np.abs(delta)
        c = (a <= 64).astype(np.float32)
        c += ((delta % 4 == 0) & (a <= 256)).astype(np.float32)
        c += ((delta % 16 == 0) & (a <= 1024)).astype(np.float32)
        cm[:, j, :] = c
    return np.ascontiguousarray(cm.reshape(128, 20 * 512))


def _bt3_table(T, G, quarter, is_prompt):
    NB, NKT = T // 512, T // 128
    cols = []
    for mode_blocks, base in ((NB // G, quarter * (NB // G)),):
        for qb in range(mode_blocks):
            gqb = base + qb
            for j in range(20):
                kt = 4 * gqb - 8 + j
                ok = 0 <= kt < NKT
                if ok and is_prompt and ((kt < NKT // 2) != (gqb < NB // 2)):
                    ok = False
                cols.append(0.0 if ok else NEG)
    t = np.asarray(cols, np.float32)
    return np.ascontiguousarray(np.broadcast_to(t[None, :], (128, t.size)))


def _pack_gains(norm1_g, norm2_g, final_norm_g, diff_norm_g, dil_norm_g, cross_bias, kh0=0.0, kh1=0.0,
                keep=(1.0, 1.0, 1.0, 1.0), neg_lcorr=0.0):
    g = np.zeros((128, 60), np.float32)
    g[:, 54:58] = np.asarray(keep, np.float32)[None, :]
    g[:, 58] = neg_lcorr
    for l in range(DEPTH):
        g[:, l * 8:(l + 1) * 8] = norm1_g[l].reshape(8, 128).T
        g[:, 16 + l * 8:16 + (l + 1) * 8] = norm2_g[l].reshape(8, 128).T
        g[:, 40 + l] = diff_norm_g[l]
        g[:, 42 + l * 4:42 + (l + 1) * 4] = dil_norm_g[l].reshape(4, 128).T
    g[:, 32:40] = final_norm_g.reshape(8, 128).T
    g[:, 50] = 0.0
    g[:, 51] = cross_bias
    g[:, 52] = kh0
    g[:, 53] = kh1
    return g


_NC_CACHE = {}


def run_cores(T, seqs, weights, n_cores=8, G=4):
    if T not in _NC_CACHE:
        _NC_CACHE[T] = build_nc(T)
    nc = _NC_CACHE[T]
    (norm1_g, w_in, lq1, lk1, lq2, lk2, diff_norm_g, dil_norm_g, w_out, norm2_g, w_ff1, w_ff2, final_norm_g) = weights
    lamv = np.concatenate([np.asarray(a, np.float32).reshape(-1) for a in (lq1, lk1, lq2, lk2)])
    lamv = np.ascontiguousarray(np.broadcast_to(lamv[None, :], (128, lamv.size)))
    common = dict(w_in=np.ascontiguousarray(w_in, np.float32), w_out=np.ascontiguousarray(w_out, np.float32),
                  w_ff1=np.ascontiguousarray(w_ff1, np.float32), w_ff2=np.ascontiguousarray(w_ff2, np.float32),
                  lamv=lamv, perm=_perm_matrix(), cmask=_cmask())
    per_seq = []
    for x, pos, is_prompt in seqs:
        cosT, sinT = _host_tables(T, pos)
        per_seq.append((np.ascontiguousarray(np.asarray(x, np.float32).T), cosT, sinT, is_prompt))
    in_maps = []
    for c in range(n_cores):
        xT, cosT, sinT, is_prompt = per_seq[(c // G) % len(per_seq)]
        quarter = c % G
        m = dict(common)
        TQ = T // G
        m["xT"] = np.ascontiguousarray(xT[:, quarter * TQ:(quarter + 1) * TQ])
        m["cosL"] = np.ascontiguousarray(cosT[:, quarter * TQ:(quarter + 1) * TQ])
        m["sinL"] = np.ascontiguousarray(sinT[:, quarter * TQ:(quarter + 1) * TQ])
        kh0 = NEG if (is_prompt and quarter >= G // 2) else 0.0
        kh1 = NEG if (is_prompt and quarter < G // 2) else 0.0
        m["gains"] = _pack_gains(np.asarray(norm1_g), np.asarray(norm2_g), np.asarray(final_norm_g),
                                 np.asarray(diff_norm_g), np.asarray(dil_norm_g), NEG if is_prompt else 0.0, kh0, kh1,
                                 keep=[1.0 if (not is_prompt or ((r < G // 2) == (quarter < G // 2))) else 0.0 for r in range(G)],
                                 neg_lcorr=(-(T // 2) if is_prompt else 0.0))
        m["bt3"] = _bt3_table(T, G, quarter, is_prompt)
        in_maps.append(m)
    res = run_bass_kernel_spmd(nc, in_maps, core_ids=list(range(n_cores)))
    if os.environ.get("KDBG", ""):
        return [res.results[c] for c in range(n_cores)]
    outs = []
    for s in range(len(seqs)):
        yT = np.concatenate([res.results[s * G + q]["yT"] for q in range(G)], axis=1)
        outs.append(np.ascontiguousarray(yT.T))
    return outs


def kernel(x_prompt, x_sample, norm1_g, w_in, lambda_q1, lambda_k1, lambda_q2, lambda_k2,
           diff_norm_g, dil_norm_g, w_out, norm2_g, w_ff1, w_ff2, final_norm_g):
    x_prompt = np.asarray(x_prompt, np.float32)
    x_sample = np.asarray(x_sample, np.float32)
    B, S, _ = x_prompt.shape
    T = x_sample.shape[1]
    assert B * S == T and x_sample.shape[0] == 1
    weights = tuple(np.asarray(a, np.float32) for a in (
        norm1_g, w_in, lambda_q1, lambda_k1, lambda_q2, lambda_k2, diff_norm_g, dil_norm_g, w_out, norm2_g,
        w_ff1, w_ff2, final_norm_g))
    seqs = [
        (x_sample[0], np.arange(T), False),
        (x_prompt.reshape(T, D), np.concatenate([np.arange(S), np.arange(S)]), True),
    ]
    ys, yp = run_cores(T, seqs, weights)
    return (yp.reshape(B, S, D).astype(np.float32), ys.reshape(1, T, D).astype(np.float32))
```

```python
import math
import os
from contextlib import ExitStack

import numpy as np
import concourse.bass as bass
import concourse.mybir as mybir
from concourse.bass_utils import run_bass_kernel_spmd

F32 = mybir.dt.float32
BF16 = mybir.dt.bfloat16
AF = mybir.ActivationFunctionType
ALU = mybir.AluOpType

D = 1024
DEPTH = 2
D_IN = 3072
D_FF = 4096
EPS = 1e-5
NEG = -30000.0
ENGS = ("pe", "act", "dve", "pool", "sp")


class Tok:
    __slots__ = ("sem", "val")

    def __init__(self, sem=None, val=None):
        self.sem, self.val = sem, val


class Res:
    __slots__ = ("name", "w", "w_eng", "rs")

    def __init__(self, name):
        self.name, self.w, self.w_eng, self.rs = name, None, None, {}


class Prog:
    def __init__(self, nc, stack):
        self.nc, self.stack = nc, stack
        self.q = {e: [] for e in ENGS}
        self.sems = {}
        self.waited = {e: {} for e in ENGS}
        self.pending = {e: [] for e in ENGS}
        self.epoch = {e: 0 for e in ENGS}
        self.store_sems = set()

    def sem(self, name):
        if name not in self.sems:
            h = self.stack.enter_context(self.nc.semaphore(name))
            self.sems[name] = [h, 0]
        return self.sems[name]

    def new_epoch(self):
        self.full_barrier()
        for e in ENGS:
            self.epoch[e] += 1

    def _wait(self, eng, tok):
        if tok is None:
            return
        assert tok.val is not None, "unresolved lazy token"
        w = self.waited[eng]
        if w.get(tok.sem, 0) >= tok.val:
            return
        w[tok.sem] = tok.val
        h, v = self.sems[tok.sem][0], tok.val
        self.q[eng].append(I("wait_ge", h, v))

    def _deps(self, eng, reads, writes):
        for r in reads:
            if r.w is not None and not (eng == "pe" and r.w_eng == "pe"):
                self._wait(eng, r.w)
        for r in writes:
            if r.w is not None and r.w_eng != eng:
                self._wait(eng, r.w)
            for e2, t in r.rs.items():
                if e2 != eng:
                    self._wait(eng, t)

    def op(self, eng, fn, reads=(), writes=(), inc=True):
        self._deps(eng, reads, writes)
        tok = Tok()
        if inc:
            name = f"{eng}_{self.epoch[eng]}"
            s = self.sem(name)
            s[1] += 1
            tok.sem, tok.val = name, s[1]
            h = s[0]
            self.q[eng].append(lambda e, h=h, fn=fn: fn(e).then_inc(h, 1))
            for t in self.pending[eng]:
                t.sem, t.val = tok.sem, tok.val
            self.pending[eng] = []
        else:
            self.pending[eng].append(tok)
            self.q[eng].append(lambda e, fn=fn: fn(e))
        for r in reads:
            r.rs[eng] = tok
        for r in writes:
            r.w, r.w_eng, r.rs = tok, eng, {}
        return tok

    def dma(self, qeng, fn, dsem, reads=(), writes=(), store=False):
        self._deps(qeng, reads, writes)
        s = self.sem(dsem)
        s[1] += 16
        tok = Tok(dsem, s[1])
        h = s[0]
        self.q[qeng].append(lambda e, h=h, fn=fn: fn(e).then_inc(h, 16))
        key = "dma:" + dsem
        for r in reads:
            r.rs[key] = tok
        for r in writes:
            r.w, r.w_eng, r.rs = tok, key, {}
        if store:
            self.store_sems.add(dsem)
        return tok

    def collective(self, in_ap, out_ap, groups):
        s = self.sem("cc")
        s[1] += 1
        h, v = s[0], s[1]

        def f(e, h=h, v=v):
            e.collective_compute("AllGather", mybir.AluOpType.bypass, replica_groups=groups,
                                 ins=[in_ap], outs=[out_ap]).then_inc(h)
            e.wait_ge(h, v)

        self.q["pool"].append(f)
        self.waited["pool"]["cc"] = v
        self.store_sems.add("cc")

    def dram_barrier(self, engines=("sp", "pool")):
        for name in sorted(self.store_sems):
            tok = Tok(name, self.sems[name][1])
            for e in engines:
                self._wait(e, tok)

    def full_barrier(self):
        for e in ENGS:
            assert not self.pending[e], f"unresolved lazy tokens on {e} at barrier"
        for name in sorted(self.sems):
            cnt = self.sems[name][1]
            if cnt == 0:
                continue
            tok = Tok(name, cnt)
            for e in ENGS:
                self._wait(e, tok)

    def play(self, block):
        q = self.q

        @block.tensor
        def _(e):
            for f in q["pe"]:
                f(e)

        @block.scalar
        def _(e):
            for f in q["act"]:
                f(e)

        @block.vector
        def _(e):
            for f in q["dve"]:
                f(e)

        @block.gpsimd
        def _(e):
            for f in q["pool"]:
                f(e)

        @block.sync
        def _(e):
            for f in q["sp"]:
                f(e)


def I(name, *a, **k):
    return lambda e: getattr(e, name)(*a, **k)


def lambda_init(l):
    return 0.8 - 0.6 * math.exp(-0.3 * l)


def build_nc(T):
    NB = T // 512
    NKT = T // 128
    HALF_KT = NKT // 2
    HALF_QB = NB // 2
    G = 4
    TQ_LAST = T // G
    PADC = 1024
    nc = bass.Bass("TRN2", target_bir_lowering=False)

    def din(name, shape, dt=F32):
        return nc.dram_tensor(name, list(shape), dt, kind="ExternalInput").ap()

    xT = din("xT", [D, T // 4])
    w_in = din("w_in", [DEPTH, D, D_IN])
    w_out = din("w_out", [DEPTH, D, D])
    w_ff1 = din("w_ff1", [DEPTH, D, D_FF])
    w_ff2 = din("w_ff2", [DEPTH, D_FF, D])
    gains = din("gains", [128, 60])
    lamv = din("lamv", [128, 8 * 64])
    perm = din("perm", [128, 128])
    cmask_in = din("cmask", [128, 20 * 512])
    cosL = din("cosL", [128, T // 4])
    sinL = din("sinL", [128, T // 4])
    NBT3 = (NB // G) * 20
    bt3_in = din("bt3", [128, NBT3])
    yT = nc.dram_tensor("yT", [D, TQ_LAST], F32, kind="ExternalOutput").ap()

    dbg = os.environ.get("KDBG", "")

    def scr(name, shape, dt):
        if dbg:
            return nc.dram_tensor(name, list(shape), dt, kind="ExternalOutput").ap()
        return nc.dram_tensor(name, list(shape), dt).ap()

    TQ = TQ_LAST
    NBQ = NB // G
    CH = 1024
    NCH = TQ // CH
    assert NCH >= 1 and TQ % CH == 0
    x1_loc = scr("x1_loc", [D, TQ], F32)
    xm_t = scr("xm_loc", [D, TQ], F32)
    mix_t = scr("mix_loc", [D, TQ], F32)
    q_d = scr("qd_loc", [512, TQ], BF16)
    q_s = scr("qs_loc", [512, TQ], BF16)
    ks_loc = scr("ks_loc", [512, TQ + 2 * PADC], BF16)
    vs_loc = scr("vs_loc", [TQ + 2 * PADC, 512], BF16)

    def cbuf(name, rows):
        return nc.dram_tensor(name, [rows, 1024], BF16).ap()

    kin_d = [cbuf(f"kin_d{j}", 512) for j in range(NCH)]
    kin_s = [cbuf(f"kin_s{j}", 512) for j in range(NCH)]
    vin_d = [cbuf(f"vin_d{j}", 512) for j in range(NCH)]
    vin_s = [cbuf(f"vin_s{j}", 512) for j in range(NCH)]
    kg_d = [cbuf(f"kg_d{j}", G * 512) for j in range(NCH)]
    kg_s = [cbuf(f"kg_s{j}", G * 512) for j in range(NCH)]
    vg_d = [cbuf(f"vg_d{j}", G * 512) for j in range(NCH)]
    vg_s = [cbuf(f"vg_s{j}", G * 512) for j in range(NCH)]
    GROUPS = [list(range(g0, g0 + G)) for g0 in range(0, 8, G)]

    def vview(ap):
        return ap.rearrange("r (two c) -> (r two) c", two=2)

    def fm(ap):
        return ap.rearrange("(c p) t -> p c t", p=128)

    state = {}

    with ExitStack() as gstack:
        P = Prog(nc, gstack)

        def _prologue(e):
            pid = nc.partition_id(engines=[mybir.EngineType.SP])
            state["q0"] = (pid % G) * TQ_LAST
            state["left"] = (pid + (G - 1)) % G
            state["right"] = (pid + 1) % G

        P.q["sp"].append(_prologue)

        uid = [0]

        def sb(stack, name, shape, dt):
            uid[0] += 1
            return stack.enter_context(nc.sbuf_tensor(f"{name}_u{uid[0]}", list(shape), dt))

        def DYN(build):
            return lambda e: build(e, state["q0"])

        ps_banks = [gstack.enter_context(nc.psum_tensor(f"ps{i}", [128, 512], F32)) for i in range(8)]
        ps_res = [Res(f"ps{i}") for i in range(8)]

        ones = sb(gstack, "ones", [128, 128], BF16)
        r_ones = Res("ones")
        perm_bf = sb(gstack, "perm_bf", [128, 128], BF16)
        r_perm = Res("perm")
        g_sb = sb(gstack, "g_sb", [128, 60], F32)
        r_g = Res("g")
        bt3 = sb(gstack, "bt3", [128, NBT3], F32)
        r_bt3 = Res("bt3")
        lam_sb = sb(gstack, "lam_sb", [128, 8 * 64], F32)
        r_lam = Res("lam")
        lam_tmp = sb(gstack, "lam_tmp", [128, 64], F32)
        lam_acc = sb(gstack, "lam_acc", [128, 8], F32)
        r_lamacc = Res("lamacc")

        P.op("pool", I("memset", ones[:], 1.0), writes=[r_ones])
        P.dma("pool", I("dma_start", out=perm_bf[:], in_=perm), "ld_c0", writes=[r_perm])
        P.dma("sp", I("dma_start", out=g_sb[:], in_=gains), "ld_c1", writes=[r_g])
        P.dma("sp", I("dma_start", out=lam_sb[:], in_=lamv), "ld_c2", writes=[r_lam])
        P.dma("sp", I("dma_start", out=bt3[:], in_=bt3_in), "ld_c3", writes=[r_bt3])
        r_lt = Res("lam_tmp")
        for l in range(DEPTH):
            for m in range(2):
                qa = lam_sb[:, ((2 * m) * 2 + l) * 64:((2 * m) * 2 + l + 1) * 64]
                ka = lam_sb[:, ((2 * m + 1) * 2 + l) * 64:((2 * m + 1) * 2 + l + 1) * 64]
                col = lam_acc[:, 2 * l + m:2 * l + m + 1]
                P.op("dve", I("tensor_tensor", out=lam_tmp[:], in0=qa, in1=ka, op=ALU.mult), reads=[r_lam], writes=[r_lt])
                P.op("dve", I("tensor_reduce", out=col, in_=lam_tmp[:], op=ALU.add, axis=mybir.AxisListType.X),
                     reads=[r_lt], writes=[r_lamacc])
            c0 = lam_acc[:, 2 * l:2 * l + 2]
            P.op("act", I("activation", out=c0, in_=c0, func=AF.Exp), reads=[r_lamacc], writes=[r_lamacc])
            nl = lam_acc[:, 4 + l:5 + l]
            P.op("dve", I("scalar_tensor_tensor", out=nl, in0=lam_acc[:, 2 * l + 1:2 * l + 2], scalar=-lambda_init(l),
                          in1=lam_acc[:, 2 * l:2 * l + 1], op0=ALU.add, op1=ALU.subtract),
                 reads=[r_lamacc], writes=[r_lamacc])

        def gcol(i):
            return g_sb[:, i:i + 1]

        def rms_rstd(bufs, chunks, denom, scale_extra, ps_i):
            srt, rstd, r_srt, r_rstd, r_sq = bufs
            n = len(chunks)
            for i, ch in enumerate(chunks):
                P.op("pe", I("matmul", ps_banks[ps_i][:], lhsT=ones[:], rhs=ch, start=(i == 0), stop=(i == n - 1)),
                     reads=[r_ones, r_sq], writes=[ps_res[ps_i]], inc=(i == n - 1))
            s2 = 1.0 / (scale_extra * scale_extra)
            P.op("act", I("activation", out=srt[:], in_=ps_banks[ps_i][:], func=AF.Sqrt, scale=s2 / denom, bias=EPS * s2),
                 reads=[ps_res[ps_i]], writes=[r_srt])
            P.op("dve", I("reciprocal", out=rstd[:], in_=srt[:]), reads=[r_srt], writes=[r_rstd])

        def xcols(ap3, b, dyn):
            if not dyn:
                return lambda q0: ap3[:, :, b * 512:(b + 1) * 512]
            return lambda q0: ap3[:, :, bass.ds(q0 + b * 512, 512)]

        def cols2(ap2, b, dyn):
            if not dyn:
                return lambda q0: ap2[:, b * 512:(b + 1) * 512]
            return lambda q0: ap2[:, bass.ds(q0 + b * 512, 512)]

        for l in range(DEPTH):
            if dbg and l == 1 and dbg != "2":
                break
            last_layer = (l == DEPTH - 1)
            x_src = xT if l == 0 else x1_loc
            bt3_base = 0

            P.new_epoch()
            with ExitStack() as st:
                w = sb(st, "w_in_sb", [128, 8, D_IN], BF16)
                r_w = Res("w_in")
                for kc in range(8):
                    for hh in range(2):
                        P.dma("pool", I("dma_start", out=w[:, kc, hh * 1536:(hh + 1) * 1536],
                                        in_=w_in[l, kc * 128:(kc + 1) * 128, hh * 1536:(hh + 1) * 1536]),
                              f"ld_w{l}", writes=[r_w])
                xb = [sb(st, f"xb{i}", [128, 8, 512], F32) for i in range(2)]
                r_xb = [Res(f"xb{i}") for i in range(2)]
                cs = [sb(st, f"cs{i}", [128, 2, 512], F32) for i in range(2)]
                r_cs = [Res(f"cs{i}") for i in range(2)]
                sq = sb(st, "sq", [128, 8, 512], BF16)
                r_sq = Res("sq")
                hT = sb(st, "hT", [128, 8, 512], BF16)
                r_h = Res("hT")
                srt = sb(st, "srt", [128, 512], F32)
                rstd = sb(st, "rstd", [128, 512], F32)
                r_srt, r_rstd = Res("srt"), Res("rstd")
                qb = [sb(st, f"qb{i}", [128, 512], BF16) for i in range(2)]
                r_qb = [Res(f"qb{i}") for i in range(2)]
                t1 = [sb(st, f"t1_{i}", [128, 512], F32) for i in range(2)]
                r_t1 = [Res(f"t1_{i}") for i in range(2)]
                t2 = [sb(st, f"t2_{i}", [128, 512], F32) for i in range(2)]
                r_t2 = [Res(f"t2_{i}") for i in range(2)]
                qo = [sb(st, f"qo{i}", [128, 512], BF16) for i in range(3)]
                r_qo = [Res(f"qo{i}") for i in range(3)]
                vo = [sb(st, f"vo{i}", [128, 512], BF16) for i in range(2)]
                r_vo = [Res(f"vo{i}") for i in range(2)]
                cnts = {"ld": 0, "qk": 0, "v": 0}

                def load_blk(b, dyn):
                    i = cnts["ld"] % 2
                    cnts["ld"] += 1
                    xs_, cc_, ss_ = (x_src, cosL, sinL)
                    P.dma("sp", I("dma_start", out=xb[i][:], in_=fm(xs_)[:, :, b * 512:(b + 1) * 512]),
                          f"ld_xb{i}", writes=[r_xb[i]])
                    P.dma("sp", I("dma_start", out=cs[i][:, 0, :], in_=cc_[:, b * 512:(b + 1) * 512]),
                          f"ld_cs{i}", writes=[r_cs[i]])
                    P.dma("sp", I("dma_start", out=cs[i][:, 1, :], in_=ss_[:, b * 512:(b + 1) * 512]),
                          f"ld_cs{i}", writes=[r_cs[i]])
                    return i

                def p1_block(i, b, qk_specs, do_v):
                    P.op("act", I("activation", out=sq[:], in_=xb[i][:], func=AF.Square), reads=[r_xb[i]], writes=[r_sq])
                    rms_rstd((srt, rstd, r_srt, r_rstd, r_sq), [sq[:, c, :] for c in range(8)], float(D), 1.0, 0)
                    for c in range(8):
                        P.op("dve", I("scalar_tensor_tensor", out=hT[:, c, :], in0=xb[i][:, c, :], scalar=gcol(l * 8 + c),
                                      in1=rstd[:], op0=ALU.mult, op1=ALU.mult),
                             reads=[r_xb[i], r_g, r_rstd], writes=[r_h])
                    for col0, dst_fn in qk_specs:
                        for j in range(4):
                            cnt = cnts["qk"]
                            cnts["qk"] += 1
                            pj, pr, bi, oi = 1 + cnt % 2, 3 + cnt % 2, cnt % 2, cnt % 3
                            cbase = col0 + j * 128
                            for kc in range(8):
                                P.op("pe", I("matmul", ps_banks[pj][:], lhsT=w[:, kc, cbase:cbase + 128], rhs=hT[:, kc, :],
                                             start=(kc == 0), stop=(kc == 7)),
                                     reads=[r_w, r_h], writes=[ps_res[pj]], inc=(kc == 7))
                            P.op("act", I("copy", out=qb[bi][:], in_=ps_banks[pj][:]), reads=[ps_res[pj]], writes=[r_qb[bi]])
                            P.op("pe", I("matmul", ps_banks[pr][:], lhsT=perm_bf[:], rhs=qb[bi][:], start=True, stop=True),
                                 reads=[r_perm, r_qb[bi]], writes=[ps_res[pr]])
                            P.op("dve", I("tensor_tensor", out=t1[bi][:], in0=qb[bi][:], in1=cs[i][:, 0, :], op=ALU.mult),
                                 reads=[r_qb[bi], r_cs[i]], writes=[r_t1[bi]])
                            P.op("dve", I("tensor_tensor", out=t2[bi][:], in0=ps_banks[pr][:], in1=cs[i][:, 1, :], op=ALU.mult),
                                 reads=[ps_res[pr], r_cs[i]], writes=[r_t2[bi]])
                            P.op("dve", I("tensor_tensor", out=qo[oi][:], in0=t1[bi][:], in1=t2[bi][:], op=ALU.add),
                                 reads=[r_t1[bi], r_t2[bi]], writes=[r_qo[oi]])
                            P.dma("sp", I("dma_start", out=dst_fn(j, b), in_=qo[oi][:]),
                                  f"st_qk{oi}", reads=[r_qo[oi]], store=True)
                    if do_v:
                        for col0, vins in ((1024, vin_d), (2560, vin_s)):
                            for sub in range(4):
                                pv, vi = 5 + cnts["v"] % 2, cnts["v"] % 2
                                cnts["v"] += 1
                                for kc in range(8):
                                    P.op("pe", I("matmul", ps_banks[pv][:], lhsT=hT[:, kc, sub * 128:(sub + 1) * 128],
                                                 rhs=w[:, kc, col0:col0 + 512], start=(kc == 0), stop=(kc == 7)),
                                         reads=[r_w, r_h], writes=[ps_res[pv]], inc=(kc == 7))
                                P.op("act", I("copy", out=vo[vi][:], in_=ps_banks[pv][:]), reads=[ps_res[pv]], writes=[r_vo[vi]])
                                r0 = (b * 512) % CH + sub * 128
                                P.dma("sp", I("dma_start", out=vview(vins[(b * 512) // CH])[r0:r0 + 128, :], in_=vo[vi][:]),
                                      f"st_v{vi}", reads=[r_vo[vi]], store=True)

                def q_dst(t):
                    return lambda j, b: t[j * 128:(j + 1) * 128, b * 512:(b + 1) * 512]

                def k_dst(chunks):
                    return lambda j, b: chunks[(b * 512) // CH][j * 128:(j + 1) * 128, (b * 512) % CH:(b * 512) % CH + 512]

                specs = [(0, q_dst(q_d)), (1536, q_dst(q_s)), (512, k_dst(kin_d)), (2048, k_dst(kin_s))]
                nxt = load_blk(0, True)
                for b in range(NBQ):
                    i = nxt
                    if b + 1 < NBQ:
                        nxt = load_blk(b + 1, True)
                    p1_block(i, b, specs, True)
                    if ((b + 1) * 512) % CH == 0:
                        j = ((b + 1) * 512) // CH - 1
                        P.dram_barrier(engines=("pool",))
                        for ins_, outs_ in ((kin_d, kg_d), (vin_d, vg_d), (kin_s, kg_s), (vin_s, vg_s)):
                            P.collective(ins_[j], outs_[j], GROUPS)
            P.dram_barrier()
            P.dram_barrier()

            P.new_epoch()
            with ExitStack() as st:
                kTb = [sb(st, f"kT{i}", [128, T], BF16) for i in range(2)]
                vvb = [sb(st, f"vv{i}", [128, NKT, 128], BF16) for i in range(2)]
                r_kTb = [Res(f"kT{i}") for i in range(2)]
                r_vvb = [Res(f"vv{i}") for i in range(2)]

                def load_kv(h):
                    i = h % 2
                    for r in range(G):
                        for j in range(NCH):
                            t0 = r * TQ + j * CH
                            P.dma("sp", I("dma_start", out=kTb[i][:, t0:t0 + CH],
                                          in_=kg_d[j][r * 512 + h * 128:r * 512 + (h + 1) * 128, :]), f"ld_kT{i}", writes=[r_kTb[i]])
                            P.dma("sp", I("dma_start", out=vvb[i][:, t0 // 128:(t0 + CH) // 128, :],
                                          in_=vview(vg_d[j])[r * CH:(r + 1) * CH, h * 128:(h + 1) * 128].rearrange(
                                              "(k p) e -> p k e", p=128)), f"ld_vv{i}", writes=[r_vvb[i]])

                def mask_kv(h):
                    i = h % 2
                    for r in range(G):
                        P.op("pool", I("tensor_scalar", out=kTb[i][:, r * TQ:(r + 1) * TQ], in0=kTb[i][:, r * TQ:(r + 1) * TQ],
                                       scalar1=gcol(54 + r), scalar2=1.0, op0=ALU.mult, op1=ALU.mult),
                             reads=[r_kTb[i], r_g], writes=[r_kTb[i]])
                        P.op("pool", I("tensor_scalar", out=vvb[i][:, r * (TQ // 128):(r + 1) * (TQ // 128), :],
                                       in0=vvb[i][:, r * (TQ // 128):(r + 1) * (TQ // 128), :],
                                       scalar1=gcol(54 + r), scalar2=1.0, op0=ALU.mult, op1=ALU.mult),
                             reads=[r_vvb[i], r_g], writes=[r_vvb[i]])

                qA = [sb(st, f"qA{i}", [128, 512], BF16) for i in range(2)]
                qB = [sb(st, f"qB{i}", [128, 512], BF16) for i in range(2)]
                r_qA = [Res(f"qA{i}") for i in range(2)]
                r_qB = [Res(f"qB{i}") for i in range(2)]
                NPB = 8
                pt = [sb(st, f"pt{i}", [128, 512], BF16) for i in range(2 * NPB)]
                r_pt = [Res(f"pt{i}") for i in range(2 * NPB)]
                sab = [sb(st, f"sab{i}", [128, 512], BF16) for i in range(4)]
                scd = [sb(st, f"scd{i}", [128, 512], BF16) for i in range(4)]
                s4 = [sb(st, f"s4_{i}", [128, 512], BF16) for i in range(4)]
                r_sab = [Res(f"sab{i}") for i in range(4)]
                r_scd = [Res(f"scd{i}") for i in range(4)]
                r_s4 = [Res(f"s4_{i}") for i in range(4)]
                slot_of = [0, 0, 0, 0]
                q4cnt = 0
                oc = [sb(st, f"oc{i}", [128, 512], F32) for i in range(4)]
                r_oc = [Res(f"oc{i}") for i in range(4)]
                ob = [sb(st, f"ob{i}", [128, 512], F32) for i in range(2)]
                r_ob = [Res(f"ob{i}") for i in range(2)]
                for i in range(2):
                    P.op("pool", I("memset", qA[i][64:128, :], 0.0), writes=[r_qA[i]])
                    P.op("pool", I("memset", qB[i][0:64, :], 0.0), writes=[r_qB[i]])
                qcnt = 0
                ucnt = 0
                load_kv(0)
                mask_kv(0)
                for h in range(4):
                    kT, vv, r_kT, r_vv = kTb[h % 2], vvb[h % 2], r_kTb[h % 2], r_vvb[h % 2]

                    def load_q(qbi, h=h):
                        i = qbi % 2
                        P.dma("sp", I("dma_start", out=qA[i][0:64, :], in_=q_d[h * 128:h * 128 + 64, qbi * 512:(qbi + 1) * 512]),
                              f"ld_qA{i}", writes=[r_qA[i]])
                        P.dma("sp", I("dma_start", out=qB[i][64:128, :],
                                      in_=q_d[h * 128 + 64:h * 128 + 128, qbi * 512:(qbi + 1) * 512]),
                              f"ld_qB{i}", writes=[r_qB[i]])

                    load_q(0)
                    for qbi in range(NBQ):
                        qi = qbi % 2
                        if qbi + 1 < NBQ:
                            load_q(qbi + 1)
                        if qbi == 0 and h + 1 < 4:
                            load_kv(h + 1)
                        if qbi == NBQ - 1 and h + 1 < 4:
                            mask_kv(h + 1)

                        def qk(kt, qi=qi, kT=kT, r_kT=r_kT):
                            sb_ = (kt % 2) * 2
                            P.op("pe", I("matmul", ps_banks[sb_][:], lhsT=kT[:, kt * 128:(kt + 1) * 128], rhs=qA[qi][:],
                                         start=True, stop=True), reads=[r_kT, r_qA[qi]], writes=[ps_res[sb_]])
                            P.op("pe", I("matmul", ps_banks[sb_ + 1][:], lhsT=kT[:, kt * 128:(kt + 1) * 128], rhs=qB[qi][:],
                                         start=True, stop=True), reads=[r_kT, r_qB[qi]], writes=[ps_res[sb_ + 1]])

                        qk(0)
                        deferred = []
                        for kt in range(NKT):
                            if kt + 1 < NKT:
                                qk(kt + 1)
                            sb_ = (kt % 2) * 2
                            bias_ap = None
                            pi = (ucnt % NPB) * 2
                            ucnt += 1
                            slot_of[kt % 4] = pi
                            for m in range(2):
                                if bias_ap is not None:
                                    P.op("act", I("activation", out=pt[pi + m][:], in_=ps_banks[sb_ + m][:], func=AF.Exp,
                                                  bias=bias_ap, scale=0.125),
                                         reads=[ps_res[sb_ + m], r_g], writes=[r_pt[pi + m]])
                                else:
                                    P.op("act", I("activation", out=pt[pi + m][:], in_=ps_banks[sb_ + m][:], func=AF.Exp,
                                                  scale=0.125), reads=[ps_res[sb_ + m]], writes=[r_pt[pi + m]])
                            last = (kt == NKT - 1)
                            for m in range(2):
                                P.op("pe", I("matmul", ps_banks[4 + m][:], lhsT=vv[:, kt, :], rhs=pt[pi + m][:],
                                             start=(kt == 0), stop=last),
                                     reads=[r_vv, r_pt[pi + m]], writes=[ps_res[4 + m]], inc=last)
                            r4 = kt % 4
                            if r4 == 1:
                                for m in range(2):
                                    bi = 2 * (q4cnt % 2) + m
                                    P.op("dve", I("tensor_tensor", out=sab[bi][:], in0=pt[slot_of[0] + m][:],
                                                  in1=pt[slot_of[1] + m][:], op=ALU.add),
                                         reads=[r_pt[slot_of[0] + m], r_pt[slot_of[1] + m]], writes=[r_sab[bi]])
                            if r4 == 3:
                                for m in range(2):
                                    bi = 2 * (q4cnt % 2) + m
                                    P.op("pool", I("tensor_tensor", out=scd[bi][:], in0=pt[slot_of[2] + m][:],
                                                   in1=pt[slot_of[3] + m][:], op=ALU.add),
                                         reads=[r_pt[slot_of[2] + m], r_pt[slot_of[3] + m]], writes=[r_scd[bi]])
                                    P.op("dve", I("tensor_tensor", out=s4[bi][:], in0=sab[bi][:], in1=scd[bi][:], op=ALU.add),
                                         reads=[r_sab[bi], r_scd[bi]], writes=[r_s4[bi]])
                                    deferred.append((kt + 2, m, bi, kt == 3, last))
                                q4cnt += 1
                            while deferred and (deferred[0][0] <= kt or last):
                                _, m, bi, first_, last_ = deferred.pop(0)
                                P.op("pe", I("matmul", ps_banks[6 + m][:], lhsT=ones[:], rhs=s4[bi][:], start=first_, stop=last_),
                                     reads=[r_ones, r_s4[bi]], writes=[ps_res[6 + m]], inc=last_)
                        for a in range(2):
                            P.op("dve", I("tensor_copy", out=oc[a][:], in_=ps_banks[4 + a][:]),
                                 reads=[ps_res[4 + a]], writes=[r_oc[a]])
                        for a in range(2, 4):
                            P.op("dve", I("tensor_scalar", out=oc[a][:], in0=ps_banks[4 + a][:], scalar1=gcol(58), scalar2=None,
                                          op0=ALU.add), reads=[ps_res[4 + a], r_g], writes=[r_oc[a]])
                        for m in range(2):
                            P.op("dve", I("reciprocal", out=oc[2 + m][:], in_=oc[2 + m][:]),
                                 reads=[r_oc[2 + m]], writes=[r_oc[2 + m]])
                            P.op("dve", I("tensor_tensor", out=oc[m][:], in0=oc[m][:], in1=oc[2 + m][:], op=ALU.mult),
                                 reads=[r_oc[m], r_oc[2 + m]], writes=[r_oc[m]])
                        oi = qcnt % 2
                        qcnt += 1
                        P.op("dve", I("scalar_tensor_tensor", out=ob[oi][:], in0=oc[1][:], scalar=lam_acc[:, 4 + l:5 + l],
                                      in1=oc[0][:], op0=ALU.mult, op1=ALU.add),
                             reads=[r_oc[0], r_oc[1], r_lamacc], writes=[r_ob[oi]])
                        P.dma("sp", I("dma_start", out=mix_t[h * 128:(h + 1) * 128, qbi * 512:(qbi + 1) * 512], in_=ob[oi][:]),
                              f"st_mx{oi}", reads=[r_ob[oi]], store=True)

            P.new_epoch()
            with ExitStack() as st:
                WT = TQ + 2 * PADC
                WKT = WT // 128
                for j in range(NCH):
                    P.dma("sp", I("dma_start", out=ks_loc[:, PADC + j * CH:PADC + (j + 1) * CH], in_=kin_s[j]), "st_kloc", store=True)
                    P.dma("sp", I("dma_start", out=vs_loc[PADC + j * CH:PADC + (j + 1) * CH, :], in_=vview(vin_s[j])), "st_kloc", store=True)
                P.dma("sp", DYN(lambda e, q0: e.dma_start(out=ks_loc[:, 0:PADC], in_=kg_s[NCH - 1][bass.ds(state["left"] * 512, 512), :])),
                      "st_kloc", store=True)
                P.dma("sp", DYN(lambda e, q0: e.dma_start(out=ks_loc[:, PADC + TQ:], in_=kg_s[0][bass.ds(state["right"] * 512, 512), :])),
                      "st_kloc", store=True)
                P.dma("sp", DYN(lambda e, q0: e.dma_start(out=vs_loc[0:PADC, :], in_=vview(vg_s[NCH - 1])[bass.ds(state["left"] * CH, CH), :])),
                      "st_kloc", store=True)
                P.dma("sp", DYN(lambda e, q0: e.dma_start(out=vs_loc[PADC + TQ:, :], in_=vview(vg_s[0])[bass.ds(state["right"] * CH, CH), :])),
                      "st_kloc", store=True)
                P.dram_barrier()
                kT = sb(st, "kTs", [128, WT], BF16)
                vv = sb(st, "vvs", [128, WKT, 128], BF16)
                r_kT, r_vv = Res("kTs"), Res("vvs")
                cm = sb(st, "cm", [128, 20, 512], BF16)
                r_cm = Res("cm")
                for j in range(20):
                    P.dma("pool", I("dma_start", out=cm[:, j, :], in_=cmask_in[:, j * 512:(j + 1) * 512]), "ld_cm", writes=[r_cm])
                qA = [sb(st, f"sqA{i}", [128, 512], BF16) for i in range(2)]
                qB = [sb(st, f"sqB{i}", [128, 512], BF16) for i in range(2)]
                r_qA = [Res(f"sqA{i}") for i in range(2)]
                r_qB = [Res(f"sqB{i}") for i in range(2)]
                NPB = 6
                LOOK = 3
                pt = [sb(st, f"spt{i}", [128, 512], BF16) for i in range(NPB)]
                r_pt = [Res(f"spt{i}") for i in range(NPB)]
                pm = [sb(st, f"spm{i}", [128, 512], BF16) for i in range(NPB)]
                r_pm = [Res(f"spm{i}") for i in range(NPB)]
                oc = [sb(st, f"soc{i}", [64, 512], F32) for i in range(2)]
                r_oc = [Res(f"soc{i}") for i in range(2)]
                ob = [sb(st, f"sob{i}", [64, 512], F32) for i in range(2)]
                r_ob = [Res(f"sob{i}") for i in range(2)]
                for i in range(2):
                    P.op("pool", I("memset", qA[i][64:128, :], 0.0), writes=[r_qA[i]])
                    P.op("pool", I("memset", qB[i][0:64, :], 0.0), writes=[r_qB[i]])
                ucnt = 0
                qcnt = 0
                scnt = [0]
                for cp in range(4):
                    nparts = 4
                    for part in range(nparts):
                        c0, c1 = part * (WT // nparts), (part + 1) * (WT // nparts)
                        k0, k1 = part * (WKT // nparts), (part + 1) * (WKT // nparts)
                        ksrc, vsrc = ks_loc, vs_loc
                        P.dma("sp" if part % 2 == 0 else "pool",
                              I("dma_start", out=kT[:, c0:c1], in_=ksrc[cp * 128:(cp + 1) * 128, c0:c1]),
                              "ld_kTs", writes=[r_kT])
                        P.dma("pool" if part % 2 == 0 else "sp",
                              I("dma_start", out=vv[:, k0:k1, :],
                                in_=vsrc[k0 * 128:k1 * 128, cp * 128:(cp + 1) * 128].rearrange("(k p) e -> p k e", p=128)),
                              "ld_vvs", writes=[r_vv])

                    def load_q(qbi, cp=cp):
                        i = qbi % 2
                        P.dma("sp", I("dma_start", out=qA[i][0:64, :], in_=q_s[cp * 128:cp * 128 + 64, qbi * 512:(qbi + 1) * 512]),
                              f"ld_sqA{i}", writes=[r_qA[i]])
                        P.dma("sp", I("dma_start", out=qB[i][64:128, :],
                                      in_=q_s[cp * 128 + 64:cp * 128 + 128, qbi * 512:(qbi + 1) * 512]),
                              f"ld_sqB{i}", writes=[r_qB[i]])

                    load_q(0)
                    for qbi in range(NBQ):
                        qi = qbi % 2
                        if qbi + 1 < NBQ:
                            load_q(qbi + 1)
                        for hh in range(2):
                            qop, r_qop = (qA[qi], r_qA[qi]) if hh == 0 else (qB[qi], r_qB[qi])
                            po, pl = 4 + hh, 6 + hh
                            tiles = list(range(20))
                            sbank = {}

                            def crange(j):
                                return max(0, 128 * j - 2048), min(512, 128 * j + 128)

                            def qk(idx, tiles=tiles, qop=qop, r_qop=r_qop, qbi=qbi):
                                wt = qbi * 4 + tiles[idx]
                                c0, c1 = crange(tiles[idx])
                                sbk = scnt[0] % 4
                                scnt[0] += 1
                                P.op("pe", I("matmul", ps_banks[sbk][:, c0:c1], lhsT=kT[:, wt * 128:(wt + 1) * 128], rhs=qop[:, c0:c1],
                                             start=True, stop=True), reads=[r_kT, r_qop], writes=[ps_res[sbk]])
                                sbank[idx] = sbk

                            for a in range(min(LOOK, len(tiles))):
                                qk(a)
                            for idx, j in enumerate(tiles):
                                if idx + LOOK < len(tiles):
                                    qk(idx + LOOK)
                                sbk = sbank[idx]
                                wt = qbi * 4 + j
                                bcol = bt3[:, bt3_base + qbi * 20 + j:bt3_base + qbi * 20 + j + 1]
                                pi = ucnt % NPB
                                ucnt += 1
                                c0, c1 = crange(j)
                                P.op("act", I("activation", out=pt[pi][:, c0:c1], in_=ps_banks[sbk][:, c0:c1], func=AF.Exp, bias=bcol,
                                              scale=0.125), reads=[ps_res[sbk], r_bt3], writes=[r_pt[pi]])
                                P.op("dve", I("tensor_tensor", out=pm[pi][:, c0:c1], in0=pt[pi][:, c0:c1], in1=cm[:, j, c0:c1], op=ALU.mult),
                                     reads=[r_pt[pi], r_cm], writes=[r_pm[pi]])
                                first, last = idx == 0, idx == len(tiles) - 1
                                P.op("pe", I("matmul", ps_banks[po][0:64, c0:c1], lhsT=vv[:, wt, hh * 64:(hh + 1) * 64], rhs=pm[pi][:, c0:c1],
                                             start=first, stop=last, skip_group_check=True),
                                     reads=[r_vv, r_pm[pi]], writes=[ps_res[po]], inc=last)
                                P.op("pe", I("matmul", ps_banks[pl][0:64, c0:c1], lhsT=ones[:, 0:64], rhs=pm[pi][:, c0:c1],
                                             start=first, stop=last, skip_group_check=True),
                                     reads=[r_ones, r_pm[pi]], writes=[ps_res[pl]], inc=last)
                            ci = qcnt % 2
                            qcnt += 1
                            P.op("dve", I("reciprocal", out=oc[ci][:], in_=ps_banks[pl][0:64, :]), reads=[ps_res[pl]], writes=[r_oc[ci]])
                            P.op("dve", I("tensor_tensor", out=ob[ci][:], in0=ps_banks[po][0:64, :], in1=oc[ci][:], op=ALU.mult),
                                 reads=[ps_res[po], r_oc[ci]], writes=[r_ob[ci]])
                            row0 = 512 + (cp * 2 + hh) * 64
                            P.dma("sp", I("dma_start", out=mix_t[row0:row0 + 64, qbi * 512:(qbi + 1) * 512], in_=ob[ci][:]),
                                  f"st_ms{ci}", reads=[r_ob[ci]], store=True)
            P.dram_barrier()

            P.new_epoch()
            wst = ExitStack()
            w1 = sb(wst, "w1", [128, 8, D_FF], BF16)
            r_w1 = Res("w1")
            with ExitStack() as st:
                wo = sb(st, "wo", [128, 8, D], BF16)
                r_wo = Res("wo")
                for kc in range(8):
                    P.dma("pool", I("dma_start", out=wo[:, kc, :], in_=w_out[l, kc * 128:(kc + 1) * 128, :]), f"ld_wo{l}", writes=[r_wo])
                for kc in range(8):
                    for hh in range(2):
                        P.dma("pool", I("dma_start", out=w1[:, kc, hh * 2048:(hh + 1) * 2048],
                                        in_=w_ff1[l, kc * 128:(kc + 1) * 128, hh * 2048:(hh + 1) * 2048]), f"ld_w1{l}", writes=[r_w1])
                xb = [sb(st, f"axb{i}", [128, 8, 512], F32) for i in range(2)]
                r_xb = [Res(f"axb{i}") for i in range(2)]
                mr = [sb(st, f"mr{i}", [128, 8, 512], F32) for i in range(2)]
                r_mr = [Res(f"mr{i}") for i in range(2)]
                sq = sb(st, "asq", [128, 8, 512], BF16)
                r_sq = Res("asq")
                mx = sb(st, "amx", [128, 8, 512], BF16)
                r_mx = Res("amx")
                srt = [sb(st, f"asrt{i}", [128, 512], F32) for i in range(2)]
                rstd = [sb(st, f"arstd{i}", [128, 512], F32) for i in range(2)]
                r_srt = [Res(f"asrt{i}") for i in range(2)]
                r_rstd = [Res(f"arstd{i}") for i in range(2)]

                def load_blk(b):
                    i = b % 2
                    xs_ = x_src
                    P.dma("sp", I("dma_start", out=xb[i][:], in_=fm(xs_)[:, :, b * 512:(b + 1) * 512]), f"ld_axb{i}", writes=[r_xb[i]])
                    P.dma("pool", I("dma_start", out=mr[i][:], in_=fm(mix_t)[:, :, b * 512:(b + 1) * 512]), f"ld_mr{i}", writes=[r_mr[i]])

                load_blk(0)
                ncnt = 0
                pcnt = 0
                li = lambda_init(l)
                for b in range(NBQ):
                    i = b % 2
                    if b + 1 < NBQ:
                        load_blk(b + 1)
                    P.op("act", I("activation", out=sq[:], in_=mr[i][:], func=AF.Square), reads=[r_mr[i]], writes=[r_sq])
                    for c in range(4):
                        ni = ncnt % 2
                        ncnt += 1
                        rms_rstd((srt[ni], rstd[ni], r_srt[ni], r_rstd[ni], r_sq), [sq[:, c, :]], 128.0, 1.0 - li, ni)
                        P.op("dve", I("scalar_tensor_tensor", out=mx[:, c, :], in0=mr[i][:, c, :], scalar=gcol(40 + l),
                                      in1=rstd[ni][:], op0=ALU.mult, op1=ALU.mult),
                             reads=[r_mr[i], r_g, r_rstd[ni]], writes=[r_mx])
                    ni = ncnt % 2
                    ncnt += 1
                    rms_rstd((srt[ni], rstd[ni], r_srt[ni], r_rstd[ni], r_sq), [sq[:, c, :] for c in range(4, 8)], 512.0, 1.0, ni)
                    for c in range(4, 8):
                        P.op("dve", I("scalar_tensor_tensor", out=mx[:, c, :], in0=mr[i][:, c, :], scalar=gcol(42 + l * 4 + (c - 4)),
                                      in1=rstd[ni][:], op0=ALU.mult, op1=ALU.mult),
                             reads=[r_mr[i], r_g, r_rstd[ni]], writes=[r_mx])
                    for oc_ in range(8):
                        pb = 2 + pcnt % 4
                        pcnt += 1
                        for kc in range(8):
                            P.op("pe", I("matmul", ps_banks[pb][:], lhsT=wo[:, kc, oc_ * 128:(oc_ + 1) * 128], rhs=mx[:, kc, :],
                                         start=(kc == 0), stop=(kc == 7)), reads=[r_wo, r_mx], writes=[ps_res[pb]], inc=(kc == 7))
                        P.op("dve", I("tensor_tensor", out=xb[i][:, oc_, :], in0=xb[i][:, oc_, :], in1=ps_banks[pb][:], op=ALU.add),
                             reads=[r_xb[i], ps_res[pb]], writes=[r_xb[i]])
                    P.dma("sp", I("dma_start", out=fm(xm_t)[:, :, b * 512:(b + 1) * 512], in_=xb[i][:]),
                          f"st_xm{i}", reads=[r_xb[i]], store=True)
            P.dram_barrier()

            P.new_epoch()
            with ExitStack() as st:
                w2 = sb(st, "w2", [128, 32, D], BF16)
                r_w2 = Res("w2")
                for fc in range(32):
                    P.dma("pool", I("dma_start", out=w2[:, fc, :], in_=w_ff2[l, fc * 128:(fc + 1) * 128, :]), f"ld_w2{l}", writes=[r_w2])
                xbb = [sb(st, f"bxb{i}", [128, 8, 512], F32) for i in range(2)]
                r_xbb = [Res(f"bxb{i}") for i in range(2)]
                h2 = sb(st, "bh2", [128, 8, 512], BF16)
                r_h2 = Res("bh2")
                sq, r_sq = h2, r_h2
                uu = sb(st, "buu", [128, 16, 512], BF16)
                r_uu = Res("buu")
                rr = [sb(st, f"brr{i}", [128, 512], F32) for i in range(2)]
                r_rr = [Res(f"brr{i}") for i in range(2)]
                srt = sb(st, "bsrt", [128, 512], F32)
                rstd = sb(st, "brstd", [128, 512], F32)
                r_srt, r_rstd = Res("bsrt"), Res("brstd")
                pcnt = 0
                rcnt = 0
                def load_bx(b):
                    P.dma("sp", I("dma_start", out=xbb[b % 2][:], in_=fm(xm_t)[:, :, b * 512:(b + 1) * 512]),
                          f"ld_bxb{b % 2}", writes=[r_xbb[b % 2]])

                load_bx(0)
                for b in range(NBQ):
                    xb, r_xb = xbb[b % 2], r_xbb[b % 2]
                    if b + 1 < NBQ:
                        load_bx(b + 1)
                    P.op("act", I("activation", out=sq[:], in_=xb[:], func=AF.Square), reads=[r_xb], writes=[r_sq])
                    rms_rstd((srt, rstd, r_srt, r_rstd, r_sq), [sq[:, c, :] for c in range(8)], float(D), 1.0, 0)
                    for c in range(8):
                        P.op("dve", I("scalar_tensor_tensor", out=h2[:, c, :], in0=xb[:, c, :], scalar=gcol(16 + l * 8 + c),
                                      in1=rstd[:], op0=ALU.mult, op1=ALU.mult), reads=[r_xb, r_g, r_rstd], writes=[r_h2])
                    for half in range(2):
                        for f in range(16):
                            fc = half * 16 + f
                            pb = 1 + pcnt % 3
                            pcnt += 1
                            ri = rcnt % 2
                            rcnt += 1
                            for kc in range(8):
                                P.op("pe", I("matmul", ps_banks[pb][:], lhsT=w1[:, kc, fc * 128:(fc + 1) * 128], rhs=h2[:, kc, :],
                                             start=(kc == 0), stop=(kc == 7)), reads=[r_w1, r_h2], writes=[ps_res[pb]], inc=(kc == 7))
                            P.op("act", I("activation", out=rr[ri][:], in_=ps_banks[pb][:], func=AF.Relu),
                                 reads=[ps_res[pb]], writes=[r_rr[ri]])
                            P.op("dve", I("tensor_tensor", out=uu[:, f, :], in0=rr[ri][:], in1=rr[ri][:], op=ALU.mult),
                                 reads=[r_rr[ri]], writes=[r_uu])
                        for oc_ in range(8):
                            pb = 4 + pcnt % 4
                            pcnt += 1
                            for f in range(16):
                                fc = half * 16 + f
                                P.op("pe", I("matmul", ps_banks[pb][:], lhsT=w2[:, fc, oc_ * 128:(oc_ + 1) * 128], rhs=uu[:, f, :],
                                             start=(f == 0), stop=(f == 15)), reads=[r_w2, r_uu], writes=[ps_res[pb]], inc=(f == 15))
                            P.op("dve", I("tensor_tensor", out=xb[:, oc_, :], in0=xb[:, oc_, :], in1=ps_banks[pb][:], op=ALU.add),
                                 reads=[r_xb, ps_res[pb]], writes=[r_xb])
                    if not last_layer:
                        P.dma("sp", I("dma_start", out=fm(x1_loc)[:, :, b * 512:(b + 1) * 512], in_=xb[:]),
                              "st_x1", reads=[r_xb], store=True)
                    else:
                        P.op("act", I("activation", out=sq[:], in_=xb[:], func=AF.Square), reads=[r_xb], writes=[r_sq])
                        rms_rstd((srt, rstd, r_srt, r_rstd, r_sq), [sq[:, c, :] for c in range(8)], float(D), 1.0, 0)
                        for c in range(8):
                            P.op("dve", I("scalar_tensor_tensor", out=xb[:, c, :], in0=xb[:, c, :], scalar=gcol(32 + c),
                                          in1=rstd[:], op0=ALU.mult, op1=ALU.mult), reads=[r_xb, r_g, r_rstd], writes=[r_xb])
                        P.dma("sp", I("dma_start", out=fm(yT)[:, :, b * 512:(b + 1) * 512], in_=xb[:]),
                              "st_y", reads=[r_xb], store=True)
            wst.close()
            P.dram_barrier()

        with nc.Block() as block:
            P.play(block)
    return nc


def _host_tables(T, pos):
    half = 32
    inv_freq = (10000.0 ** (-np.arange(half, dtype=np.float32) / half)).astype(np.float32)
    ang = pos.astype(np.float32)[None, :] * inv_freq[:, None]
    cos = np.cos(ang).astype(np.float32)
    sin = np.sin(ang).astype(np.float32)
    cosT = np.tile(cos, (4, 1))
    sinT = np.tile(sin, (4, 1))
    return np.ascontiguousarray(cosT), np.ascontiguousarray(sinT)


def _perm_matrix():
    Pm = np.zeros((128, 128), np.float32)
    for blk in range(2):
        for d in range(64):
            m = blk * 64 + d
            if d < 32:
                Pm[blk * 64 + d + 32, m] = -1.0
            else:
                Pm[blk * 64 + d - 32, m] = 1.0
    return Pm


def _cmask():
    cm = np.zeros((128, 20, 512), np.float32)
    kk = np.arange(128)[:, None]
    qq = np.arange(512)[None, :]
    for j in range(20):
        delta = 128 * j - 1024 + kk - qq
        a = np.abs(delta)
        c = (a <= 64).astype(np.float32)
        c += ((delta % 4 == 0) & (a <= 256)).astype(np.float32)
        c += ((delta % 16 == 0) & (a <= 1024)).astype(np.float32)
        cm[:, j, :] = c
    return np.ascontiguousarray(cm.reshape(128, 20 * 512))


def _bt3_table(T, G, quarter, is_prompt):
    NB, NKT = T // 512, T // 128
    cols = []
    for mode_blocks, base in ((NB // G, quarter * (NB // G)),):
        for qb in range(mode_blocks):
            gqb = base + qb
            for j in range(20):
                kt = 4 * gqb - 8 + j
                ok = 0 <= kt < NKT
                if ok and is_prompt and ((kt < NKT // 2) != (gqb < NB // 2)):
                    ok = False
                cols.append(0.0 if ok else NEG)
    t = np.asarray(cols, np.float32)
    return np.ascontiguousarray(np.broadcast_to(t[None, :], (128, t.size)))


def _pack_gains(norm1_g, norm2_g, final_norm_g, diff_norm_g, dil_norm_g, cross_bias, kh0=0.0, kh1=0.0,
                keep=(1.0, 1.0, 1.0, 1.0), neg_lcorr=0.0):
    g = np.zeros((128, 60), np.float32)
    g[:, 54:58] = np.asarray(keep, np.float32)[None, :]
    g[:, 58] = neg_lcorr
    for l in range(DEPTH):
        g[:, l * 8:(l + 1) * 8] = norm1_g[l].reshape(8, 128).T
        g[:, 16 + l * 8:16 + (l + 1) * 8] = norm2_g[l].reshape(8, 128).T
        g[:, 40 + l] = diff_norm_g[l]
        g[:, 42 + l * 4:42 + (l + 1) * 4] = dil_norm_g[l].reshape(4, 128).T
    g[:, 32:40] = final_norm_g.reshape(8, 128).T
    g[:, 50] = 0.0
    g[:, 51] = cross_bias
    g[:, 52] = kh0
    g[:, 53] = kh1
    return g


_NC_CACHE = {}


def run_cores(T, seqs, weights, n_cores=8, G=4):
    if T not in _NC_CACHE:
        _NC_CACHE[T] = build_nc(T)
    nc = _NC_CACHE[T]
    (norm1_g, w_in, lq1, lk1, lq2, lk2, diff_norm_g, dil_norm_g, w_out, norm2_g, w_ff1, w_ff2, final_norm_g) = weights
    lamv = np.concatenate([np.asarray(a, np.float32).reshape(-1) for a in (lq1, lk1, lq2, lk2)])
    lamv = np.ascontiguousarray(np.broadcast_to(lamv[None, :], (128, lamv.size)))
    common = dict(w_in=np.ascontiguousarray(w_in, np.float32), w_out=np.ascontiguousarray(w_out, np.float32),
                  w_ff1=np.ascontiguousarray(w_ff1, np.float32), w_ff2=np.ascontiguousarray(w_ff2, np.float32),
                  lamv=lamv, perm=_perm_matrix(), cmask=_cmask())
    per_seq = []
    for x, pos, is_prompt in seqs:
        cosT, sinT = _host_tables(T, pos)
        per_seq.append((np.ascontiguousarray(np.asarray(x, np.float32).T), cosT, sinT, is_prompt))
    in_maps = []
    for c in range(n_cores):
        xT, cosT, sinT, is_prompt = per_seq[(c // G) % len(per_seq)]
        quarter = c % G
        m = dict(common)
        TQ = T // G
        m["xT"] = np.ascontiguousarray(xT[:, quarter * TQ:(quarter + 1) * TQ])
        m["cosL"] = np.ascontiguousarray(cosT[:, quarter * TQ:(quarter + 1) * TQ])
        m["sinL"] = np.ascontiguousarray(sinT[:, quarter * TQ:(quarter + 1) * TQ])
        kh0 = NEG if (is_prompt and quarter >= G // 2) else 0.0
        kh1 = NEG if (is_prompt and quarter < G // 2) else 0.0
        m["gains"] = _pack_gains(np.asarray(norm1_g), np.asarray(norm2_g), np.asarray(final_norm_g),
                                 np.asarray(diff_norm_g), np.asarray(dil_norm_g), NEG if is_prompt else 0.0, kh0, kh1,
                                 keep=[1.0 if (not is_prompt or ((r < G // 2) == (quarter < G // 2))) else 0.0 for r in range(G)],
                                 neg_lcorr=(-(T // 2) if is_prompt else 0.0))
        m["bt3"] = _bt3_table(T, G, quarter, is_prompt)
        in_maps.append(m)
    res = run_bass_kernel_spmd(nc, in_maps, core_ids=list(range(n_cores)))
    if os.environ.get("KDBG", ""):
        return [res.results[c] for c in range(n_cores)]
    outs = []
    for s in range(len(seqs)):
        yT = np.concatenate([res.results[s * G + q]["yT"] for q in range(G)], axis=1)
        outs.append(np.ascontiguousarray(yT.T))
    return outs


def kernel(x_prompt, x_sample, norm1_g, w_in, lambda_q1, lambda_k1, lambda_q2, lambda_k2,
           diff_norm_g, dil_norm_g, w_out, norm2_g, w_ff1, w_ff2, final_norm_g):
    x_prompt = np.asarray(x_prompt, np.float32)
    x_sample = np.asarray(x_sample, np.float32)
    B, S, _ = x_prompt.shape
    T = x_sample.shape[1]
    assert B * S == T and x_sample.shape[0] == 1
    weights = tuple(np.asarray(a, np.float32) for a in (
        norm1_g, w_in, lambda_q1, lambda_k1, lambda_q2, lambda_k2, diff_norm_g, dil_norm_g, w_out, norm2_g,
        w_ff1, w_ff2, final_norm_g))
    seqs = [
        (x_sample[0], np.arange(T), False),
        (x_prompt.reshape(T, D), np.concatenate([np.arange(S), np.arange(S)]), True),
    ]
    ys, yp = run_cores(T, seqs, weights)
    return (yp.reshape(B, S, D).astype(np.float32), ys.reshape(1, T, D).astype(np.float32))
```

```python
import math
import os
from contextlib import ExitStack

import numpy as np
import concourse.bass as bass
import concourse.mybir as mybir
from concourse.bass_utils import run_bass_kernel_spmd

F32 = mybir.dt.float32
BF16 = mybir.dt.bfloat16
AF = mybir.ActivationFunctionType
ALU = mybir.AluOpType

D = 1024
DEPTH = 2
D_IN = 3072
D_FF = 4096
EPS = 1e-5
NEG = -30000.0
ENGS = ("pe", "act", "dve", "pool", "sp")


class Tok:
    __slots__ = ("sem", "val")

    def __init__(self, sem=None, val=None):
        self.sem, self.val = sem, val


class Res:
    __slots__ = ("name", "w", "w_eng", "rs")

    def __init__(self, name):
        self.name, self.w, self.w_eng, self.rs = name, None, None, {}


class Prog:
    def __init__(self, nc, stack):
        self.nc, self.stack = nc, stack
        self.q = {e: [] for e in ENGS}
        self.sems = {}
        self.waited = {e: {} for e in ENGS}
        self.pending = {e: [] for e in ENGS}
        self.epoch = {e: 0 for e in ENGS}
        self.store_sems = set()

    def sem(self, name):
        if name not in self.sems:
            h = self.stack.enter_context(self.nc.semaphore(name))
            self.sems[name] = [h, 0]
        return self.sems[name]

    def new_epoch(self):
        self.full_barrier()
        for e in ENGS:
            self.epoch[e] += 1

    def _wait(self, eng, tok):
        if tok is None:
            return
        assert tok.val is not None, "unresolved lazy token"
        w = self.waited[eng]
        if w.get(tok.sem, 0) >= tok.val:
            return
        w[tok.sem] = tok.val
        h, v = self.sems[tok.sem][0], tok.val
        self.q[eng].append(I("wait_ge", h, v))

    def _deps(self, eng, reads, writes):
        for r in reads:
            if r.w is not None and not (eng == "pe" and r.w_eng == "pe"):
                self._wait(eng, r.w)
        for r in writes:
            if r.w is not None and r.w_eng != eng:
                self._wait(eng, r.w)
            for e2, t in r.rs.items():
                if e2 != eng:
                    self._wait(eng, t)

    def op(self, eng, fn, reads=(), writes=(), inc=True):
        self._deps(eng, reads, writes)
        tok = Tok()
        if inc:
            name = f"{eng}_{self.epoch[eng]}"
            s = self.sem(name)
            s[1] += 1
            tok.sem, tok.val = name, s[1]
            h = s[0]
            self.q[eng].append(lambda e, h=h, fn=fn: fn(e).then_inc(h, 1))
            for t in self.pending[eng]:
                t.sem, t.val = tok.sem, tok.val
            self.pending[eng] = []
        else:
            self.pending[eng].append(tok)
            self.q[eng].append(lambda e, fn=fn: fn(e))
        for r in reads:
            r.rs[eng] = tok
        for r in writes:
            r.w, r.w_eng, r.rs = tok, eng, {}
        return tok

    def dma(self, qeng, fn, dsem, reads=(), writes=(), store=False):
        self._deps(qeng, reads, writes)
        s = self.sem(dsem)
        s[1] += 16
        tok = Tok(dsem, s[1])
        h = s[0]
        self.q[qeng].append(lambda e, h=h, fn=fn: fn(e).then_inc(h, 16))
        key = "dma:" + dsem
        for r in reads:
            r.rs[key] = tok
        for r in writes:
            r.w, r.w_eng, r.rs = tok, key, {}
        if store:
            self.store_sems.add(dsem)
        return tok

    def collective(self, in_ap, out_ap, groups):
        s = self.sem("cc")
        s[1] += 1
        h, v = s[0], s[1]

        def f(e, h=h, v=v):
            e.collective_compute("AllGather", mybir.AluOpType.bypass, replica_groups=groups,
                                 ins=[in_ap], outs=[out_ap]).then_inc(h)
            e.wait_ge(h, v)

        self.q["pool"].append(f)
        self.waited["pool"]["cc"] = v
        self.store_sems.add("cc")

    def dram_barrier(self, engines=("sp", "pool")):
        for name in sorted(self.store_sems):
            tok = Tok(name, self.sems[name][1])
            for e in engines:
                self._wait(e, tok)

    def full_barrier(self):
        for e in ENGS:
            assert not self.pending[e], f"unresolved lazy tokens on {e} at barrier"
        for name in sorted(self.sems):
            cnt = self.sems[name][1]
            if cnt == 0:
                continue
            tok = Tok(name, cnt)
            for e in ENGS:
                self._wait(e, tok)

    def play(self, block):
        q = self.q

        @block.tensor
        def _(e):
            for f in q["pe"]:
                f(e)

        @block.scalar
        def _(e):
            for f in q["act"]:
                f(e)

        @block.vector
        def _(e):
            for f in q["dve"]:
                f(e)

        @block.gpsimd
        def _(e):
            for f in q["pool"]:
                f(e)

        @block.sync
        def _(e):
            for f in q["sp"]:
                f(e)


def I(name, *a, **k):
    return lambda e: getattr(e, name)(*a, **k)


def lambda_init(l):
    return 0.8 - 0.6 * math.exp(-0.3 * l)


def build_nc(T):
    NB = T // 512
    NKT = T // 128
    HALF_KT = NKT // 2
    HALF_QB = NB // 2
    G = 4
    TQ_LAST = T // G
    PADC = 1024
    nc = bass.Bass("TRN2", target_bir_lowering=False)

    def din(name, shape, dt=F32):
        return nc.dram_tensor(name, list(shape), dt, kind="ExternalInput").ap()

    xT = din("xT", [D, T // 4])
    w_in = din("w_in", [DEPTH, D, D_IN])
    w_out = din("w_out", [DEPTH, D, D])
    w_ff1 = din("w_ff1", [DEPTH, D, D_FF])
    w_ff2 = din("w_ff2", [DEPTH, D_FF, D])
    gains = din("gains", [128, 60])
    lamv = din("lamv", [128, 8 * 64])
    perm = din("perm", [128, 128])
    cmask_in = din("cmask", [128, 20 * 512])
    cosL = din("cosL", [128, T // 4])
    sinL = din("sinL", [128, T // 4])
    NBT3 = (NB // G) * 20
    bt3_in = din("bt3", [128, NBT3])
    yT = nc.dram_tensor("yT", [D, TQ_LAST], F32, kind="ExternalOutput").ap()

    dbg = os.environ.get("KDBG", "")

    def scr(name, shape, dt):
        if dbg:
            return nc.dram_tensor(name, list(shape), dt, kind="ExternalOutput").ap()
        return nc.dram_tensor(name, list(shape), dt).ap()

    TQ = TQ_LAST
    NBQ = NB // G
    CH = 1024
    NCH = TQ // CH
    assert NCH >= 1 and TQ % CH == 0
    x1_loc = scr("x1_loc", [D, TQ], F32)
    xm_t = scr("xm_loc", [D, TQ], F32)
    mix_t = scr("mix_loc", [D, TQ], F32)
    q_d = scr("qd_loc", [512, TQ], BF16)
    q_s = scr("qs_loc", [512, TQ], BF16)
    ks_loc = scr("ks_loc", [512, TQ + 2 * PADC], BF16)
    vs_loc = scr("vs_loc", [TQ + 2 * PADC, 512], BF16)

    def cbuf(name, rows):
        return nc.dram_tensor(name, [rows, 1024], BF16).ap()

    kin_d = [cbuf(f"kin_d{j}", 512) for j in range(NCH)]
    kin_s = [cbuf(f"kin_s{j}", 512) for j in range(NCH)]
    vin_d = [cbuf(f"vin_d{j}", 512) for j in range(NCH)]
    vin_s = [cbuf(f"vin_s{j}", 512) for j in range(NCH)]
    kg_d = [cbuf(f"kg_d{j}", G * 512) for j in range(NCH)]
    kg_s = [cbuf(f"kg_s{j}", G * 512) for j in range(NCH)]
    vg_d = [cbuf(f"vg_d{j}", G * 512) for j in range(NCH)]
    vg_s = [cbuf(f"vg_s{j}", G * 512) for j in range(NCH)]
    GROUPS = [list(range(g0, g0 + G)) for g0 in range(0, 8, G)]

    def vview(ap):
        return ap.rearrange("r (two c) -> (r two) c", two=2)

    def fm(ap):
        return ap.rearrange("(c p) t -> p c t", p=128)

    state = {}

    with ExitStack() as gstack:
        P = Prog(nc, gstack)

        def _prologue(e):
            pid = nc.partition_id(engines=[mybir.EngineType.SP])
            state["q0"] = (pid % G) * TQ_LAST
            state["left"] = (pid + (G - 1)) % G
            state["right"] = (pid + 1) % G

        P.q["sp"].append(_prologue)

        uid = [0]

        def sb(stack, name, shape, dt):
            uid[0] += 1
            return stack.enter_context(nc.sbuf_tensor(f"{name}_u{uid[0]}", list(shape), dt))

        def DYN(build):
            return lambda e: build(e, state["q0"])

        ps_banks = [gstack.enter_context(nc.psum_tensor(f"ps{i}", [128, 512], F32)) for i in range(8)]
        ps_res = [Res(f"ps{i}") for i in range(8)]

        ones = sb(gstack, "ones", [128, 128], BF16)
        r_ones = Res("ones")
        perm_bf = sb(gstack, "perm_bf", [128, 128], BF16)
        r_perm = Res("perm")
        g_sb = sb(gstack, "g_sb", [128, 60], F32)
        r_g = Res("g")
        bt3 = sb(gstack, "bt3", [128, NBT3], F32)
        r_bt3 = Res("bt3")
        lam_sb = sb(gstack, "lam_sb", [128, 8 * 64], F32)
        r_lam = Res("lam")
        lam_tmp = sb(gstack, "lam_tmp", [128, 64], F32)
        lam_acc = sb(gstack, "lam_acc", [128, 8], F32)
        r_lamacc = Res("lamacc")

        P.op("pool", I("memset", ones[:], 1.0), writes=[r_ones])
        P.dma("pool", I("dma_start", out=perm_bf[:], in_=perm), "ld_c0", writes=[r_perm])
        P.dma("sp", I("dma_start", out=g_sb[:], in_=gains), "ld_c1", writes=[r_g])
        P.dma("sp", I("dma_start", out=lam_sb[:], in_=lamv), "ld_c2", writes=[r_lam])
        P.dma("sp", I("dma_start", out=bt3[:], in_=bt3_in), "ld_c3", writes=[r_bt3])
        r_lt = Res("lam_tmp")
        for l in range(DEPTH):
            for m in range(2):
                qa = lam_sb[:, ((2 * m) * 2 + l) * 64:((2 * m) * 2 + l + 1) * 64]
                ka = lam_sb[:, ((2 * m + 1) * 2 + l) * 64:((2 * m + 1) * 2 + l + 1) * 64]
                col = lam_acc[:, 2 * l + m:2 * l + m + 1]
                P.op("dve", I("tensor_tensor", out=lam_tmp[:], in0=qa, in1=ka, op=ALU.mult), reads=[r_lam], writes=[r_lt])
                P.op("dve", I("tensor_reduce", out=col, in_=lam_tmp[:], op=ALU.add, axis=mybir.AxisListType.X),
                     reads=[r_lt], writes=[r_lamacc])
            c0 = lam_acc[:, 2 * l:2 * l + 2]
            P.op("act", I("activation", out=c0, in_=c0, func=AF.Exp), reads=[r_lamacc], writes=[r_lamacc])
            nl = lam_acc[:, 4 + l:5 + l]
            P.op("dve", I("scalar_tensor_tensor", out=nl, in0=lam_acc[:, 2 * l + 1:2 * l + 2], scalar=-lambda_init(l),
                          in1=lam_acc[:, 2 * l:2 * l + 1], op0=ALU.add, op1=ALU.subtract),
                 reads=[r_lamacc], writes=[r_lamacc])

        def gcol(i):
            return g_sb[:, i:i + 1]

        def rms_rstd(bufs, chunks, denom, scale_extra, ps_i):
            srt, rstd, r_srt, r_rstd, r_sq = bufs
            n = len(chunks)
            for i, ch in enumerate(chunks):
                P.op("pe", I("matmul", ps_banks[ps_i][:], lhsT=ones[:], rhs=ch, start=(i == 0), stop=(i == n - 1)),
                     reads=[r_ones, r_sq], writes=[ps_res[ps_i]], inc=(i == n - 1))
            s2 = 1.0 / (scale_extra * scale_extra)
            P.op("act", I("activation", out=srt[:], in_=ps_banks[ps_i][:], func=AF.Sqrt, scale=s2 / denom, bias=EPS * s2),
                 reads=[ps_res[ps_i]], writes=[r_srt])
            P.op("dve", I("reciprocal", out=rstd[:], in_=srt[:]), reads=[r_srt], writes=[r_rstd])

        def xcols(ap3, b, dyn):
            if not dyn:
                return lambda q0: ap3[:, :, b * 512:(b + 1) * 512]
            return lambda q0: ap3[:, :, bass.ds(q0 + b * 512, 512)]

        def cols2(ap2, b, dyn):
            if not dyn:
                return lambda q0: ap2[:, b * 512:(b + 1) * 512]
            return lambda q0: ap2[:, bass.ds(q0 + b * 512, 512)]

        for l in range(DEPTH):
            if dbg and l == 1 and dbg != "2":
                break
            last_layer = (l == DEPTH - 1)
            x_src = xT if l == 0 else x1_loc
            bt3_base = 0

            P.new_epoch()
            with ExitStack() as st:
                w = sb(st, "w_in_sb", [128, 8, D_IN], BF16)
                r_w = Res("w_in")
                for kc in range(8):
                    for hh in range(2):
                        P.dma("pool", I("dma_start", out=w[:, kc, hh * 1536:(hh + 1) * 1536],
                                        in_=w_in[l, kc * 128:(kc + 1) * 128, hh * 1536:(hh + 1) * 1536]),
                              f"ld_w{l}", writes=[r_w])
                xb = [sb(st, f"xb{i}", [128, 8, 512], F32) for i in range(2)]
                r_xb = [Res(f"xb{i}") for i in range(2)]
                cs = [sb(st, f"cs{i}", [128, 2, 512], F32) for i in range(2)]
                r_cs = [Res(f"cs{i}") for i in range(2)]
                sq = sb(st, "sq", [128, 8, 512], BF16)
                r_sq = Res("sq")
                hTb = [sb(st, f"hT{i}", [128, 8, 512], BF16) for i in range(2)]
                r_hb = [Res(f"hT{i}") for i in range(2)]
                srt = sb(st, "srt", [128, 512], F32)
                rstd = sb(st, "rstd", [128, 512], F32)
                r_srt, r_rstd = Res("srt"), Res("rstd")
                qb = [sb(st, f"qb{i}", [128, 512], BF16) for i in range(2)]
                r_qb = [Res(f"qb{i}") for i in range(2)]
                t1 = [sb(st, f"t1_{i}", [128, 512], F32) for i in range(2)]
                r_t1 = [Res(f"t1_{i}") for i in range(2)]
                t2 = [sb(st, f"t2_{i}", [128, 512], F32) for i in range(2)]
                r_t2 = [Res(f"t2_{i}") for i in range(2)]
                qo = [sb(st, f"qo{i}", [128, 512], BF16) for i in range(3)]
                r_qo = [Res(f"qo{i}") for i in range(3)]
                vo = [sb(st, f"vo{i}", [128, 512], BF16) for i in range(2)]
                r_vo = [Res(f"vo{i}") for i in range(2)]
                cnts = {"ld": 0, "qk": 0, "v": 0}

                def load_blk(b, dyn):
                    i = cnts["ld"] % 2
                    cnts["ld"] += 1
                    xs_, cc_, ss_ = (x_src, cosL, sinL)
                    P.dma("sp", I("dma_start", out=xb[i][:], in_=fm(xs_)[:, :, b * 512:(b + 1) * 512]),
                          f"ld_xb{i}", writes=[r_xb[i]])
                    P.dma("sp", I("dma_start", out=cs[i][:, 0, :], in_=cc_[:, b * 512:(b + 1) * 512]),
                          f"ld_cs{i}", writes=[r_cs[i]])
                    P.dma("sp", I("dma_start", out=cs[i][:, 1, :], in_=ss_[:, b * 512:(b + 1) * 512]),
                          f"ld_cs{i}", writes=[r_cs[i]])
                    return i

                def norm_stage(i, hsel):
                    hT, r_h = hTb[hsel], r_hb[hsel]
                    P.op("act", I("activation", out=sq[:], in_=xb[i][:], func=AF.Square), reads=[r_xb[i]], writes=[r_sq])
                    rms_rstd((srt, rstd, r_srt, r_rstd, r_sq), [sq[:, c, :] for c in range(8)], float(D), 1.0, 0)
                    for c in range(8):
                        P.op("dve", I("scalar_tensor_tensor", out=hT[:, c, :], in0=xb[i][:, c, :], scalar=gcol(l * 8 + c),
                                      in1=rstd[:], op0=ALU.mult, op1=ALU.mult),
                             reads=[r_xb[i], r_g, r_rstd], writes=[r_h])

                def p1_block(i, b, qk_specs, do_v, hsel, mid_hook):
                    hT, r_h = hTb[hsel], r_hb[hsel]
                    nchunk = 0
                    for col0, dst_fn in qk_specs:
                        for j in range(4):
                            nchunk += 1
                            if nchunk == 9 and mid_hook is not None:
                                mid_hook()
                            cnt = cnts["qk"]
                            cnts["qk"] += 1
                            pj, pr, bi, oi = 1 + cnt % 2, 3 + cnt % 2, cnt % 2, cnt % 3
                            cbase = col0 + j * 128
                            for kc in range(8):
                                P.op("pe", I("matmul", ps_banks[pj][:], lhsT=w[:, kc, cbase:cbase + 128], rhs=hT[:, kc, :],
                                             start=(kc == 0), stop=(kc == 7)),
                                     reads=[r_w, r_h], writes=[ps_res[pj]], inc=(kc == 7))
                            P.op("act", I("copy", out=qb[bi][:], in_=ps_banks[pj][:]), reads=[ps_res[pj]], writes=[r_qb[bi]])
                            P.op("pe", I("matmul", ps_banks[pr][:], lhsT=perm_bf[:], rhs=qb[bi][:], start=True, stop=True),
                                 reads=[r_perm, r_qb[bi]], writes=[ps_res[pr]])
                            P.op("dve", I("tensor_tensor", out=t1[bi][:], in0=qb[bi][:], in1=cs[i][:, 0, :], op=ALU.mult),
                                 reads=[r_qb[bi], r_cs[i]], writes=[r_t1[bi]])
                            P.op("dve", I("tensor_tensor", out=t2[bi][:], in0=ps_banks[pr][:], in1=cs[i][:, 1, :], op=ALU.mult),
                                 reads=[ps_res[pr], r_cs[i]], writes=[r_t2[bi]])
                            P.op("dve", I("tensor_tensor", out=qo[oi][:], in0=t1[bi][:], in1=t2[bi][:], op=ALU.add),
                                 reads=[r_t1[bi], r_t2[bi]], writes=[r_qo[oi]])
                            P.dma("sp", I("dma_start", out=dst_fn(j, b), in_=qo[oi][:]),
                                  f"st_qk{oi}", reads=[r_qo[oi]], store=True)
                    if do_v:
                        for col0, vins in ((1024, vin_d), (2560, vin_s)):
                            for sub in range(4):
                                pv, vi = 5 + cnts["v"] % 2, cnts["v"] % 2
                                cnts["v"] += 1
                                for kc in range(8):
                                    P.op("pe", I("matmul", ps_banks[pv][:], lhsT=hT[:, kc, sub * 128:(sub + 1) * 128],
                                                 rhs=w[:, kc, col0:col0 + 512], start=(kc == 0), stop=(kc == 7)),
                                         reads=[r_w, r_h], writes=[ps_res[pv]], inc=(kc == 7))
                                P.op("act", I("copy", out=vo[vi][:], in_=ps_banks[pv][:]), reads=[ps_res[pv]], writes=[r_vo[vi]])
                                r0 = (b * 512) % CH + sub * 128
                                P.dma("sp", I("dma_start", out=vview(vins[(b * 512) // CH])[r0:r0 + 128, :], in_=vo[vi][:]),
                                      f"st_v{vi}", reads=[r_vo[vi]], store=True)

                def q_dst(t):
                    return lambda j, b: t[j * 128:(j + 1) * 128, b * 512:(b + 1) * 512]

                def k_dst(chunks):
                    return lambda j, b: chunks[(b * 512) // CH][j * 128:(j + 1) * 128, (b * 512) % CH:(b * 512) % CH + 512]

                specs = [(0, q_dst(q_d)), (1536, q_dst(q_s)), (512, k_dst(kin_d)), (2048, k_dst(kin_s))]
                nxt = load_blk(0, True)
                norm_stage(nxt, 0)
                for b in range(NBQ):
                    i = nxt
                    hook = None
                    if b + 1 < NBQ:
                        nxt = load_blk(b + 1, True)
                        hook = (lambda nxt=nxt, b=b: norm_stage(nxt, (b + 1) % 2))
                    p1_block(i, b, specs, True, b % 2, hook)
                    if ((b + 1) * 512) % CH == 0:
                        j = ((b + 1) * 512) // CH - 1
                        P.dram_barrier(engines=("pool",))
                        for ins_, outs_ in ((kin_d, kg_d), (vin_d, vg_d), (kin_s, kg_s), (vin_s, vg_s)):
                            P.collective(ins_[j], outs_[j], GROUPS)
            P.dram_barrier()
            P.dram_barrier()

            P.new_epoch()
            with ExitStack() as st:
                kTb = [sb(st, f"kT{i}", [128, T], BF16) for i in range(2)]
                vvb = [sb(st, f"vv{i}", [128, NKT, 128], BF16) for i in range(2)]
                r_kTb = [Res(f"kT{i}") for i in range(2)]
                r_vvb = [Res(f"vv{i}") for i in range(2)]

                def load_kv(h):
                    i = h % 2
                    for r in range(G):
                        for j in range(NCH):
                            t0 = r * TQ + j * CH
                            P.dma("sp", I("dma_start", out=kTb[i][:, t0:t0 + CH],
                                          in_=kg_d[j][r * 512 + h * 128:r * 512 + (h + 1) * 128, :]), f"ld_kT{i}", writes=[r_kTb[i]])
                            P.dma("sp", I("dma_start", out=vvb[i][:, t0 // 128:(t0 + CH) // 128, :],
                                          in_=vview(vg_d[j])[r * CH:(r + 1) * CH, h * 128:(h + 1) * 128].rearrange(
                                              "(k p) e -> p k e", p=128)), f"ld_vv{i}", writes=[r_vvb[i]])

                def mask_kv(h):
                    i = h % 2
                    for r in range(G):
                        P.op("pool", I("tensor_scalar", out=kTb[i][:, r * TQ:(r + 1) * TQ], in0=kTb[i][:, r * TQ:(r + 1) * TQ],
                                       scalar1=gcol(54 + r), scalar2=1.0, op0=ALU.mult, op1=ALU.mult),
                             reads=[r_kTb[i], r_g], writes=[r_kTb[i]])
                        P.op("pool", I("tensor_scalar", out=vvb[i][:, r * (TQ // 128):(r + 1) * (TQ // 128), :],
                                       in0=vvb[i][:, r * (TQ // 128):(r + 1) * (TQ // 128), :],
                                       scalar1=gcol(54 + r), scalar2=1.0, op0=ALU.mult, op1=ALU.mult),
                             reads=[r_vvb[i], r_g], writes=[r_vvb[i]])

                qA = [sb(st, f"qA{i}", [128, 512], BF16) for i in range(2)]
                qB = [sb(st, f"qB{i}", [128, 512], BF16) for i in range(2)]
                r_qA = [Res(f"qA{i}") for i in range(2)]
                r_qB = [Res(f"qB{i}") for i in range(2)]
                NPB = 8
                pt = [sb(st, f"pt{i}", [128, 512], BF16) for i in range(2 * NPB)]
                r_pt = [Res(f"pt{i}") for i in range(2 * NPB)]
                sab = [sb(st, f"sab{i}", [128, 512], BF16) for i in range(4)]
                scd = [sb(st, f"scd{i}", [128, 512], BF16) for i in range(4)]
                s4 = [sb(st, f"s4_{i}", [128, 512], BF16) for i in range(4)]
                r_sab = [Res(f"sab{i}") for i in range(4)]
                r_scd = [Res(f"scd{i}") for i in range(4)]
                r_s4 = [Res(f"s4_{i}") for i in range(4)]
                slot_of = [0, 0, 0, 0]
                q4cnt = 0
                oc = [sb(st, f"oc{i}", [128, 512], F32) for i in range(4)]
                r_oc = [Res(f"oc{i}") for i in range(4)]
                ob = [sb(st, f"ob{i}", [128, 512], F32) for i in range(2)]
                r_ob = [Res(f"ob{i}") for i in range(2)]
                for i in range(2):
                    P.op("pool", I("memset", qA[i][64:128, :], 0.0), writes=[r_qA[i]])
                    P.op("pool", I("memset", qB[i][0:64, :], 0.0), writes=[r_qB[i]])
                qcnt = 0
                ucnt = 0
                load_kv(0)
                mask_kv(0)
                for h in range(4):
                    kT, vv, r_kT, r_vv = kTb[h % 2], vvb[h % 2], r_kTb[h % 2], r_vvb[h % 2]

                    def load_q(qbi, h=h):
                        i = qbi % 2
                        P.dma("sp", I("dma_start", out=qA[i][0:64, :], in_=q_d[h * 128:h * 128 + 64, qbi * 512:(qbi + 1) * 512]),
                              f"ld_qA{i}", writes=[r_qA[i]])
                        P.dma("sp", I("dma_start", out=qB[i][64:128, :],
                                      in_=q_d[h * 128 + 64:h * 128 + 128, qbi * 512:(qbi + 1) * 512]),
                              f"ld_qB{i}", writes=[r_qB[i]])

                    load_q(0)
                    for qbi in range(NBQ):
                        qi = qbi % 2
                        if qbi + 1 < NBQ:
                            load_q(qbi + 1)
                        if qbi == 0 and h + 1 < 4:
                            load_kv(h + 1)
                        if qbi == NBQ - 1 and h + 1 < 4:
                            mask_kv(h + 1)

                        def qk(kt, qi=qi, kT=kT, r_kT=r_kT):
                            sb_ = (kt % 2) * 2
                            P.op("pe", I("matmul", ps_banks[sb_][:], lhsT=kT[:, kt * 128:(kt + 1) * 128], rhs=qA[qi][:],
                                         start=True, stop=True), reads=[r_kT, r_qA[qi]], writes=[ps_res[sb_]])
                            P.op("pe", I("matmul", ps_banks[sb_ + 1][:], lhsT=kT[:, kt * 128:(kt + 1) * 128], rhs=qB[qi][:],
                                         start=True, stop=True), reads=[r_kT, r_qB[qi]], writes=[ps_res[sb_ + 1]])

                        qk(0)
                        deferred = []
                        for kt in range(NKT):
                            if kt + 1 < NKT:
                                qk(kt + 1)
                            sb_ = (kt % 2) * 2
                            bias_ap = None
                            pi = (ucnt % NPB) * 2
                            ucnt += 1
                            slot_of[kt % 4] = pi
                            for m in range(2):
                                if bias_ap is not None:
                                    P.op("act", I("activation", out=pt[pi + m][:], in_=ps_banks[sb_ + m][:], func=AF.Exp,
                                                  bias=bias_ap, scale=0.125),
                                         reads=[ps_res[sb_ + m], r_g], writes=[r_pt[pi + m]])
                                else:
                                    P.op("act", I("activation", out=pt[pi + m][:], in_=ps_banks[sb_ + m][:], func=AF.Exp,
                                                  scale=0.125), reads=[ps_res[sb_ + m]], writes=[r_pt[pi + m]])
                            last = (kt == NKT - 1)
                            for m in range(2):
                                P.op("pe", I("matmul", ps_banks[4 + m][:], lhsT=vv[:, kt, :], rhs=pt[pi + m][:],
                                             start=(kt == 0), stop=last),
                                     reads=[r_vv, r_pt[pi + m]], writes=[ps_res[4 + m]], inc=last)
                            r4 = kt % 4
                            if r4 == 1:
                                for m in range(2):
                                    bi = 2 * (q4cnt % 2) + m
                                    P.op("dve", I("tensor_tensor", out=sab[bi][:], in0=pt[slot_of[0] + m][:],
                                                  in1=pt[slot_of[1] + m][:], op=ALU.add),
                                         reads=[r_pt[slot_of[0] + m], r_pt[slot_of[1] + m]], writes=[r_sab[bi]])
                            if r4 == 3:
                                for m in range(2):
                                    bi = 2 * (q4cnt % 2) + m
                                    P.op("pool", I("tensor_tensor", out=scd[bi][:], in0=pt[slot_of[2] + m][:],
                                                   in1=pt[slot_of[3] + m][:], op=ALU.add),
                                         reads=[r_pt[slot_of[2] + m], r_pt[slot_of[3] + m]], writes=[r_scd[bi]])
                                    P.op("dve", I("tensor_tensor", out=s4[bi][:], in0=sab[bi][:], in1=scd[bi][:], op=ALU.add),
                                         reads=[r_sab[bi], r_scd[bi]], writes=[r_s4[bi]])
                                    deferred.append((kt + 2, m, bi, kt == 3, last))
                                q4cnt += 1
                            while deferred and (deferred[0][0] <= kt or last):
                                _, m, bi, first_, last_ = deferred.pop(0)
                                P.op("pe", I("matmul", ps_banks[6 + m][:], lhsT=ones[:], rhs=s4[bi][:], start=first_, stop=last_),
                                     reads=[r_ones, r_s4[bi]], writes=[ps_res[6 + m]], inc=last_)
                        for a in range(2):
                            P.op("dve", I("tensor_copy", out=oc[a][:], in_=ps_banks[4 + a][:]),
                                 reads=[ps_res[4 + a]], writes=[r_oc[a]])
                        for a in range(2, 4):
                            P.op("dve", I("tensor_scalar", out=oc[a][:], in0=ps_banks[4 + a][:], scalar1=gcol(58), scalar2=None,
                                          op0=ALU.add), reads=[ps_res[4 + a], r_g], writes=[r_oc[a]])
                        for m in range(2):
                            P.op("dve", I("reciprocal", out=oc[2 + m][:], in_=oc[2 + m][:]),
                                 reads=[r_oc[2 + m]], writes=[r_oc[2 + m]])
                            P.op("dve", I("tensor_tensor", out=oc[m][:], in0=oc[m][:], in1=oc[2 + m][:], op=ALU.mult),
                                 reads=[r_oc[m], r_oc[2 + m]], writes=[r_oc[m]])
                        oi = qcnt % 2
                        qcnt += 1
                        P.op("dve", I("scalar_tensor_tensor", out=ob[oi][:], in0=oc[1][:], scalar=lam_acc[:, 4 + l:5 + l],
                                      in1=oc[0][:], op0=ALU.mult, op1=ALU.add),
                             reads=[r_oc[0], r_oc[1], r_lamacc], writes=[r_ob[oi]])
                        P.dma("sp", I("dma_start", out=mix_t[h * 128:(h + 1) * 128, qbi * 512:(qbi + 1) * 512], in_=ob[oi][:]),
                              f"st_mx{oi}", reads=[r_ob[oi]], store=True)

            P.new_epoch()
            with ExitStack() as st:
                WT = TQ + 2 * PADC
                WKT = WT // 128
                for j in range(NCH):
                    P.dma("sp", I("dma_start", out=ks_loc[:, PADC + j * CH:PADC + (j + 1) * CH], in_=kin_s[j]), "st_kloc", store=True)
                    P.dma("sp", I("dma_start", out=vs_loc[PADC + j * CH:PADC + (j + 1) * CH, :], in_=vview(vin_s[j])), "st_kloc", store=True)
                P.dma("sp", DYN(lambda e, q0: e.dma_start(out=ks_loc[:, 0:PADC], in_=kg_s[NCH - 1][bass.ds(state["left"] * 512, 512), :])),
                      "st_kloc", store=True)
                P.dma("sp", DYN(lambda e, q0: e.dma_start(out=ks_loc[:, PADC + TQ:], in_=kg_s[0][bass.ds(state["right"] * 512, 512), :])),
                      "st_kloc", store=True)
                P.dma("sp", DYN(lambda e, q0: e.dma_start(out=vs_loc[0:PADC, :], in_=vview(vg_s[NCH - 1])[bass.ds(state["left"] * CH, CH), :])),
                      "st_kloc", store=True)
                P.dma("sp", DYN(lambda e, q0: e.dma_start(out=vs_loc[PADC + TQ:, :], in_=vview(vg_s[0])[bass.ds(state["right"] * CH, CH), :])),
                      "st_kloc", store=True)
                P.dram_barrier()
                kT = sb(st, "kTs", [128, WT], BF16)
                vv = sb(st, "vvs", [128, WKT, 128], BF16)
                r_kT, r_vv = Res("kTs"), Res("vvs")
                cm = sb(st, "cm", [128, 20, 512], BF16)
                r_cm = Res("cm")
                for j in range(20):
                    P.dma("pool", I("dma_start", out=cm[:, j, :], in_=cmask_in[:, j * 512:(j + 1) * 512]), "ld_cm", writes=[r_cm])
                qA = [sb(st, f"sqA{i}", [128, 512], BF16) for i in range(2)]
                qB = [sb(st, f"sqB{i}", [128, 512], BF16) for i in range(2)]
                r_qA = [Res(f"sqA{i}") for i in range(2)]
                r_qB = [Res(f"sqB{i}") for i in range(2)]
                NPB = 6
                LOOK = 3
                pt = [sb(st, f"spt{i}", [128, 512], BF16) for i in range(NPB)]
                r_pt = [Res(f"spt{i}") for i in range(NPB)]
                pm = [sb(st, f"spm{i}", [128, 512], BF16) for i in range(NPB)]
                r_pm = [Res(f"spm{i}") for i in range(NPB)]
                oc = [sb(st, f"soc{i}", [64, 512], F32) for i in range(2)]
                r_oc = [Res(f"soc{i}") for i in range(2)]
                ob = [sb(st, f"sob{i}", [64, 512], F32) for i in range(2)]
                r_ob = [Res(f"sob{i}") for i in range(2)]
                for i in range(2):
                    P.op("pool", I("memset", qA[i][64:128, :], 0.0), writes=[r_qA[i]])
                    P.op("pool", I("memset", qB[i][0:64, :], 0.0), writes=[r_qB[i]])
                ucnt = 0
                qcnt = 0
                scnt = [0]
                for cp in range(4):
                    nparts = 4
                    for part in range(nparts):
                        c0, c1 = part * (WT // nparts), (part + 1) * (WT // nparts)
                        k0, k1 = part * (WKT // nparts), (part + 1) * (WKT // nparts)
                        ksrc, vsrc = ks_loc, vs_loc
                        P.dma("sp" if part % 2 == 0 else "pool",
                              I("dma_start", out=kT[:, c0:c1], in_=ksrc[cp * 128:(cp + 1) * 128, c0:c1]),
                              "ld_kTs", writes=[r_kT])
                        P.dma("pool" if part % 2 == 0 else "sp",
                              I("dma_start", out=vv[:, k0:k1, :],
                                in_=vsrc[k0 * 128:k1 * 128, cp * 128:(cp + 1) * 128].rearrange("(k p) e -> p k e", p=128)),
                              "ld_vvs", writes=[r_vv])

                    def load_q(qbi, cp=cp):
                        i = qbi % 2
                        P.dma("sp", I("dma_start", out=qA[i][0:64, :], in_=q_s[cp * 128:cp * 128 + 64, qbi * 512:(qbi + 1) * 512]),
                              f"ld_sqA{i}", writes=[r_qA[i]])
                        P.dma("sp", I("dma_start", out=qB[i][64:128, :],
                                      in_=q_s[cp * 128 + 64:cp * 128 + 128, qbi * 512:(qbi + 1) * 512]),
                              f"ld_sqB{i}", writes=[r_qB[i]])

                    load_q(0)
                    for qbi in range(NBQ):
                        qi = qbi % 2
                        if qbi + 1 < NBQ:
                            load_q(qbi + 1)
                        for hh in range(2):
                            qop, r_qop = (qA[qi], r_qA[qi]) if hh == 0 else (qB[qi], r_qB[qi])
                            po, pl = 4 + hh, 6 + hh
                            tiles = list(range(20))
                            sbank = {}

                            def crange(j):
                                return max(0, 128 * j - 2048), min(512, 128 * j + 128)

                            def qk(idx, tiles=tiles, qop=qop, r_qop=r_qop, qbi=qbi):
                                wt = qbi * 4 + tiles[idx]
                                c0, c1 = crange(tiles[idx])
                                sbk = scnt[0] % 4
                                scnt[0] += 1
                                P.op("pe", I("matmul", ps_banks[sbk][:, c0:c1], lhsT=kT[:, wt * 128:(wt + 1) * 128], rhs=qop[:, c0:c1],
                                             start=True, stop=True), reads=[r_kT, r_qop], writes=[ps_res[sbk]])
                                sbank[idx] = sbk

                            for a in range(min(LOOK, len(tiles))):
                                qk(a)
                            for idx, j in enumerate(tiles):
                                if idx + LOOK < len(tiles):
                                    qk(idx + LOOK)
                                sbk = sbank[idx]
                                wt = qbi * 4 + j
                                bcol = bt3[:, bt3_base + qbi * 20 + j:bt3_base + qbi * 20 + j + 1]
                                pi = ucnt % NPB
                                ucnt += 1
                                c0, c1 = crange(j)
                                P.op("act", I("activation", out=pt[pi][:, c0:c1], in_=ps_banks[sbk][:, c0:c1], func=AF.Exp, bias=bcol,
                                              scale=0.125), reads=[ps_res[sbk], r_bt3], writes=[r_pt[pi]])
                                P.op("dve", I("tensor_tensor", out=pm[pi][:, c0:c1], in0=pt[pi][:, c0:c1], in1=cm[:, j, c0:c1], op=ALU.mult),
                                     reads=[r_pt[pi], r_cm], writes=[r_pm[pi]])
                                first, last = idx == 0, idx == len(tiles) - 1
                                P.op("pe", I("matmul", ps_banks[po][0:64, c0:c1], lhsT=vv[:, wt, hh * 64:(hh + 1) * 64], rhs=pm[pi][:, c0:c1],
                                             start=first, stop=last, skip_group_check=True),
                                     reads=[r_vv, r_pm[pi]], writes=[ps_res[po]], inc=last)
                                P.op("pe", I("matmul", ps_banks[pl][0:64, c0:c1], lhsT=ones[:, 0:64], rhs=pm[pi][:, c0:c1],
                                             start=first, stop=last, skip_group_check=True),
                                     reads=[r_ones, r_pm[pi]], writes=[ps_res[pl]], inc=last)
                            ci = qcnt % 2
                            qcnt += 1
                            P.op("dve", I("reciprocal", out=oc[ci][:], in_=ps_banks[pl][0:64, :]), reads=[ps_res[pl]], writes=[r_oc[ci]])
                            P.op("dve", I("tensor_tensor", out=ob[ci][:], in0=ps_banks[po][0:64, :], in1=oc[ci][:], op=ALU.mult),
                                 reads=[ps_res[po], r_oc[ci]], writes=[r_ob[ci]])
                            row0 = 512 + (cp * 2 + hh) * 64
                            P.dma("sp", I("dma_start", out=mix_t[row0:row0 + 64, qbi * 512:(qbi + 1) * 512], in_=ob[ci][:]),
                                  f"st_ms{ci}", reads=[r_ob[ci]], store=True)
            P.dram_barrier()

            P.new_epoch()
            wst = ExitStack()
            w1 = sb(wst, "w1", [128, 8, D_FF], BF16)
            r_w1 = Res("w1")
            with ExitStack() as st:
                wo = sb(st, "wo", [128, 8, D], BF16)
                r_wo = Res("wo")
                for kc in range(8):
                    P.dma("pool", I("dma_start", out=wo[:, kc, :], in_=w_out[l, kc * 128:(kc + 1) * 128, :]), f"ld_wo{l}", writes=[r_wo])
                for kc in range(8):
                    for hh in range(2):
                        P.dma("pool", I("dma_start", out=w1[:, kc, hh * 2048:(hh + 1) * 2048],
                                        in_=w_ff1[l, kc * 128:(kc + 1) * 128, hh * 2048:(hh + 1) * 2048]), f"ld_w1{l}", writes=[r_w1])
                xb = [sb(st, f"axb{i}", [128, 8, 512], F32) for i in range(2)]
                r_xb = [Res(f"axb{i}") for i in range(2)]
                mr = [sb(st, f"mr{i}", [128, 8, 512], F32) for i in range(2)]
                r_mr = [Res(f"mr{i}") for i in range(2)]
                sq = sb(st, "asq", [128, 8, 512], BF16)
                r_sq = Res("asq")
                mx = sb(st, "amx", [128, 8, 512], BF16)
                r_mx = Res("amx")
                srt = [sb(st, f"asrt{i}", [128, 512], F32) for i in range(2)]
                rstd = [sb(st, f"arstd{i}", [128, 512], F32) for i in range(2)]
                r_srt = [Res(f"asrt{i}") for i in range(2)]
                r_rstd = [Res(f"arstd{i}") for i in range(2)]

                def load_blk(b):
                    i = b % 2
                    xs_ = x_src
                    P.dma("sp", I("dma_start", out=xb[i][:], in_=fm(xs_)[:, :, b * 512:(b + 1) * 512]), f"ld_axb{i}", writes=[r_xb[i]])
                    P.dma("pool", I("dma_start", out=mr[i][:], in_=fm(mix_t)[:, :, b * 512:(b + 1) * 512]), f"ld_mr{i}", writes=[r_mr[i]])

                load_blk(0)
                ncnt = 0
                pcnt = 0
                li = lambda_init(l)
                for b in range(NBQ):
                    i = b % 2
                    if b + 1 < NBQ:
                        load_blk(b + 1)
                    P.op("act", I("activation", out=sq[:], in_=mr[i][:], func=AF.Square), reads=[r_mr[i]], writes=[r_sq])
                    for c in range(4):
                        ni = ncnt % 2
                        ncnt += 1
                        rms_rstd((srt[ni], rstd[ni], r_srt[ni], r_rstd[ni], r_sq), [sq[:, c, :]], 128.0, 1.0 - li, ni)
                        P.op("dve", I("scalar_tensor_tensor", out=mx[:, c, :], in0=mr[i][:, c, :], scalar=gcol(40 + l),
                                      in1=rstd[ni][:], op0=ALU.mult, op1=ALU.mult),
                             reads=[r_mr[i], r_g, r_rstd[ni]], writes=[r_mx])
                    ni = ncnt % 2
                    ncnt += 1
                    rms_rstd((srt[ni], rstd[ni], r_srt[ni], r_rstd[ni], r_sq), [sq[:, c, :] for c in range(4, 8)], 512.0, 1.0, ni)
                    for c in range(4, 8):
                        P.op("dve", I("scalar_tensor_tensor", out=mx[:, c, :], in0=mr[i][:, c, :], scalar=gcol(42 + l * 4 + (c - 4)),
                                      in1=rstd[ni][:], op0=ALU.mult, op1=ALU.mult),
                             reads=[r_mr[i], r_g, r_rstd[ni]], writes=[r_mx])
                    for oc_ in range(8):
                        pb = 2 + pcnt % 4
                        pcnt += 1
                        for kc in range(8):
                            P.op("pe", I("matmul", ps_banks[pb][:], lhsT=wo[:, kc, oc_ * 128:(oc_ + 1) * 128], rhs=mx[:, kc, :],
                                         start=(kc == 0), stop=(kc == 7)), reads=[r_wo, r_mx], writes=[ps_res[pb]], inc=(kc == 7))
                        P.op("dve", I("tensor_tensor", out=xb[i][:, oc_, :], in0=xb[i][:, oc_, :], in1=ps_banks[pb][:], op=ALU.add),
                             reads=[r_xb[i], ps_res[pb]], writes=[r_xb[i]])
                    P.dma("sp", I("dma_start", out=fm(xm_t)[:, :, b * 512:(b + 1) * 512], in_=xb[i][:]),
                          f"st_xm{i}", reads=[r_xb[i]], store=True)
            P.dram_barrier()

            P.new_epoch()
            with ExitStack() as st:
                w2 = sb(st, "w2", [128, 32, D], BF16)
                r_w2 = Res("w2")
                for fc in range(32):
                    P.dma("pool", I("dma_start", out=w2[:, fc, :], in_=w_ff2[l, fc * 128:(fc + 1) * 128, :]), f"ld_w2{l}", writes=[r_w2])
                xbb = [sb(st, f"bxb{i}", [128, 8, 512], F32) for i in range(2)]
                r_xbb = [Res(f"bxb{i}") for i in range(2)]
                h2b = [sb(st, f"bh2{i}", [128, 8, 512], BF16) for i in range(2)]
                r_h2b = [Res(f"bh2{i}") for i in range(2)]
                uu = sb(st, "buu", [128, 16, 512], BF16)
                r_uu = Res("buu")
                rr = [sb(st, f"brr{i}", [128, 512], F32) for i in range(2)]
                r_rr = [Res(f"brr{i}") for i in range(2)]
                srt = sb(st, "bsrt", [128, 512], F32)
                rstd = sb(st, "brstd", [128, 512], F32)
                r_srt, r_rstd = Res("bsrt"), Res("brstd")
                pcnt = 0
                rcnt = 0
                def load_bx(b):
                    P.dma("sp", I("dma_start", out=xbb[b % 2][:], in_=fm(xm_t)[:, :, b * 512:(b + 1) * 512]),
                          f"ld_bxb{b % 2}", writes=[r_xbb[b % 2]])

                def norm2_stage(b):
                    xb, r_xb, h2, r_h2 = xbb[b % 2], r_xbb[b % 2], h2b[b % 2], r_h2b[b % 2]
                    P.op("act", I("activation", out=h2[:], in_=xb[:], func=AF.Square), reads=[r_xb], writes=[r_h2])
                    rms_rstd((srt, rstd, r_srt, r_rstd, r_h2), [h2[:, c, :] for c in range(8)], float(D), 1.0, 0)
                    for c in range(8):
                        P.op("dve", I("scalar_tensor_tensor", out=h2[:, c, :], in0=xb[:, c, :], scalar=gcol(16 + l * 8 + c),
                                      in1=rstd[:], op0=ALU.mult, op1=ALU.mult), reads=[r_xb, r_g, r_rstd], writes=[r_h2])

                load_bx(0)
                norm2_stage(0)
                for b in range(NBQ):
                    xb, r_xb, h2, r_h2 = xbb[b % 2], r_xbb[b % 2], h2b[b % 2], r_h2b[b % 2]
                    sq, r_sq = h2, r_h2
                    if b + 1 < NBQ:
                        load_bx(b + 1)
                    for half in range(2):
                        if half == 1 and b + 1 < NBQ:
                            norm2_stage(b + 1)
                        for f in range(16):
                            fc = half * 16 + f
                            pb = 1 + pcnt % 3
                            pcnt += 1
                            ri = rcnt % 2
                            rcnt += 1
                            for kc in range(8):
                                P.op("pe", I("matmul", ps_banks[pb][:], lhsT=w1[:, kc, fc * 128:(fc + 1) * 128], rhs=h2[:, kc, :],
                                             start=(kc == 0), stop=(kc == 7)), reads=[r_w1, r_h2], writes=[ps_res[pb]], inc=(kc == 7))
                            P.op("act", I("activation", out=rr[ri][:], in_=ps_banks[pb][:], func=AF.Relu),
                                 reads=[ps_res[pb]], writes=[r_rr[ri]])
                            P.op("dve", I("tensor_tensor", out=uu[:, f, :], in0=rr[ri][:], in1=rr[ri][:], op=ALU.mult),
                                 reads=[r_rr[ri]], writes=[r_uu])
                        for oc_ in range(8):
                            pb = 4 + pcnt % 4
                            pcnt += 1
                            for f in range(16):
                                fc = half * 16 + f
                                P.op("pe", I("matmul", ps_banks[pb][:], lhsT=w2[:, fc, oc_ * 128:(oc_ + 1) * 128], rhs=uu[:, f, :],
                                             start=(f == 0), stop=(f == 15)), reads=[r_w2, r_uu], writes=[ps_res[pb]], inc=(f == 15))
                            P.op("dve", I("tensor_tensor", out=xb[:, oc_, :], in0=xb[:, oc_, :], in1=ps_banks[pb][:], op=ALU.add),
                                 reads=[r_xb, ps_res[pb]], writes=[r_xb])
                    if not last_layer:
                        P.dma("sp", I("dma_start", out=fm(x1_loc)[:, :, b * 512:(b + 1) * 512], in_=xb[:]),
                              "st_x1", reads=[r_xb], store=True)
                    else:
                        P.op("act", I("activation", out=sq[:], in_=xb[:], func=AF.Square), reads=[r_xb], writes=[r_sq])
                        rms_rstd((srt, rstd, r_srt, r_rstd, r_sq), [sq[:, c, :] for c in range(8)], float(D), 1.0, 0)
                        for c in range(8):
                            P.op("dve", I("scalar_tensor_tensor", out=xb[:, c, :], in0=xb[:, c, :], scalar=gcol(32 + c),
                                          in1=rstd[:], op0=ALU.mult, op1=ALU.mult), reads=[r_xb, r_g, r_rstd], writes=[r_xb])
                        P.dma("sp", I("dma_start", out=fm(yT)[:, :, b * 512:(b + 1) * 512], in_=xb[:]),
                              "st_y", reads=[r_xb], store=True)
            wst.close()
            P.dram_barrier()

        with nc.Block() as block:
            P.play(block)
    return nc


def _host_tables(T, pos):
    half = 32
    inv_freq = (10000.0 ** (-np.arange(half, dtype=np.float32) / half)).astype(np.float32)
    ang = pos.astype(np.float32)[None, :] * inv_freq[:, None]
    cos = np.cos(ang).astype(np.float32)
    sin = np.sin(ang).astype(np.float32)
    cosT = np.tile(cos, (4, 1))
    sinT = np.tile(sin, (4, 1))
    return np.ascontiguousarray(cosT), np.ascontiguousarray(sinT)


def _perm_matrix():
    Pm = np.zeros((128, 128), np.float32)
    for blk in range(2):
        for d in range(64):
            m = blk * 64 + d
            if d < 32:
                Pm[blk * 64 + d + 32, m] = -1.0
            else:
                Pm[blk * 64 + d - 32, m] = 1.0
    return Pm


def _cmask():
    cm = np.zeros((128, 20, 512), np.float32)
    kk = np.arange(128)[:, None]
    qq = np.arange(512)[None, :]
    for j in range(20):
        delta = 128 * j - 1024 + kk - qq
        a = np.abs(delta)
        c = (a <= 64).astype(np.float32)
        c += ((delta % 4 == 0) & (a <= 256)).astype(np.float32)
        c += ((delta % 16 == 0) & (a <= 1024)).astype(np.float32)
        cm[:, j, :] = c
    return np.ascontiguousarray(cm.reshape(128, 20 * 512))


def _bt3_table(T, G, quarter, is_prompt):
    NB, NKT = T // 512, T // 128
    cols = []
    for mode_blocks, base in ((NB // G, quarter * (NB // G)),):
        for qb in range(mode_blocks):
            gqb = base + qb
            for j in range(20):
                kt = 4 * gqb - 8 + j
                ok = 0 <= kt < NKT
                if ok and is_prompt and ((kt < NKT // 2) != (gqb < NB // 2)):
                    ok = False
                cols.append(0.0 if ok else NEG)
    t = np.asarray(cols, np.float32)
    return np.ascontiguousarray(np.broadcast_to(t[None, :], (128, t.size)))


def _pack_gains(norm1_g, norm2_g, final_norm_g, diff_norm_g, dil_norm_g, cross_bias, kh0=0.0, kh1=0.0,
                keep=(1.0, 1.0, 1.0, 1.0), neg_lcorr=0.0):
    g = np.zeros((128, 60), np.float32)
    g[:, 54:58] = np.asarray(keep, np.float32)[None, :]
    g[:, 58] = neg_lcorr
    for l in range(DEPTH):
        g[:, l * 8:(l + 1) * 8] = norm1_g[l].reshape(8, 128).T
        g[:, 16 + l * 8:16 + (l + 1) * 8] = norm2_g[l].reshape(8, 128).T
        g[:, 40 + l] = diff_norm_g[l]
        g[:, 42 + l * 4:42 + (l + 1) * 4] = dil_norm_g[l].reshape(4, 128).T
    g[:, 32:40] = final_norm_g.reshape(8, 128).T
    g[:, 50] = 0.0
    g[:, 51] = cross_bias
    g[:, 52] = kh0
    g[:, 53] = kh1
    return g


_NC_CACHE = {}


def run_cores(T, seqs, weights, n_cores=8, G=4):
    if T not in _NC_CACHE:
        _NC_CACHE[T] = build_nc(T)
    nc = _NC_CACHE[T]
    (norm1_g, w_in, lq1, lk1, lq2, lk2, diff_norm_g, dil_norm_g, w_out, norm2_g, w_ff1, w_ff2, final_norm_g) = weights
    lamv = np.concatenate([np.asarray(a, np.float32).reshape(-1) for a in (lq1, lk1, lq2, lk2)])
    lamv = np.ascontiguousarray(np.broadcast_to(lamv[None, :], (128, lamv.size)))
    common = dict(w_in=np.ascontiguousarray(w_in, np.float32), w_out=np.ascontiguousarray(w_out, np.float32),
                  w_ff1=np.ascontiguousarray(w_ff1, np.float32), w_ff2=np.ascontiguousarray(w_ff2, np.float32),
                  lamv=lamv, perm=_perm_matrix(), cmask=_cmask())
    per_seq = []
    for x, pos, is_prompt in seqs:
        cosT, sinT = _host_tables(T, pos)
        per_seq.append((np.ascontiguousarray(np.asarray(x, np.float32).T), cosT, sinT, is_prompt))
    in_maps = []
    for c in range(n_cores):
        xT, cosT, sinT, is_prompt = per_seq[(c // G) % len(per_seq)]
        quarter = c % G
        m = dict(common)
        TQ = T // G
        m["xT"] = np.ascontiguousarray(xT[:, quarter * TQ:(quarter + 1) * TQ])
        m["cosL"] = np.ascontiguousarray(cosT[:, quarter * TQ:(quarter + 1) * TQ])
        m["sinL"] = np.ascontiguousarray(sinT[:, quarter * TQ:(quarter + 1) * TQ])
        kh0 = NEG if (is_prompt and quarter >= G // 2) else 0.0
        kh1 = NEG if (is_prompt and quarter < G // 2) else 0.0
        m["gains"] = _pack_gains(np.asarray(norm1_g), np.asarray(norm2_g), np.asarray(final_norm_g),
                                 np.asarray(diff_norm_g), np.asarray(dil_norm_g), NEG if is_prompt else 0.0, kh0, kh1,
                                 keep=[1.0 if (not is_prompt or ((r < G // 2) == (quarter < G // 2))) else 0.0 for r in range(G)],
                                 neg_lcorr=(-(T // 2) if is_prompt else 0.0))
        m["bt3"] = _bt3_table(T, G, quarter, is_prompt)
        in_maps.append(m)
    res = run_bass_kernel_spmd(nc, in_maps, core_ids=list(range(n_cores)))
    if os.environ.get("KDBG", ""):
        return [res.results[c] for c in range(n_cores)]
    outs = []
    for s in range(len(seqs)):
        yT = np.concatenate([res.results[s * G + q]["yT"] for q in range(G)], axis=1)
        outs.append(np.ascontiguousarray(yT.T))
    return outs


def kernel(x_prompt, x_sample, norm1_g, w_in, lambda_q1, lambda_k1, lambda_q2, lambda_k2,
           diff_norm_g, dil_norm_g, w_out, norm2_g, w_ff1, w_ff2, final_norm_g):
    x_prompt = np.asarray(x_prompt, np.float32)
    x_sample = np.asarray(x_sample, np.float32)
    B, S, _ = x_prompt.shape
    T = x_sample.shape[1]
    assert B * S == T and x_sample.shape[0] == 1
    weights = tuple(np.asarray(a, np.float32) for a in (
        norm1_g, w_in, lambda_q1, lambda_k1, lambda_q2, lambda_k2, diff_norm_g, dil_norm_g, w_out, norm2_g,
        w_ff1, w_ff2, final_norm_g))
    seqs = [
        (x_sample[0], np.arange(T), False),
        (x_prompt.reshape(T, D), np.concatenate([np.arange(S), np.arange(S)]), True),
    ]
    ys, yp = run_cores(T, seqs, weights)
    return (yp.reshape(B, S, D).astype(np.float32), ys.reshape(1, T, D).astype(np.float32))
```

```python
import math
import os
from contextlib import ExitStack

import numpy as np
import concourse.bass as bass
import concourse.mybir as mybir
from concourse.bass_utils import run_bass_kernel_spmd

F32 = mybir.dt.float32
BF16 = mybir.dt.bfloat16
AF = mybir.ActivationFunctionType
ALU = mybir.AluOpType

D = 1024
DEPTH = 2
D_IN = 3072
D_FF = 4096
EPS = 1e-5
NEG = -30000.0
ENGS = ("pe", "act", "dve", "pool", "sp")


class Tok:
    __slots__ = ("sem", "val")

    def __init__(self, sem=None, val=None):
        self.sem, self.val = sem, val


class Res:
    __slots__ = ("name", "w", "w_eng", "rs")

    def __init__(self, name):
        self.name, self.w, self.w_eng, self.rs = name, None, None, {}


class Prog:
    def __init__(self, nc, stack):
        self.nc, self.stack = nc, stack
        self.q = {e: [] for e in ENGS}
        self.sems = {}
        self.waited = {e: {} for e in ENGS}
        self.pending = {e: [] for e in ENGS}
        self.epoch = {e: 0 for e in ENGS}
        self.store_sems = set()

    def sem(self, name):
        if name not in self.sems:
            h = self.stack.enter_context(self.nc.semaphore(name))
            self.sems[name] = [h, 0]
        return self.sems[name]

    def new_epoch(self):
        self.full_barrier()
        for e in ENGS:
            self.epoch[e] += 1

    def _wait(self, eng, tok):
        if tok is None:
            return
        assert tok.val is not None, "unresolved lazy token"
        w = self.waited[eng]
        if w.get(tok.sem, 0) >= tok.val:
            return
        w[tok.sem] = tok.val
        h, v = self.sems[tok.sem][0], tok.val
        self.q[eng].append(I("wait_ge", h, v))

    def _deps(self, eng, reads, writes):
        for r in reads:
            if r.w is not None and not (eng == "pe" and r.w_eng == "pe"):
                self._wait(eng, r.w)
        for r in writes:
            if r.w is not None and r.w_eng != eng:
                self._wait(eng, r.w)
            for e2, t in r.rs.items():
                if e2 != eng:
                    self._wait(eng, t)

    def op(self, eng, fn, reads=(), writes=(), inc=True):
        self._deps(eng, reads, writes)
        tok = Tok()
        if inc:
            name = f"{eng}_{self.epoch[eng]}"
            s = self.sem(name)
            s[1] += 1
            tok.sem, tok.val = name, s[1]
            h = s[0]
            self.q[eng].append(lambda e, h=h, fn=fn: fn(e).then_inc(h, 1))
            for t in self.pending[eng]:
                t.sem, t.val = tok.sem, tok.val
            self.pending[eng] = []
        else:
            self.pending[eng].append(tok)
            self.q[eng].append(lambda e, fn=fn: fn(e))
        for r in reads:
            r.rs[eng] = tok
        for r in writes:
            r.w, r.w_eng, r.rs = tok, eng, {}
        return tok

    def dma(self, qeng, fn, dsem, reads=(), writes=(), store=False):
        self._deps(qeng, reads, writes)
        s = self.sem(dsem)
        s[1] += 16
        tok = Tok(dsem, s[1])
        h = s[0]
        self.q[qeng].append(lambda e, h=h, fn=fn: fn(e).then_inc(h, 16))
        key = "dma:" + dsem
        for r in reads:
            r.rs[key] = tok
        for r in writes:
            r.w, r.w_eng, r.rs = tok, key, {}
        if store:
            self.store_sems.add(dsem)
        return tok

    def collective(self, in_ap, out_ap, groups):
        s = self.sem("cc")
        s[1] += 1
        h, v = s[0], s[1]

        def f(e, h=h, v=v):
            e.collective_compute("AllGather", mybir.AluOpType.bypass, replica_groups=groups,
                                 ins=[in_ap], outs=[out_ap]).then_inc(h)
            e.wait_ge(h, v)

        self.q["pool"].append(f)
        self.waited["pool"]["cc"] = v
        self.store_sems.add("cc")

    def dram_barrier(self, engines=("sp", "pool")):
        for name in sorted(self.store_sems):
            tok = Tok(name, self.sems[name][1])
            for e in engines:
                self._wait(e, tok)

    def full_barrier(self):
        for e in ENGS:
            assert not self.pending[e], f"unresolved lazy tokens on {e} at barrier"
        for name in sorted(self.sems):
            cnt = self.sems[name][1]
            if cnt == 0:
                continue
            tok = Tok(name, cnt)
            for e in ENGS:
                self._wait(e, tok)

    def play(self, block):
        q = self.q

        @block.tensor
        def _(e):
            for f in q["pe"]:
                f(e)

        @block.scalar
        def _(e):
            for f in q["act"]:
                f(e)

        @block.vector
        def _(e):
            for f in q["dve"]:
                f(e)

        @block.gpsimd
        def _(e):
            for f in q["pool"]:
                f(e)

        @block.sync
        def _(e):
            for f in q["sp"]:
                f(e)


def I(name, *a, **k):
    return lambda e: getattr(e, name)(*a, **k)


def lambda_init(l):
    return 0.8 - 0.6 * math.exp(-0.3 * l)


def build_nc(T):
    NB = T // 512
    NKT = T // 128
    HALF_KT = NKT // 2
    HALF_QB = NB // 2
    G = 4
    TQ_LAST = T // G
    PADC = 1024
    nc = bass.Bass("TRN2", target_bir_lowering=False)

    def din(name, shape, dt=F32):
        return nc.dram_tensor(name, list(shape), dt, kind="ExternalInput").ap()

    xT = din("xT", [D, T // 4])
    w_in = din("w_in", [DEPTH, D, D_IN])
    w_out = din("w_out", [DEPTH, D, D])
    w_ff1 = din("w_ff1", [DEPTH, D, D_FF])
    w_ff2 = din("w_ff2", [DEPTH, D_FF, D])
    gains = din("gains", [128, 60])
    lamv = din("lamv", [128, 8 * 64])
    perm = din("perm", [128, 128])
    cmask_in = din("cmask", [128, 20 * 512])
    cosL = din("cosL", [128, T // 4])
    sinL = din("sinL", [128, T // 4])
    NBT3 = (NB // G) * 20
    bt3_in = din("bt3", [128, NBT3])
    yT = nc.dram_tensor("yT", [D, TQ_LAST], F32, kind="ExternalOutput").ap()

    dbg = os.environ.get("KDBG", "")

    def scr(name, shape, dt):
        if dbg:
            return nc.dram_tensor(name, list(shape), dt, kind="ExternalOutput").ap()
        return nc.dram_tensor(name, list(shape), dt).ap()

    TQ = TQ_LAST
    NBQ = NB // G
    CH = 1024
    NCH = TQ // CH
    assert NCH >= 1 and TQ % CH == 0
    x1_loc = scr("x1_loc", [D, TQ], F32)
    xm_t = scr("xm_loc", [D, TQ], F32)
    mix_t = scr("mix_loc", [D, TQ], F32)
    q_d = scr("qd_loc", [512, TQ], BF16)
    q_s = scr("qs_loc", [512, TQ], BF16)
    ks_loc = scr("ks_loc", [512, TQ + 2 * PADC], BF16)
    vs_loc = scr("vs_loc", [TQ + 2 * PADC, 512], BF16)

    def cbuf(name, rows):
        return nc.dram_tensor(name, [rows, 1024], BF16).ap()

    kin_d = [cbuf(f"kin_d{j}", 512) for j in range(NCH)]
    kin_s = [cbuf(f"kin_s{j}", 512) for j in range(NCH)]
    vin_d = [cbuf(f"vin_d{j}", 512) for j in range(NCH)]
    vin_s = [cbuf(f"vin_s{j}", 512) for j in range(NCH)]
    kg_d = [cbuf(f"kg_d{j}", G * 512) for j in range(NCH)]
    kg_s = [cbuf(f"kg_s{j}", G * 512) for j in range(NCH)]
    vg_d = [cbuf(f"vg_d{j}", G * 512) for j in range(NCH)]
    vg_s = [cbuf(f"vg_s{j}", G * 512) for j in range(NCH)]
    GROUPS = [list(range(g0, g0 + G)) for g0 in range(0, 8, G)]

    def vview(ap):
        return ap.rearrange("r (two c) -> (r two) c", two=2)

    def fm(ap):
        return ap.rearrange("(c p) t -> p c t", p=128)

    state = {}

    with ExitStack() as gstack:
        P = Prog(nc, gstack)

        def _prologue(e):
            pid = nc.partition_id(engines=[mybir.EngineType.SP])
            state["q0"] = (pid % G) * TQ_LAST
            state["left"] = (pid + (G - 1)) % G
            state["right"] = (pid + 1) % G

        P.q["sp"].append(_prologue)

        uid = [0]

        def sb(stack, name, shape, dt):
            uid[0] += 1
            return stack.enter_context(nc.sbuf_tensor(f"{name}_u{uid[0]}", list(shape), dt))

        def DYN(build):
            return lambda e: build(e, state["q0"])

        ps_banks = [gstack.enter_context(nc.psum_tensor(f"ps{i}", [128, 512], F32)) for i in range(8)]
        ps_res = [Res(f"ps{i}") for i in range(8)]

        ones = sb(gstack, "ones", [128, 128], BF16)
        r_ones = Res("ones")
        perm_bf = sb(gstack, "perm_bf", [128, 128], BF16)
        r_perm = Res("perm")
        g_sb = sb(gstack, "g_sb", [128, 60], F32)
        r_g = Res("g")
        bt3 = sb(gstack, "bt3", [128, NBT3], F32)
        r_bt3 = Res("bt3")
        lam_sb = sb(gstack, "lam_sb", [128, 8 * 64], F32)
        r_lam = Res("lam")
        lam_tmp = sb(gstack, "lam_tmp", [128, 64], F32)
        lam_acc = sb(gstack, "lam_acc", [128, 8], F32)
        r_lamacc = Res("lamacc")

        P.op("pool", I("memset", ones[:], 1.0), writes=[r_ones])
        P.dma("pool", I("dma_start", out=perm_bf[:], in_=perm), "ld_c0", writes=[r_perm])
        P.dma("sp", I("dma_start", out=g_sb[:], in_=gains), "ld_c1", writes=[r_g])
        P.dma("sp", I("dma_start", out=lam_sb[:], in_=lamv), "ld_c2", writes=[r_lam])
        P.dma("sp", I("dma_start", out=bt3[:], in_=bt3_in), "ld_c3", writes=[r_bt3])
        r_lt = Res("lam_tmp")
        for l in range(DEPTH):
            for m in range(2):
                qa = lam_sb[:, ((2 * m) * 2 + l) * 64:((2 * m) * 2 + l + 1) * 64]
                ka = lam_sb[:, ((2 * m + 1) * 2 + l) * 64:((2 * m + 1) * 2 + l + 1) * 64]
                col = lam_acc[:, 2 * l + m:2 * l + m + 1]
                P.op("dve", I("tensor_tensor", out=lam_tmp[:], in0=qa, in1=ka, op=ALU.mult), reads=[r_lam], writes=[r_lt])
                P.op("dve", I("tensor_reduce", out=col, in_=lam_tmp[:], op=ALU.add, axis=mybir.AxisListType.X),
                     reads=[r_lt], writes=[r_lamacc])
            c0 = lam_acc[:, 2 * l:2 * l + 2]
            P.op("act", I("activation", out=c0, in_=c0, func=AF.Exp), reads=[r_lamacc], writes=[r_lamacc])
            nl = lam_acc[:, 4 + l:5 + l]
            P.op("dve", I("scalar_tensor_tensor", out=nl, in0=lam_acc[:, 2 * l + 1:2 * l + 2], scalar=-lambda_init(l),
                          in1=lam_acc[:, 2 * l:2 * l + 1], op0=ALU.add, op1=ALU.subtract),
                 reads=[r_lamacc], writes=[r_lamacc])

        def gcol(i):
            return g_sb[:, i:i + 1]

        def rms_rstd(bufs, chunks, denom, scale_extra, ps_i):
            srt, rstd, r_srt, r_rstd, r_sq = bufs
            n = len(chunks)
            for i, ch in enumerate(chunks):
                P.op("pe", I("matmul", ps_banks[ps_i][:], lhsT=ones[:], rhs=ch, start=(i == 0), stop=(i == n - 1)),
                     reads=[r_ones, r_sq], writes=[ps_res[ps_i]], inc=(i == n - 1))
            s2 = 1.0 / (scale_extra * scale_extra)
            P.op("act", I("activation", out=srt[:], in_=ps_banks[ps_i][:], func=AF.Sqrt, scale=s2 / denom, bias=EPS * s2),
                 reads=[ps_res[ps_i]], writes=[r_srt])
            P.op("dve", I("reciprocal", out=rstd[:], in_=srt[:]), reads=[r_srt], writes=[r_rstd])

        def xcols(ap3, b, dyn):
            if not dyn:
                return lambda q0: ap3[:, :, b * 512:(b + 1) * 512]
            return lambda q0: ap3[:, :, bass.ds(q0 + b * 512, 512)]

        def cols2(ap2, b, dyn):
            if not dyn:
                return lambda q0: ap2[:, b * 512:(b + 1) * 512]
            return lambda q0: ap2[:, bass.ds(q0 + b * 512, 512)]

        for l in range(DEPTH):
            if dbg and l == 1 and dbg != "2":
                break
            last_layer = (l == DEPTH - 1)
            x_src = xT if l == 0 else x1_loc
            bt3_base = 0

            P.new_epoch()
            with ExitStack() as st:
                w = sb(st, "w_in_sb", [128, 8, D_IN], BF16)
                r_w = Res("w_in")
                for kc in range(8):
                    for hh in range(2):
                        P.dma("pool", I("dma_start", out=w[:, kc, hh * 1536:(hh + 1) * 1536],
                                        in_=w_in[l, kc * 128:(kc + 1) * 128, hh * 1536:(hh + 1) * 1536]),
                              f"ld_w{l}", writes=[r_w])
                xb = [sb(st, f"xb{i}", [128, 8, 512], F32) for i in range(2)]
                r_xb = [Res(f"xb{i}") for i in range(2)]
                cs = [sb(st, f"cs{i}", [128, 2, 512], F32) for i in range(2)]
                r_cs = [Res(f"cs{i}") for i in range(2)]
                sq = sb(st, "sq", [128, 8, 512], BF16)
                r_sq = Res("sq")
                hTb = [sb(st, f"hT{i}", [128, 8, 512], BF16) for i in range(2)]
                r_hb = [Res(f"hT{i}") for i in range(2)]
                srt = sb(st, "srt", [128, 512], F32)
                rstd = sb(st, "rstd", [128, 512], F32)
                r_srt, r_rstd = Res("srt"), Res("rstd")
                qb = [sb(st, f"qb{i}", [128, 512], BF16) for i in range(2)]
                r_qb = [Res(f"qb{i}") for i in range(2)]
                t1 = [sb(st, f"t1_{i}", [128, 512], F32) for i in range(2)]
                r_t1 = [Res(f"t1_{i}") for i in range(2)]
                t2 = [sb(st, f"t2_{i}", [128, 512], F32) for i in range(2)]
                r_t2 = [Res(f"t2_{i}") for i in range(2)]
                qo = [sb(st, f"qo{i}", [128, 512], BF16) for i in range(3)]
                r_qo = [Res(f"qo{i}") for i in range(3)]
                vo = [sb(st, f"vo{i}", [128, 512], BF16) for i in range(2)]
                r_vo = [Res(f"vo{i}") for i in range(2)]
                cnts = {"ld": 0, "qk": 0, "v": 0}

                def load_blk(b, dyn):
                    i = cnts["ld"] % 2
                    cnts["ld"] += 1
                    xs_, cc_, ss_ = (x_src, cosL, sinL)
                    P.dma("sp", I("dma_start", out=xb[i][:], in_=fm(xs_)[:, :, b * 512:(b + 1) * 512]),
                          f"ld_xb{i}", writes=[r_xb[i]])
                    P.dma("sp", I("dma_start", out=cs[i][:, 0, :], in_=cc_[:, b * 512:(b + 1) * 512]),
                          f"ld_cs{i}", writes=[r_cs[i]])
                    P.dma("sp", I("dma_start", out=cs[i][:, 1, :], in_=ss_[:, b * 512:(b + 1) * 512]),
                          f"ld_cs{i}", writes=[r_cs[i]])
                    return i

                def norm_stage(i, hsel):
                    hT, r_h = hTb[hsel], r_hb[hsel]
                    P.op("act", I("activation", out=sq[:], in_=xb[i][:], func=AF.Square), reads=[r_xb[i]], writes=[r_sq])
                    rms_rstd((srt, rstd, r_srt, r_rstd, r_sq), [sq[:, c, :] for c in range(8)], float(D), 1.0, 0)
                    for c in range(8):
                        P.op("dve", I("scalar_tensor_tensor", out=hT[:, c, :], in0=xb[i][:, c, :], scalar=gcol(l * 8 + c),
                                      in1=rstd[:], op0=ALU.mult, op1=ALU.mult),
                             reads=[r_xb[i], r_g, r_rstd], writes=[r_h])

                def p1_block(i, b, qk_specs, do_v, hsel, mid_hook):
                    hT, r_h = hTb[hsel], r_hb[hsel]
                    nchunk = 0
                    for col0, dst_fn in qk_specs:
                        for j in range(4):
                            nchunk += 1
                            if nchunk == 9 and mid_hook is not None:
                                mid_hook()
                            cnt = cnts["qk"]
                            cnts["qk"] += 1
                            pj, pr, bi, oi = 1 + cnt % 2, 3 + cnt % 2, cnt % 2, cnt % 3
                            cbase = col0 + j * 128
                            for kc in range(8):
                                P.op("pe", I("matmul", ps_banks[pj][:], lhsT=w[:, kc, cbase:cbase + 128], rhs=hT[:, kc, :],
                                             start=(kc == 0), stop=(kc == 7)),
                                     reads=[r_w, r_h], writes=[ps_res[pj]], inc=(kc == 7))
                            P.op("act", I("copy", out=qb[bi][:], in_=ps_banks[pj][:]), reads=[ps_res[pj]], writes=[r_qb[bi]])
                            P.op("pe", I("matmul", ps_banks[pr][:], lhsT=perm_bf[:], rhs=qb[bi][:], start=True, stop=True),
                                 reads=[r_perm, r_qb[bi]], writes=[ps_res[pr]])
                            P.op("dve", I("tensor_tensor", out=t1[bi][:], in0=qb[bi][:], in1=cs[i][:, 0, :], op=ALU.mult),
                                 reads=[r_qb[bi], r_cs[i]], writes=[r_t1[bi]])
                            P.op("dve", I("tensor_tensor", out=t2[bi][:], in0=ps_banks[pr][:], in1=cs[i][:, 1, :], op=ALU.mult),
                                 reads=[ps_res[pr], r_cs[i]], writes=[r_t2[bi]])
                            P.op("dve", I("tensor_tensor", out=qo[oi][:], in0=t1[bi][:], in1=t2[bi][:], op=ALU.add),
                                 reads=[r_t1[bi], r_t2[bi]], writes=[r_qo[oi]])
                            P.dma("sp", I("dma_start", out=dst_fn(j, b), in_=qo[oi][:]),
                                  f"st_qk{oi}", reads=[r_qo[oi]], store=True)
                    if do_v:
                        for col0, vins in ((1024, vin_d), (2560, vin_s)):
                            for sub in range(4):
                                pv, vi = 5 + cnts["v"] % 2, cnts["v"] % 2
                                cnts["v"] += 1
                                for kc in range(8):
                                    P.op("pe", I("matmul", ps_banks[pv][:], lhsT=hT[:, kc, sub * 128:(sub + 1) * 128],
                                                 rhs=w[:, kc, col0:col0 + 512], start=(kc == 0), stop=(kc == 7)),
                                         reads=[r_w, r_h], writes=[ps_res[pv]], inc=(kc == 7))
                                P.op("act", I("copy", out=vo[vi][:], in_=ps_banks[pv][:]), reads=[ps_res[pv]], writes=[r_vo[vi]])
                                r0 = (b * 512) % CH + sub * 128
                                P.dma("sp", I("dma_start", out=vview(vins[(b * 512) // CH])[r0:r0 + 128, :], in_=vo[vi][:]),
                                      f"st_v{vi}", reads=[r_vo[vi]], store=True)

                def q_dst(t):
                    return lambda j, b: t[j * 128:(j + 1) * 128, b * 512:(b + 1) * 512]

                def k_dst(chunks):
                    return lambda j, b: chunks[(b * 512) // CH][j * 128:(j + 1) * 128, (b * 512) % CH:(b * 512) % CH + 512]

                specs = [(0, q_dst(q_d)), (1536, q_dst(q_s)), (512, k_dst(kin_d)), (2048, k_dst(kin_s))]
                nxt = load_blk(0, True)
                norm_stage(nxt, 0)
                for b in range(NBQ):
                    i = nxt
                    hook = None
                    if b + 1 < NBQ:
                        nxt = load_blk(b + 1, True)
                        hook = (lambda nxt=nxt, b=b: norm_stage(nxt, (b + 1) % 2))
                    p1_block(i, b, specs, True, b % 2, hook)
                    if ((b + 1) * 512) % CH == 0:
                        j = ((b + 1) * 512) // CH - 1
                        P.dram_barrier(engines=("pool",))
                        for ins_, outs_ in ((kin_d, kg_d), (vin_d, vg_d)):
                            P.collective(ins_[j], outs_[j], GROUPS)
                        if j in (0, NCH - 1):
                            for ins_, outs_ in ((kin_s, kg_s), (vin_s, vg_s)):
                                P.collective(ins_[j], outs_[j], GROUPS)
            P.dram_barrier()
            P.dram_barrier()

            P.new_epoch()
            with ExitStack() as st:
                kTb = [sb(st, f"kT{i}", [128, T], BF16) for i in range(2)]
                vvb = [sb(st, f"vv{i}", [128, NKT, 128], BF16) for i in range(2)]
                r_kTb = [Res(f"kT{i}") for i in range(2)]
                r_vvb = [Res(f"vv{i}") for i in range(2)]

                def load_kv(h):
                    i = h % 2
                    for r in range(G):
                        for j in range(NCH):
                            t0 = r * TQ + j * CH
                            P.dma("sp", I("dma_start", out=kTb[i][:, t0:t0 + CH],
                                          in_=kg_d[j][r * 512 + h * 128:r * 512 + (h + 1) * 128, :]), f"ld_kT{i}", writes=[r_kTb[i]])
                            P.dma("sp", I("dma_start", out=vvb[i][:, t0 // 128:(t0 + CH) // 128, :],
                                          in_=vview(vg_d[j])[r * CH:(r + 1) * CH, h * 128:(h + 1) * 128].rearrange(
                                              "(k p) e -> p k e", p=128)), f"ld_vv{i}", writes=[r_vvb[i]])

                def mask_kv(h):
                    i = h % 2
                    for r in range(G):
                        P.op("pool", I("tensor_scalar", out=kTb[i][:, r * TQ:(r + 1) * TQ], in0=kTb[i][:, r * TQ:(r + 1) * TQ],
                                       scalar1=gcol(54 + r), scalar2=1.0, op0=ALU.mult, op1=ALU.mult),
                             reads=[r_kTb[i], r_g], writes=[r_kTb[i]])
                        P.op("pool", I("tensor_scalar", out=vvb[i][:, r * (TQ // 128):(r + 1) * (TQ // 128), :],
                                       in0=vvb[i][:, r * (TQ // 128):(r + 1) * (TQ // 128), :],
                                       scalar1=gcol(54 + r), scalar2=1.0, op0=ALU.mult, op1=ALU.mult),
                             reads=[r_vvb[i], r_g], writes=[r_vvb[i]])

                qA = [sb(st, f"qA{i}", [128, 512], BF16) for i in range(2)]
                qB = [sb(st, f"qB{i}", [128, 512], BF16) for i in range(2)]
                r_qA = [Res(f"qA{i}") for i in range(2)]
                r_qB = [Res(f"qB{i}") for i in range(2)]
                NPB = 8
                pt = [sb(st, f"pt{i}", [128, 512], BF16) for i in range(2 * NPB)]
                r_pt = [Res(f"pt{i}") for i in range(2 * NPB)]
                sab = [sb(st, f"sab{i}", [128, 512], BF16) for i in range(4)]
                scd = [sb(st, f"scd{i}", [128, 512], BF16) for i in range(4)]
                s4 = [sb(st, f"s4_{i}", [128, 512], BF16) for i in range(4)]
                r_sab = [Res(f"sab{i}") for i in range(4)]
                r_scd = [Res(f"scd{i}") for i in range(4)]
                r_s4 = [Res(f"s4_{i}") for i in range(4)]
                slot_of = [0, 0, 0, 0]
                q4cnt = 0
                oc = [sb(st, f"oc{i}", [128, 512], F32) for i in range(4)]
                r_oc = [Res(f"oc{i}") for i in range(4)]
                ob = [sb(st, f"ob{i}", [128, 512], F32) for i in range(2)]
                r_ob = [Res(f"ob{i}") for i in range(2)]
                for i in range(2):
                    P.op("pool", I("memset", qA[i][64:128, :], 0.0), writes=[r_qA[i]])
                    P.op("pool", I("memset", qB[i][0:64, :], 0.0), writes=[r_qB[i]])
                qcnt = 0
                ucnt = 0
                load_kv(0)
                mask_kv(0)
                for h in range(4):
                    kT, vv, r_kT, r_vv = kTb[h % 2], vvb[h % 2], r_kTb[h % 2], r_vvb[h % 2]

                    def load_q(qbi, h=h):
                        i = qbi % 2
                        P.dma("sp", I("dma_start", out=qA[i][0:64, :], in_=q_d[h * 128:h * 128 + 64, qbi * 512:(qbi + 1) * 512]),
                              f"ld_qA{i}", writes=[r_qA[i]])
                        P.dma("sp", I("dma_start", out=qB[i][64:128, :],
                                      in_=q_d[h * 128 + 64:h * 128 + 128, qbi * 512:(qbi + 1) * 512]),
                              f"ld_qB{i}", writes=[r_qB[i]])

                    load_q(0)
                    for qbi in range(NBQ):
                        qi = qbi % 2
                        if qbi + 1 < NBQ:
                            load_q(qbi + 1)
                        if qbi == 0 and h + 1 < 4:
                            load_kv(h + 1)
                        if qbi == NBQ - 1 and h + 1 < 4:
                            mask_kv(h + 1)

                        def qk(kt, qi=qi, kT=kT, r_kT=r_kT):
                            sb_ = (kt % 2) * 2
                            P.op("pe", I("matmul", ps_banks[sb_][:], lhsT=kT[:, kt * 128:(kt + 1) * 128], rhs=qA[qi][:],
                                         start=True, stop=True), reads=[r_kT, r_qA[qi]], writes=[ps_res[sb_]])
                            P.op("pe", I("matmul", ps_banks[sb_ + 1][:], lhsT=kT[:, kt * 128:(kt + 1) * 128], rhs=qB[qi][:],
                                         start=True, stop=True), reads=[r_kT, r_qB[qi]], writes=[ps_res[sb_ + 1]])

                        qk(0)
                        deferred = []
                        for kt in range(NKT):
                            if kt + 1 < NKT:
                                qk(kt + 1)
                            sb_ = (kt % 2) * 2
                            bias_ap = None
                            pi = (ucnt % NPB) * 2
                            ucnt += 1
                            slot_of[kt % 4] = pi
                            for m in range(2):
                                if bias_ap is not None:
                                    P.op("act", I("activation", out=pt[pi + m][:], in_=ps_banks[sb_ + m][:], func=AF.Exp,
                                                  bias=bias_ap, scale=0.125),
                                         reads=[ps_res[sb_ + m], r_g], writes=[r_pt[pi + m]])
                                else:
                                    P.op("act", I("activation", out=pt[pi + m][:], in_=ps_banks[sb_ + m][:], func=AF.Exp,
                                                  scale=0.125), reads=[ps_res[sb_ + m]], writes=[r_pt[pi + m]])
                            last = (kt == NKT - 1)
                            for m in range(2):
                                P.op("pe", I("matmul", ps_banks[4 + m][:], lhsT=vv[:, kt, :], rhs=pt[pi + m][:],
                                             start=(kt == 0), stop=last),
                                     reads=[r_vv, r_pt[pi + m]], writes=[ps_res[4 + m]], inc=last)
                            r4 = kt % 4
                            if r4 == 1:
                                for m in range(2):
                                    bi = 2 * (q4cnt % 2) + m
                                    P.op("dve", I("tensor_tensor", out=sab[bi][:], in0=pt[slot_of[0] + m][:],
                                                  in1=pt[slot_of[1] + m][:], op=ALU.add),
                                         reads=[r_pt[slot_of[0] + m], r_pt[slot_of[1] + m]], writes=[r_sab[bi]])
                            if r4 == 3:
                                for m in range(2):
                                    bi = 2 * (q4cnt % 2) + m
                                    P.op("pool", I("tensor_tensor", out=scd[bi][:], in0=pt[slot_of[2] + m][:],
                                                   in1=pt[slot_of[3] + m][:], op=ALU.add),
                                         reads=[r_pt[slot_of[2] + m], r_pt[slot_of[3] + m]], writes=[r_scd[bi]])
                                    P.op("dve", I("tensor_tensor", out=s4[bi][:], in0=sab[bi][:], in1=scd[bi][:], op=ALU.add),
                                         reads=[r_sab[bi], r_scd[bi]], writes=[r_s4[bi]])
                                    deferred.append((kt + 2, m, bi, kt == 3, last))
                                q4cnt += 1
                            while deferred and (deferred[0][0] <= kt or last):
                                _, m, bi, first_, last_ = deferred.pop(0)
                                P.op("pe", I("matmul", ps_banks[6 + m][:], lhsT=ones[:], rhs=s4[bi][:], start=first_, stop=last_),
                                     reads=[r_ones, r_s4[bi]], writes=[ps_res[6 + m]], inc=last_)
                        for a in range(2):
                            P.op("dve", I("tensor_copy", out=oc[a][:], in_=ps_banks[4 + a][:]),
                                 reads=[ps_res[4 + a]], writes=[r_oc[a]])
                        for a in range(2, 4):
                            P.op("dve", I("tensor_scalar", out=oc[a][:], in0=ps_banks[4 + a][:], scalar1=gcol(58), scalar2=None,
                                          op0=ALU.add), reads=[ps_res[4 + a], r_g], writes=[r_oc[a]])
                        for m in range(2):
                            P.op("dve", I("reciprocal", out=oc[2 + m][:], in_=oc[2 + m][:]),
                                 reads=[r_oc[2 + m]], writes=[r_oc[2 + m]])
                            P.op("dve", I("tensor_tensor", out=oc[m][:], in0=oc[m][:], in1=oc[2 + m][:], op=ALU.mult),
                                 reads=[r_oc[m], r_oc[2 + m]], writes=[r_oc[m]])
                        oi = qcnt % 2
                        qcnt += 1
                        P.op("dve", I("scalar_tensor_tensor", out=ob[oi][:], in0=oc[1][:], scalar=lam_acc[:, 4 + l:5 + l],
                                      in1=oc[0][:], op0=ALU.mult, op1=ALU.add),
                             reads=[r_oc[0], r_oc[1], r_lamacc], writes=[r_ob[oi]])
                        P.dma("sp", I("dma_start", out=mix_t[h * 128:(h + 1) * 128, qbi * 512:(qbi + 1) * 512], in_=ob[oi][:]),
                              f"st_mx{oi}", reads=[r_ob[oi]], store=True)

            P.new_epoch()
            with ExitStack() as st:
                WT = TQ + 2 * PADC
                WKT = WT // 128
                for j in range(NCH):
                    P.dma("sp", I("dma_start", out=ks_loc[:, PADC + j * CH:PADC + (j + 1) * CH], in_=kin_s[j]), "st_kloc", store=True)
                    P.dma("sp", I("dma_start", out=vs_loc[PADC + j * CH:PADC + (j + 1) * CH, :], in_=vview(vin_s[j])), "st_kloc", store=True)
                P.dma("sp", DYN(lambda e, q0: e.dma_start(out=ks_loc[:, 0:PADC], in_=kg_s[NCH - 1][bass.ds(state["left"] * 512, 512), :])),
                      "st_kloc", store=True)
                P.dma("sp", DYN(lambda e, q0: e.dma_start(out=ks_loc[:, PADC + TQ:], in_=kg_s[0][bass.ds(state["right"] * 512, 512), :])),
                      "st_kloc", store=True)
                P.dma("sp", DYN(lambda e, q0: e.dma_start(out=vs_loc[0:PADC, :], in_=vview(vg_s[NCH - 1])[bass.ds(state["left"] * CH, CH), :])),
                      "st_kloc", store=True)
                P.dma("sp", DYN(lambda e, q0: e.dma_start(out=vs_loc[PADC + TQ:, :], in_=vview(vg_s[0])[bass.ds(state["right"] * CH, CH), :])),
                      "st_kloc", store=True)
                P.dram_barrier()
                kT = sb(st, "kTs", [128, WT], BF16)
                vv = sb(st, "vvs", [128, WKT, 128], BF16)
                r_kT, r_vv = Res("kTs"), Res("vvs")
                cm = sb(st, "cm", [128, 20, 512], BF16)
                r_cm = Res("cm")
                for j in range(20):
                    P.dma("pool", I("dma_start", out=cm[:, j, :], in_=cmask_in[:, j * 512:(j + 1) * 512]), "ld_cm", writes=[r_cm])
                qA = [sb(st, f"sqA{i}", [128, 512], BF16) for i in range(2)]
                qB = [sb(st, f"sqB{i}", [128, 512], BF16) for i in range(2)]
                r_qA = [Res(f"sqA{i}") for i in range(2)]
                r_qB = [Res(f"sqB{i}") for i in range(2)]
                NPB = 6
                LOOK = 3
                pt = [sb(st, f"spt{i}", [128, 512], BF16) for i in range(NPB)]
                r_pt = [Res(f"spt{i}") for i in range(NPB)]
                pm = [sb(st, f"spm{i}", [128, 512], BF16) for i in range(NPB)]
                r_pm = [Res(f"spm{i}") for i in range(NPB)]
                oc = [sb(st, f"soc{i}", [64, 512], F32) for i in range(2)]
                r_oc = [Res(f"soc{i}") for i in range(2)]
                ob = [sb(st, f"sob{i}", [64, 512], F32) for i in range(2)]
                r_ob = [Res(f"sob{i}") for i in range(2)]
                for i in range(2):
                    P.op("pool", I("memset", qA[i][64:128, :], 0.0), writes=[r_qA[i]])
                    P.op("pool", I("memset", qB[i][0:64, :], 0.0), writes=[r_qB[i]])
                ucnt = 0
                qcnt = 0
                scnt = [0]
                for cp in range(4):
                    nparts = 4
                    for part in range(nparts):
                        c0, c1 = part * (WT // nparts), (part + 1) * (WT // nparts)
                        k0, k1 = part * (WKT // nparts), (part + 1) * (WKT // nparts)
                        ksrc, vsrc = ks_loc, vs_loc
                        P.dma("sp" if part % 2 == 0 else "pool",
                              I("dma_start", out=kT[:, c0:c1], in_=ksrc[cp * 128:(cp + 1) * 128, c0:c1]),
                              "ld_kTs", writes=[r_kT])
                        P.dma("pool" if part % 2 == 0 else "sp",
                              I("dma_start", out=vv[:, k0:k1, :],
                                in_=vsrc[k0 * 128:k1 * 128, cp * 128:(cp + 1) * 128].rearrange("(k p) e -> p k e", p=128)),
                              "ld_vvs", writes=[r_vv])

                    def load_q(qbi, cp=cp):
                        i = qbi % 2
                        P.dma("sp", I("dma_start", out=qA[i][0:64, :], in_=q_s[cp * 128:cp * 128 + 64, qbi * 512:(qbi + 1) * 512]),
                              f"ld_sqA{i}", writes=[r_qA[i]])
                        P.dma("sp", I("dma_start", out=qB[i][64:128, :],
                                      in_=q_s[cp * 128 + 64:cp * 128 + 128, qbi * 512:(qbi + 1) * 512]),
                              f"ld_sqB{i}", writes=[r_qB[i]])

                    load_q(0)
                    for qbi in range(NBQ):
                        qi = qbi % 2
                        if qbi + 1 < NBQ:
                            load_q(qbi + 1)
                        for hh in range(2):
                            qop, r_qop = (qA[qi], r_qA[qi]) if hh == 0 else (qB[qi], r_qB[qi])
                            po, pl = 4 + hh, 6 + hh
                            tiles = list(range(20))
                            sbank = {}

                            def crange(j):
                                return max(0, 128 * j - 2048), min(512, 128 * j + 128)

                            def qk(idx, tiles=tiles, qop=qop, r_qop=r_qop, qbi=qbi):
                                wt = qbi * 4 + tiles[idx]
                                c0, c1 = crange(tiles[idx])
                                sbk = scnt[0] % 4
                                scnt[0] += 1
                                P.op("pe", I("matmul", ps_banks[sbk][:, c0:c1], lhsT=kT[:, wt * 128:(wt + 1) * 128], rhs=qop[:, c0:c1],
                                             start=True, stop=True), reads=[r_kT, r_qop], writes=[ps_res[sbk]])
                                sbank[idx] = sbk

                            for a in range(min(LOOK, len(tiles))):
                                qk(a)
                            for idx, j in enumerate(tiles):
                                if idx + LOOK < len(tiles):
                                    qk(idx + LOOK)
                                sbk = sbank[idx]
                                wt = qbi * 4 + j
                                bcol = bt3[:, bt3_base + qbi * 20 + j:bt3_base + qbi * 20 + j + 1]
                                pi = ucnt % NPB
                                ucnt += 1
                                c0, c1 = crange(j)
                                P.op("act", I("activation", out=pt[pi][:, c0:c1], in_=ps_banks[sbk][:, c0:c1], func=AF.Exp, bias=bcol,
                                              scale=0.125), reads=[ps_res[sbk], r_bt3], writes=[r_pt[pi]])
                                P.op("dve", I("tensor_tensor", out=pm[pi][:, c0:c1], in0=pt[pi][:, c0:c1], in1=cm[:, j, c0:c1], op=ALU.mult),
                                     reads=[r_pt[pi], r_cm], writes=[r_pm[pi]])
                                first, last = idx == 0, idx == len(tiles) - 1
                                P.op("pe", I("matmul", ps_banks[po][0:64, c0:c1], lhsT=vv[:, wt, hh * 64:(hh + 1) * 64], rhs=pm[pi][:, c0:c1],
                                             start=first, stop=last, skip_group_check=True),
                                     reads=[r_vv, r_pm[pi]], writes=[ps_res[po]], inc=last)
                                P.op("pe", I("matmul", ps_banks[pl][0:64, c0:c1], lhsT=ones[:, 0:64], rhs=pm[pi][:, c0:c1],
                                             start=first, stop=last, skip_group_check=True),
                                     reads=[r_ones, r_pm[pi]], writes=[ps_res[pl]], inc=last)
                            ci = qcnt % 2
                            qcnt += 1
                            P.op("dve", I("reciprocal", out=oc[ci][:], in_=ps_banks[pl][0:64, :]), reads=[ps_res[pl]], writes=[r_oc[ci]])
                            P.op("dve", I("tensor_tensor", out=ob[ci][:], in0=ps_banks[po][0:64, :], in1=oc[ci][:], op=ALU.mult),
                                 reads=[ps_res[po], r_oc[ci]], writes=[r_ob[ci]])
                            row0 = 512 + (cp * 2 + hh) * 64
                            P.dma("sp", I("dma_start", out=mix_t[row0:row0 + 64, qbi * 512:(qbi + 1) * 512], in_=ob[ci][:]),
                                  f"st_ms{ci}", reads=[r_ob[ci]], store=True)
            P.dram_barrier()

            P.new_epoch()
            wst = ExitStack()
            w1 = sb(wst, "w1", [128, 8, D_FF], BF16)
            r_w1 = Res("w1")
            with ExitStack() as st:
                wo = sb(st, "wo", [128, 8, D], BF16)
                r_wo = Res("wo")
                for kc in range(8):
                    P.dma("pool", I("dma_start", out=wo[:, kc, :], in_=w_out[l, kc * 128:(kc + 1) * 128, :]), f"ld_wo{l}", writes=[r_wo])
                for kc in range(8):
                    for hh in range(2):
                        P.dma("pool", I("dma_start", out=w1[:, kc, hh * 2048:(hh + 1) * 2048],
                                        in_=w_ff1[l, kc * 128:(kc + 1) * 128, hh * 2048:(hh + 1) * 2048]), f"ld_w1{l}", writes=[r_w1])
                xb = [sb(st, f"axb{i}", [128, 8, 512], F32) for i in range(2)]
                r_xb = [Res(f"axb{i}") for i in range(2)]
                mr = [sb(st, f"mr{i}", [128, 8, 512], F32) for i in range(2)]
                r_mr = [Res(f"mr{i}") for i in range(2)]
                sq = sb(st, "asq", [128, 8, 512], BF16)
                r_sq = Res("asq")
                mx = sb(st, "amx", [128, 8, 512], BF16)
                r_mx = Res("amx")
                srt = [sb(st, f"asrt{i}", [128, 512], F32) for i in range(2)]
                rstd = [sb(st, f"arstd{i}", [128, 512], F32) for i in range(2)]
                r_srt = [Res(f"asrt{i}") for i in range(2)]
                r_rstd = [Res(f"arstd{i}") for i in range(2)]

                def load_blk(b):
                    i = b % 2
                    xs_ = x_src
                    P.dma("sp", I("dma_start", out=xb[i][:], in_=fm(xs_)[:, :, b * 512:(b + 1) * 512]), f"ld_axb{i}", writes=[r_xb[i]])
                    P.dma("pool", I("dma_start", out=mr[i][:], in_=fm(mix_t)[:, :, b * 512:(b + 1) * 512]), f"ld_mr{i}", writes=[r_mr[i]])

                load_blk(0)
                ncnt = 0
                pcnt = 0
                li = lambda_init(l)
                for b in range(NBQ):
                    i = b % 2
                    if b + 1 < NBQ:
                        load_blk(b + 1)
                    P.op("act", I("activation", out=sq[:], in_=mr[i][:], func=AF.Square), reads=[r_mr[i]], writes=[r_sq])
                    for c in range(4):
                        ni = ncnt % 2
                        ncnt += 1
                        rms_rstd((srt[ni], rstd[ni], r_srt[ni], r_rstd[ni], r_sq), [sq[:, c, :]], 128.0, 1.0 - li, ni)
                        P.op("dve", I("scalar_tensor_tensor", out=mx[:, c, :], in0=mr[i][:, c, :], scalar=gcol(40 + l),
                                      in1=rstd[ni][:], op0=ALU.mult, op1=ALU.mult),
                             reads=[r_mr[i], r_g, r_rstd[ni]], writes=[r_mx])
                    ni = ncnt % 2
                    ncnt += 1
                    rms_rstd((srt[ni], rstd[ni], r_srt[ni], r_rstd[ni], r_sq), [sq[:, c, :] for c in range(4, 8)], 512.0, 1.0, ni)
                    for c in range(4, 8):
                        P.op("dve", I("scalar_tensor_tensor", out=mx[:, c, :], in0=mr[i][:, c, :], scalar=gcol(42 + l * 4 + (c - 4)),
                                      in1=rstd[ni][:], op0=ALU.mult, op1=ALU.mult),
                             reads=[r_mr[i], r_g, r_rstd[ni]], writes=[r_mx])
                    for oc_ in range(8):
                        pb = 2 + pcnt % 4
                        pcnt += 1
                        for kc in range(8):
                            P.op("pe", I("matmul", ps_banks[pb][:], lhsT=wo[:, kc, oc_ * 128:(oc_ + 1) * 128], rhs=mx[:, kc, :],
                                         start=(kc == 0), stop=(kc == 7)), reads=[r_wo, r_mx], writes=[ps_res[pb]], inc=(kc == 7))
                        P.op("dve", I("tensor_tensor", out=xb[i][:, oc_, :], in0=xb[i][:, oc_, :], in1=ps_banks[pb][:], op=ALU.add),
                             reads=[r_xb[i], ps_res[pb]], writes=[r_xb[i]])
                    P.dma("sp", I("dma_start", out=fm(xm_t)[:, :, b * 512:(b + 1) * 512], in_=xb[i][:]),
                          f"st_xm{i}", reads=[r_xb[i]], store=True)
            P.dram_barrier()

            P.new_epoch()
            with ExitStack() as st:
                w2 = sb(st, "w2", [128, 32, D], BF16)
                r_w2 = Res("w2")
                for fc in range(32):
                    P.dma("pool", I("dma_start", out=w2[:, fc, :], in_=w_ff2[l, fc * 128:(fc + 1) * 128, :]), f"ld_w2{l}", writes=[r_w2])
                xbb = [sb(st, f"bxb{i}", [128, 8, 512], F32) for i in range(2)]
                r_xbb = [Res(f"bxb{i}") for i in range(2)]
                h2b = [sb(st, f"bh2{i}", [128, 8, 512], BF16) for i in range(2)]
                r_h2b = [Res(f"bh2{i}") for i in range(2)]
                uu = sb(st, "buu", [128, 16, 512], BF16)
                r_uu = Res("buu")
                rr = [sb(st, f"brr{i}", [128, 512], F32) for i in range(2)]
                r_rr = [Res(f"brr{i}") for i in range(2)]
                srt = sb(st, "bsrt", [128, 512], F32)
                rstd = sb(st, "brstd", [128, 512], F32)
                r_srt, r_rstd = Res("bsrt"), Res("brstd")
                pcnt = 0
                rcnt = 0
                def load_bx(b):
                    P.dma("sp", I("dma_start", out=xbb[b % 2][:], in_=fm(xm_t)[:, :, b * 512:(b + 1) * 512]),
                          f"ld_bxb{b % 2}", writes=[r_xbb[b % 2]])

                def norm2_stage(b):
                    xb, r_xb, h2, r_h2 = xbb[b % 2], r_xbb[b % 2], h2b[b % 2], r_h2b[b % 2]
                    P.op("act", I("activation", out=h2[:], in_=xb[:], func=AF.Square), reads=[r_xb], writes=[r_h2])
                    rms_rstd((srt, rstd, r_srt, r_rstd, r_h2), [h2[:, c, :] for c in range(8)], float(D), 1.0, 0)
                    for c in range(8):
                        P.op("dve", I("scalar_tensor_tensor", out=h2[:, c, :], in0=xb[:, c, :], scalar=gcol(16 + l * 8 + c),
                                      in1=rstd[:], op0=ALU.mult, op1=ALU.mult), reads=[r_xb, r_g, r_rstd], writes=[r_h2])

                load_bx(0)
                norm2_stage(0)
                for b in range(NBQ):
                    xb, r_xb, h2, r_h2 = xbb[b % 2], r_xbb[b % 2], h2b[b % 2], r_h2b[b % 2]
                    sq, r_sq = h2, r_h2
                    if b + 1 < NBQ:
                        load_bx(b + 1)
                    for half in range(2):
                        if half == 1 and b + 1 < NBQ:
                            norm2_stage(b + 1)
                        for f in range(16):
                            fc = half * 16 + f
                            pb = 1 + pcnt % 3
                            pcnt += 1
                            ri = rcnt % 2
                            rcnt += 1
                            for kc in range(8):
                                P.op("pe", I("matmul", ps_banks[pb][:], lhsT=w1[:, kc, fc * 128:(fc + 1) * 128], rhs=h2[:, kc, :],
                                             start=(kc == 0), stop=(kc == 7)), reads=[r_w1, r_h2], writes=[ps_res[pb]], inc=(kc == 7))
                            P.op("act", I("activation", out=rr[ri][:], in_=ps_banks[pb][:], func=AF.Relu),
                                 reads=[ps_res[pb]], writes=[r_rr[ri]])
                            P.op("dve", I("tensor_tensor", out=uu[:, f, :], in0=rr[ri][:], in1=rr[ri][:], op=ALU.mult),
                                 reads=[r_rr[ri]], writes=[r_uu])
                        for oc_ in range(8):
                            pb = 4 + pcnt % 4
                            pcnt += 1
                            for f in range(16):
                                fc = half * 16 + f
                                P.op("pe", I("matmul", ps_banks[pb][:], lhsT=w2[:, fc, oc_ * 128:(oc_ + 1) * 128], rhs=uu[:, f, :],
                                             start=(f == 0), stop=(f == 15)), reads=[r_w2, r_uu], writes=[ps_res[pb]], inc=(f == 15))
                            P.op("dve", I("tensor_tensor", out=xb[:, oc_, :], in0=xb[:, oc_, :], in1=ps_banks[pb][:], op=ALU.add),
                                 reads=[r_xb, ps_res[pb]], writes=[r_xb])
                    if not last_layer:
                        P.dma("sp", I("dma_start", out=fm(x1_loc)[:, :, b * 512:(b + 1) * 512], in_=xb[:]),
                              "st_x1", reads=[r_xb], store=True)
                    else:
                        P.op("act", I("activation", out=sq[:], in_=xb[:], func=AF.Square), reads=[r_xb], writes=[r_sq])
                        rms_rstd((srt, rstd, r_srt, r_rstd, r_sq), [sq[:, c, :] for c in range(8)], float(D), 1.0, 0)
                        for c in range(8):
                            P.op("dve", I("scalar_tensor_tensor", out=xb[:, c, :], in0=xb[:, c, :], scalar=gcol(32 + c),
                                          in1=rstd[:], op0=ALU.mult, op1=ALU.mult), reads=[r_xb, r_g, r_rstd], writes=[r_xb])
                        P.dma("sp", I("dma_start", out=fm(yT)[:, :, b * 512:(b + 1) * 512], in_=xb[:]),
                              "st_y", reads=[r_xb], store=True)
            wst.close()
            P.dram_barrier()

        with nc.Block() as block:
            P.play(block)
    return nc


def _host_tables(T, pos):
    half = 32
    inv_freq = (10000.0 ** (-np.arange(half, dtype=np.float32) / half)).astype(np.float32)
    ang = pos.astype(np.float32)[None, :] * inv_freq[:, None]
    cos = np.cos(ang).astype(np.float32)
    sin = np.sin(ang).astype(np.float32)
    cosT = np.tile(cos, (4, 1))
    sinT = np.tile(sin, (4, 1))
    return np.ascontiguousarray(cosT), np.ascontiguousarray(sinT)


def _perm_matrix():
    Pm = np.zeros((128, 128), np.float32)
    for blk in range(2):
        for d in range(64):
            m = blk * 64 + d
            if d < 32:
                Pm[blk * 64 + d + 32, m] = -1.0
            else:
                Pm[blk * 64 + d - 32, m] = 1.0
    return Pm


def _cmask():
    cm = np.zeros((128, 20, 512), np.float32)
    kk = np.arange(128)[:, None]
    qq = np.arange(512)[None, :]
    for j in range(20):
        delta = 128 * j - 1024 + kk - qq
        a = np.abs(delta)
        c = (a <= 64).astype(np.float32)
        c += ((delta % 4 == 0) & (a <= 256)).astype(np.float32)
        c += ((delta % 16 == 0) & (a <= 1024)).astype(np.float32)
        cm[:, j, :] = c
    return np.ascontiguousarray(cm.reshape(128, 20 * 512))


def _bt3_table(T, G, quarter, is_prompt):
    NB, NKT = T // 512, T // 128
    cols = []
    for mode_blocks, base in ((NB // G, quarter * (NB // G)),):
        for qb in range(mode_blocks):
            gqb = base + qb
            for j in range(20):
                kt = 4 * gqb - 8 + j
                ok = 0 <= kt < NKT
                if ok and is_prompt and ((kt < NKT // 2) != (gqb < NB // 2)):
                    ok = False
                cols.append(0.0 if ok else NEG)
    t = np.asarray(cols, np.float32)
    return np.ascontiguousarray(np.broadcast_to(t[None, :], (128, t.size)))


def _pack_gains(norm1_g, norm2_g, final_norm_g, diff_norm_g, dil_norm_g, cross_bias, kh0=0.0, kh1=0.0,
                keep=(1.0, 1.0, 1.0, 1.0), neg_lcorr=0.0):
    g = np.zeros((128, 60), np.float32)
    g[:, 54:58] = np.asarray(keep, np.float32)[None, :]
    g[:, 58] = neg_lcorr
    for l in range(DEPTH):
        g[:, l * 8:(l + 1) * 8] = norm1_g[l].reshape(8, 128).T
        g[:, 16 + l * 8:16 + (l + 1) * 8] = norm2_g[l].reshape(8, 128).T
        g[:, 40 + l] = diff_norm_g[l]
        g[:, 42 + l * 4:42 + (l + 1) * 4] = dil_norm_g[l].reshape(4, 128).T
    g[:, 32:40] = final_norm_g.reshape(8, 128).T
    g[:, 50] = 0.0
    g[:, 51] = cross_bias
    g[:, 52] = kh0
    g[:, 53] = kh1
    return g


_NC_CACHE = {}


def run_cores(T, seqs, weights, n_cores=8, G=4):
    if T not in _NC_CACHE:
        _NC_CACHE[T] = build_nc(T)
    nc = _NC_CACHE[T]
    (norm1_g, w_in, lq1, lk1, lq2, lk2, diff_norm_g, dil_norm_g, w_out, norm2_g, w_ff1, w_ff2, final_norm_g) = weights
    lamv = np.concatenate([np.asarray(a, np.float32).reshape(-1) for a in (lq1, lk1, lq2, lk2)])
    lamv = np.ascontiguousarray(np.broadcast_to(lamv[None, :], (128, lamv.size)))
    common = dict(w_in=np.ascontiguousarray(w_in, np.float32), w_out=np.ascontiguousarray(w_out, np.float32),
                  w_ff1=np.ascontiguousarray(w_ff1, np.float32), w_ff2=np.ascontiguousarray(w_ff2, np.float32),
                  lamv=lamv, perm=_perm_matrix(), cmask=_cmask())
    per_seq = []
    for x, pos, is_prompt in seqs:
        cosT, sinT = _host_tables(T, pos)
        per_seq.append((np.ascontiguousarray(np.asarray(x, np.float32).T), cosT, sinT, is_prompt))
    in_maps = []
    for c in range(n_cores):
        xT, cosT, sinT, is_prompt = per_seq[(c // G) % len(per_seq)]
        quarter = c % G
        m = dict(common)
        TQ = T // G
        m["xT"] = np.ascontiguousarray(xT[:, quarter * TQ:(quarter + 1) * TQ])
        m["cosL"] = np.ascontiguousarray(cosT[:, quarter * TQ:(quarter + 1) * TQ])
        m["sinL"] = np.ascontiguousarray(sinT[:, quarter * TQ:(quarter + 1) * TQ])
        kh0 = NEG if (is_prompt and quarter >= G // 2) else 0.0
        kh1 = NEG if (is_prompt and quarter < G // 2) else 0.0
        m["gains"] = _pack_gains(np.asarray(norm1_g), np.asarray(norm2_g), np.asarray(final_norm_g),
                                 np.asarray(diff_norm_g), np.asarray(dil_norm_g), NEG if is_prompt else 0.0, kh0, kh1,
                                 keep=[1.0 if (not is_prompt or ((r < G // 2) == (quarter < G // 2))) else 0.0 for r in range(G)],
                                 neg_lcorr=(-(T // 2) if is_prompt else 0.0))
        m["bt3"] = _bt3_table(T, G, quarter, is_prompt)
        in_maps.append(m)
    res = run_bass_kernel_spmd(nc, in_maps, core_ids=list(range(n_cores)))
    if os.environ.get("KDBG", ""):
        return [res.results[c] for c in range(n_cores)]
    outs = []
    for s in range(len(seqs)):
        yT = np.concatenate([res.results[s * G + q]["yT"] for q in range(G)], axis=1)
        outs.append(np.ascontiguousarray(yT.T))
    return outs


def kernel(x_prompt, x_sample, norm1_g, w_in, lambda_q1, lambda_k1, lambda_q2, lambda_k2,
           diff_norm_g, dil_norm_g, w_out, norm2_g, w_ff1, w_ff2, final_norm_g):
    x_prompt = np.asarray(x_prompt, np.float32)
    x_sample = np.asarray(x_sample, np.float32)
    B, S, _ = x_prompt.shape
    T = x_sample.shape[1]
    assert B * S == T and x_sample.shape[0] == 1
    weights = tuple(np.asarray(a, np.float32) for a in (
        norm1_g, w_in, lambda_q1, lambda_k1, lambda_q2, lambda_k2, diff_norm_g, dil_norm_g, w_out, norm2_g,
        w_ff1, w_ff2, final_norm_g))
    seqs = [
        (x_sample[0], np.arange(T), False),
        (x_prompt.reshape(T, D), np.concatenate([np.arange(S), np.arange(S)]), True),
    ]
    ys, yp = run_cores(T, seqs, weights)
    return (yp.reshape(B, S, D).astype(np.float32), ys.reshape(1, T, D).astype(np.float32))
```
